# Optimizing a Trainium2 kernel written in Bass

```python
import jax, jax.numpy as jnp
from jax import lax
import numpy as np

D_MODEL = 1024
BATCH = 4
SEQ = 4096
DEPTH = 4
DEC_BATCH = 8
DEC_SEQ = 32
PAST_LEN = 2048

CHUNK = 64
N_MIXERS = 4
EXPAND = 2
D_INNER = EXPAND * D_MODEL
MLP_BLOCK = 128
MLP_GROUPS = 8
MLP_GROUP_W = D_INNER // MLP_GROUPS
SCONV_W = 3
CCONV_W = 31
POOL_WINDOWS = (2, 4, 8, 16)
POOL_GROUPS = len(POOL_WINDOWS)
POOL_GROUP_W = D_INNER // POOL_GROUPS
POOL_HIST = max(POOL_WINDOWS) - 1
EPS = 1e-6

kernel_name = "hybrid_streaming_encoder_step"


def rms_norm(x, g):
    xf = x.astype(jnp.float32)
    y = xf * lax.rsqrt(jnp.mean(xf * xf, axis=-1, keepdims=True) + EPS)
    return (y * g.astype(jnp.float32)).astype(x.dtype)


def layer_norm(x, g, b):
    xf = x.astype(jnp.float32)
    mu = jnp.mean(xf, axis=-1, keepdims=True)
    var = jnp.mean(jnp.square(xf - mu), axis=-1, keepdims=True)
    y = (xf - mu) * lax.rsqrt(var + EPS)
    return (y * g.astype(jnp.float32) + b.astype(jnp.float32)).astype(x.dtype)


def ada_modulation(c, w, b):
    mod = jnp.einsum('bd,de->be', jax.nn.silu(c), w) + b
    shift, scale, gate = jnp.split(mod, 3, axis=-1)
    return shift[:, None, :], scale[:, None, :], gate[:, None, :]


def with_history(x, hist, n_hist):
    if hist is None:
        ext = jnp.pad(x, ((0, 0), (n_hist, 0), (0, 0)))
    else:
        ext = jnp.concatenate([hist.astype(x.dtype), x], axis=1)
    return ext, ext[:, ext.shape[1] - n_hist:, :]


def causal_depthwise_conv(x_ext, w):
    k_w, ch = w.shape
    return lax.conv_general_dilated(
        x_ext, w[:, None, :].astype(x_ext.dtype), window_strides=(1,), padding='VALID',
        dimension_numbers=('NWC', 'WIO', 'NWC'), feature_group_count=ch)


def chunk_mlp_mix(v, w_s, b_s):
    bsz, s_len, e = v.shape
    blk = MLP_BLOCK if s_len >= MLP_BLOCK else s_len
    nb = s_len // blk
    cidx = jnp.arange(blk) // CHUNK
    mask = cidx[:, None] >= cidx[None, :]
    w = jnp.where(mask[None], w_s[:, :blk, :blk], jnp.zeros((), w_s.dtype))
    vb = v.reshape(bsz, nb, blk, MLP_GROUPS, MLP_GROUP_W)
    out = jnp.einsum('gij,bnjgc->bnigc', w, vb) + jnp.transpose(b_s[:, :blk])[None, None, :, :, None]
    return out.reshape(bsz, s_len, e)


def multi_scale_pool(ext, pos0, n_out):
    bsz, t_len, e = ext.shape
    xg = ext.astype(jnp.float32).reshape(bsz, t_len, POOL_GROUPS, POOL_GROUP_W)
    cs = jnp.pad(jnp.cumsum(xg, axis=1), ((0, 0), (1, 0), (0, 0), (0, 0)))
    hi = cs[:, POOL_HIST + 1:POOL_HIST + 1 + n_out]
    pos = pos0 + jnp.arange(n_out)
    means = []
    for g, win in enumerate(POOL_WINDOWS):
        lo = cs[:, POOL_HIST + 1 - win:POOL_HIST + 1 - win + n_out, g]
        cnt = jnp.minimum(win, pos + 1).astype(jnp.float32)
        means.append((hi[:, :, g] - lo) / cnt[None, :, None])
    return jnp.stack(means, axis=2).reshape(bsz, n_out, e).astype(ext.dtype)


def run_trunk(x, c, hist_b, hist_c, hist_d, pos0,
              w_ada, b_ada, g_norm,
              w_in_a, ln_a_g, ln_a_b, w_s_a, b_s_a, w_out_a,
              w_in_b, w_conv_b, w_out_b,
              w_in_c, w_conv_c, b_conv_c, ln_c_g, ln_c_b, w_out_c,
              w_in_d, w_pool_d, scale_pool_d, w_out_d, g_final):
    bsz, s_len, _ = x.shape
    v_rows = conv_b_new = conv_c_new = pool_new = None
    for i in range(DEPTH):
        shift, scale, gate = ada_modulation(c, w_ada[i], b_ada[i])
        h = rms_norm(x, g_norm[i]) * (1.0 + scale) + shift
        kind = i % N_MIXERS
        if kind == 0:
            u, v, z = jnp.split(jnp.einsum('bsd,de->bse', h, w_in_a), 3, axis=-1)
            u = jax.nn.gelu(u, approximate=False)
            v = layer_norm(jax.nn.gelu(v, approximate=False), ln_a_g, ln_a_b)
            y = u * chunk_mlp_mix(v, w_s_a, b_s_a)
            out = jnp.einsum('bse,ed->bsd', y * jax.nn.silu(z), w_out_a)
            v_rows = v
        elif kind == 1:
            bg, cg, hv, z = jnp.split(jnp.einsum('bsd,de->bse', h, w_in_b), 4, axis=-1)
            ext, conv_b_new = with_history(cg * hv, hist_b, SCONV_W - 1)
            y = bg * causal_depthwise_conv(ext, w_conv_b)
            out = jnp.einsum('bse,ed->bsd', y * jax.nn.silu(z), w_out_b)
        elif kind == 2:
            a, ag, z = jnp.split(jnp.einsum('bsd,de->bse', h, w_in_c), 3, axis=-1)
            glu = a * jax.nn.sigmoid(ag)
            ext, conv_c_new = with_history(glu, hist_c, CCONV_W - 1)
            y = causal_depthwise_conv(ext, w_conv_c) + b_conv_c
            y = jax.nn.silu(layer_norm(y, ln_c_g, ln_c_b))
            out = jnp.einsum('bse,ed->bsd', y * jax.nn.silu(z), w_out_c)
        else:
            pv, z = jnp.split(jnp.einsum('bsd,de->bse', h, w_in_d), 2, axis=-1)
            ext, pool_new = with_history(pv, hist_d, POOL_HIST)
            dlt = (multi_scale_pool(ext, pos0, s_len) - pv).reshape(bsz, s_len, POOL_GROUPS, POOL_GROUP_W)
            y = jnp.einsum('bsgc,gcf->bsgf', dlt, w_pool_d).reshape(bsz, s_len, D_INNER) * scale_pool_d
            out = jnp.einsum('bse,ed->bsd', y * jax.nn.silu(z), w_out_d)
        x = x + gate * out
    return rms_norm(x, g_final), v_rows, conv_b_new, conv_c_new, pool_new


def setup_inputs(seed: int = 0) -> dict:
    key = jax.random.key(seed)
    ks = jax.random.split(key, 32)
    f32 = jnp.float32
    D, E = D_MODEL, D_INNER

    def nrm(k, shape, scale=1.0):
        return jax.random.normal(k, shape, f32) * scale

    return {
        "x_prompt": nrm(ks[0], (BATCH, SEQ, D)),
        "x_sample": nrm(ks[1], (DEC_BATCH, DEC_SEQ, D)),
        "state_conv_b": nrm(ks[2], (DEC_BATCH, SCONV_W - 1, E)),
        "state_conv_c": nrm(ks[3], (DEC_BATCH, CCONV_W - 1, E), 0.5),
        "state_pool_d": nrm(ks[4], (DEC_BATCH, POOL_HIST, E)),
        "c_prompt": nrm(ks[5], (BATCH, D)),
        "c_sample": nrm(ks[6], (DEC_BATCH, D)),
        "w_ada": nrm(ks[7], (DEPTH, D, 3 * D), 0.5 * D ** -0.5),
        "b_ada": nrm(ks[8], (DEPTH, 3 * D), 0.02),
        "g_norm": 1.0 + nrm(ks[9], (DEPTH, D), 0.02),
        "w_in_a": nrm(ks[10], (D, 3 * E), D ** -0.5),
        "ln_a_g": 1.0 + nrm(ks[11], (E,), 0.02),
        "ln_a_b": nrm(ks[12], (E,), 0.02),
        "w_s_a": nrm(ks[13], (MLP_GROUPS, MLP_BLOCK, MLP_BLOCK), MLP_BLOCK ** -0.5),
        "b_s_a": 1.0 + nrm(ks[14], (MLP_GROUPS, MLP_BLOCK), 0.02),
        "w_out_a": nrm(ks[15], (E, D), E ** -0.5),
        "w_in_b": nrm(ks[16], (D, 4 * E), D ** -0.5),
        "w_conv_b": nrm(ks[17], (SCONV_W, E), SCONV_W ** -0.5),
        "w_out_b": nrm(ks[18], (E, D), E ** -0.5),
        "w_in_c": nrm(ks[19], (D, 3 * E), D ** -0.5),
        "w_conv_c": nrm(ks[20], (CCONV_W, E), CCONV_W ** -0.5),
        "b_conv_c": nrm(ks[21], (E,), 0.02),
        "ln_c_g": 1.0 + nrm(ks[22], (E,), 0.02),
        "ln_c_b": nrm(ks[23], (E,), 0.02),
        "w_out_c": nrm(ks[24], (E, D), E ** -0.5),
        "w_in_d": nrm(ks[25], (D, 2 * E), D ** -0.5),
        "w_pool_d": nrm(ks[26], (POOL_GROUPS, POOL_GROUP_W, POOL_GROUP_W), POOL_GROUP_W ** -0.5),
        "scale_pool_d": 1.0 + nrm(ks[27], (E,), 0.1),
        "w_out_d": nrm(ks[28], (E, D), E ** -0.5),
        "g_final": 1.0 + nrm(ks[29], (D,), 0.02),
    }


def reference(x_prompt, x_sample, state_conv_b, state_conv_c, state_pool_d, c_prompt, c_sample,
              w_ada, b_ada, g_norm,
              w_in_a, ln_a_g, ln_a_b, w_s_a, b_s_a, w_out_a,
              w_in_b, w_conv_b, w_out_b,
              w_in_c, w_conv_c, b_conv_c, ln_c_g, ln_c_b, w_out_c,
              w_in_d, w_pool_d, scale_pool_d, w_out_d, g_final):
    weights = (w_ada, b_ada, g_norm,
               w_in_a, ln_a_g, ln_a_b, w_s_a, b_s_a, w_out_a,
               w_in_b, w_conv_b, w_out_b,
               w_in_c, w_conv_c, b_conv_c, ln_c_g, ln_c_b, w_out_c,
               w_in_d, w_pool_d, scale_pool_d, w_out_d, g_final)
    y_prompt, _, conv_b_prompt, conv_c_prompt, pool_d_prompt = run_trunk(
        x_prompt, c_prompt, None, None, None, 0, *weights)
    y_sample, mlp_v_sample, conv_b_sample, conv_c_sample, pool_d_sample = run_trunk(
        x_sample, c_sample, state_conv_b, state_conv_c, state_pool_d, PAST_LEN, *weights)
    return (y_prompt, y_sample, mlp_v_sample, conv_b_prompt, conv_b_sample,
            conv_c_prompt, conv_c_sample, pool_d_prompt, pool_d_sample)
```

```python
import numpy as np
import concourse.bass as bass
import concourse.mybir as mybir
from concourse.bass_utils import run_bass_kernel_spmd

F32 = mybir.dt.float32
BF16 = mybir.dt.bfloat16
ALU = mybir.AluOpType
AF = mybir.ActivationFunctionType

ENGS = ("pe", "act", "dve", "pool", "sp")
D = 1024
E = 2048
EPS = 1e-6


class Sched:
    def __init__(self, nc):
        self.nc = nc
        self.streams = {e: [] for e in ENGS}
        self.cnt = {}
        self.seen = {e: {} for e in ENGS}
        self.res = {}
        self.sem = {}
        for e in ENGS:
            self.sem[e] = nc.alloc_semaphore("p_" + e)
            self.cnt[e] = 0

    def _deps(self, eng, reads, writes):
        waits = {}

        def add(sv):
            if sv is None:
                return
            sname, v = sv
            if v > waits.get(sname, 0):
                waits[sname] = v

        for k in reads:
            r = self.res.get(k)
            if r is not None:
                add(r[0])
        for k in writes:
            r = self.res.get(k)
            if r is not None:
                add(r[0])
                for sname, v in r[1].items():
                    add((sname, v))
        out = []
        for sname, v in waits.items():
            if self.seen[eng].get(sname, 0) >= v:
                continue
            self.seen[eng][sname] = v
            out.append((sname, v))
        return out

    def _mark(self, src, val, reads, writes):
        for k in writes:
            self.res[k] = [(src, val), {}]
        for k in reads:
            r = self.res.get(k)
            if r is None:
                r = self.res[k] = [None, {}]
            r[1][src] = val

    def op(self, eng, fn, reads=(), writes=()):
        ro = getattr(self, "ring_out", None)
        if ro:
            for k in reads:
                ro.discard(k)
        waits = self._deps(eng, reads, writes)
        if eng == "pe":
            waits = [(a, v) for (a, v) in waits if a != "pe"]
        self.cnt[eng] += 1
        val = self.cnt[eng]
        sems = self.sem
        wl = [(sems[a], v) for a, v in waits]
        mysem = sems[eng]

        def emit(h):
            for sm, v in wl:
                h.wait_ge(sm, v)
            ins = fn(h)
            ins.then_inc(mysem, 1)

        self.streams[eng].append(emit)
        self._mark(eng, val, reads, writes)

    def dma(self, q, slot, out, in_, reads=(), writes=(), **kw):
        if slot not in self.sem:
            self.sem[slot] = self.nc.alloc_semaphore("d_" + slot)
            self.cnt[slot] = 0
        waits = self._deps(q, reads, writes)
        self.cnt[slot] += 16
        val = self.cnt[slot]
        sems = self.sem
        wl = [(sems[a], v) for a, v in waits]
        dsm = sems[slot]

        def emit(h):
            for sm, v in wl:
                h.wait_ge(sm, v)
            h.dma_start(out=out, in_=in_, **kw).then_inc(dsm, 16)

        self.streams[q].append(emit)
        self._mark(slot, val, reads, writes)

    def wait_all(self, eng, keys):
        waits = self._deps(eng, keys, keys)
        sems = self.sem
        wl = [(sems[a], v) for a, v in waits]

        def emit(h):
            for sm, v in wl:
                h.wait_ge(sm, v)

        self.streams[eng].append(emit)

    def emit_all(self, block):
        st = self.streams

        @block.tensor
        def _(h):
            for f in st["pe"]:
                f(h)

        @block.scalar
        def _(h):
            for f in st["act"]:
                f(h)

        @block.vector
        def _(h):
            for f in st["dve"]:
                f(h)

        @block.gpsimd
        def _(h):
            for f in st["pool"]:
                f(h)

        @block.sync
        def _(h):
            for f in st["sp"]:
                f(h)


def build_program():
    nc = bass.Bass("TRN2", target_bir_lowering=False)

    def din(name, shape):
        return nc.dram_tensor(name, list(shape), F32, kind="ExternalInput").ap()

    def dout(name, shape):
        return nc.dram_tensor(name, list(shape), F32, kind="ExternalOutput").ap()

    xp = din("xp", [2176, D])
    xsm = din("xsm", [32, D])
    st_b = din("st_b", [2, E])
    st_c = din("st_c", [30, E])
    st_d = din("st_d", [15, E])
    cvec = din("cvec", [2, D])
    maskc_d = din("maskc", [128, 1])
    invc_d = din("invc", [128, 64])
    w_ada = din("w_ada", [4, D, 3 * D])
    b_ada = din("b_ada", [4, 3 * D])
    g_norm = din("g_norm", [4, D])
    w_in = [din("w_in_a", [D, 3 * E]), din("w_in_b", [D, 4 * E]), din("w_in_c", [D, 3 * E]), din("w_in_d", [D, 2 * E])]
    w_out = [din("w_out_a", [E, D]), din("w_out_b", [E, D]), din("w_out_c", [E, D]), din("w_out_d", [E, D])]
    prm9 = din("prm9", [9, E])
    w_conv_c = din("w_conv_c", [31, E])
    w_s_a = din("w_s_a", [8, 128, 128])
    b_s_a = din("b_s_a", [8, 128])
    w_pool = din("w_pool_d", [4, 512, 512])
    g_final = din("g_final", [1, D])
    ln_a_gb = din("ln_a_gb", [2, E])

    yp = dout("yp", [2048, D])
    ysm = dout("ysm", [32, D])
    vrows = dout("vrows", [32, E])
    o_cb_p = dout("cb_p", [2, E]); o_cb_s = dout("cb_s", [2, E])
    o_cc_p = dout("cc_p", [30, E]); o_cc_s = dout("cc_s", [30, E])
    o_pd_p = dout("pd_p", [15, E]); o_pd_s = dout("pd_s", [15, E])

    s = Sched(nc)
    TOTAL = 53000
    big = nc.alloc_sbuf_tensor("big", [128, TOTAL], F32)
    off = [0]

    def carve(nw):
        v = big[:, off[0]:off[0] + nw]
        off[0] += nw
        assert off[0] <= TOTAL, off[0]
        return v

    def v3(v, a):
        return v.rearrange("p (a b) -> p a b", a=a)

    X = v3(carve(6 * 1024), 6)
    XS = X[:, 5, :]
    hT = v3(carve(3072).bitcast(BF16), 8)
    ysT_raw = carve(6144)
    ysT = v3(ysT_raw.bitcast(BF16), 16)
    AUX_raw = carve(6400)
    AUX = v3(AUX_raw.bitcast(BF16), 16)
    RING = carve(16 * 1024)
    Traw = [carve(800) for _ in range(4)]
    gateP = [carve(1024), carve(1024)]
    gateS = carve(1024)
    dscr = [carve(128), carve(128)]
    scTb = v3(carve(8).bitcast(BF16), 8)
    gf_bc = carve(1024)
    biasT = v3(carve(2048), 16)
    biasT_s = v3(carve(512), 16)
    WsT = v3(carve(512).bitcast(BF16), 8)
    WsT_s = v3(carve(128).bitcast(BF16), 8)
    identb = carve(64).bitcast(BF16)
    identf = carve(128)
    onesf = carve(128)
    onesb = carve(64).bitcast(BF16)
    prmT = v3(carve(16 * 9), 16)
    wccT = v3(carve(16 * 32), 16)
    stT = v3(carve(16 * 48), 16)
    hist = v3(carve(16 * 48), 16)
    outS = v3(carve(16 * 48), 16)
    gnT = v3(carve(32), 8)
    badaT = v3(carve(96), 24)
    cT = v3(carve(16), 8)
    scT = v3(carve(16), 8)
    modT = carve(4 * 48)
    gsT = carve(4 * 16)
    stat = carve(64)
    maskc = carve(1)
    invc = carve(64)
    print("SBUF words used", off[0])

    def modv(l):
        return v3(modT[:, l * 48:(l + 1) * 48], 24)

    def gsv(l):
        return v3(gsT[:, l * 16:(l + 1) * 16], 8)

    PS = nc.alloc_psum_tensor("ps", [128, 4096], F32)

    def bank(b):
        return PS[:, b * 512:(b + 1) * 512]

    def bk(*bs):
        return ["bk.%d" % b for b in bs]

    def Tf(i, n=800):
        return Traw[i][:, 0:n]

    def Tb(i, n=1600):
        return Traw[i].bitcast(BF16)[:, 0:n]

    def Tk(*idx):
        return ["T.%d" % i for i in idx]

    Tcat = big[:, Traw[0].offset - big.offset:Traw[0].offset - big.offset + 3200] if False else None

    t0_off = 6 * 1024 + 1024 + 3072 + 6144 + 6400 + 16 * 1024
    stage = big[:, t0_off:t0_off + 3200]
    Tpair01 = big[:, t0_off:t0_off + 1600]

    s.op("pool", lambda h: h.memset(identf, 0.0), writes=["identf"])
    s.op("pool", lambda h: h.affine_select(out=identf, in_=identf, pattern=[[-1, 128]], compare_op=ALU.not_equal,
                                           fill=1.0, base=0, channel_multiplier=1), reads=["identf"], writes=["identf"])
    s.op("dve", lambda h: h.tensor_copy(out=identb, in_=identf), reads=["identf"], writes=["identb"])
    s.op("dve", lambda h: h.memset(onesf, 1.0), writes=["onesf"])
    s.op("dve", lambda h: h.memset(onesb, 1.0), writes=["onesb"])
    s.op("dve", lambda h: h.memset(stat[:, 0:63], 0.0), writes=["stat_init"])
    s.op("dve", lambda h: h.memset(stat[:, 63:64], -0.5), writes=["neghalf"])
    s.op("dve", lambda h: h.memset(hist.rearrange("p a b -> p (a b)"), 0.0), writes=["hist"])
    s.dma("sp", "su_m", maskc, maskc_d, writes=["maskc"])
    s.dma("sp", "su_i", invc, invc_d, writes=["invc"])
    s.dma("sp", "su_g", gf_bc, g_final.partition_broadcast(128), writes=["gf_bc"])

    def load_rows_T(src, R, C, dst3, dkey, rp=32):
        nch = C // 128
        s.dma("sp", "su_st", stage[0:R, 0:C], src, writes=Tk(0, 1, 2, 3))
        pst = v3(bank(0)[:, 0:nch * rp], nch)

        def f(h):
            ins = None
            for c in range(nch):
                ins = h.transpose(out=pst[:, c, 0:R], in_=stage[0:R, c * 128:(c + 1) * 128], identity=identf[0:R, 0:R])
            return ins
        s.op("pe", f, reads=Tk(0, 1, 2, 3) + ["identf"], writes=bk(0))
        s.op("dve", lambda h: h.tensor_copy(out=dst3[:, :, 0:R], in_=pst[:, :, 0:R]), reads=[], writes=bk(0) + [dkey])

    load_rows_T(cvec, 2, D, cT, "cT")
    load_rows_T(g_norm, 4, D, gnT, "gnT")
    load_rows_T(b_ada, 4, 3 * D, badaT, "badaT", rp=4)
    load_rows_T(prm9, 9, E, prmT, "prmT")
    load_rows_T(w_conv_c, 31, E, wccT, "wccT")
    load_rows_T(st_b, 2, E, stT[:, :, 0:2], "stT")
    load_rows_T(st_c, 30, E, stT[:, :, 2:32], "stT")
    load_rows_T(st_d, 15, E, stT[:, :, 32:47], "stT")
    s.op("act", lambda h: h.activation(out=scT.rearrange("p a b -> p (a b)"), in_=cT.rearrange("p a b -> p (a b)"), func=AF.Silu),
         reads=["cT"], writes=["scT"])

    wsI = v3(stage[:, 0:1024], 8)
    s.dma("sp", "su_st", wsI, w_s_a.rearrange("g i j -> i g j"), writes=Tk(0, 1, 2))
    ps_w = v3(PS[:, 0:1024], 8)

    def f_ws_s(h):
        ins = None
        for g in range(8):
            ins = h.transpose(out=ps_w[0:32, g, 0:32], in_=wsI[0:32, g, 0:32], identity=identf[0:32, 0:32])
        return ins
    s.op("pe", f_ws_s, reads=Tk(0, 1, 2) + ["identf"], writes=bk(0, 1))
    s.op("dve", lambda h: h.tensor_copy(out=WsT_s[0:32, :, :], in_=ps_w[0:32, :, 0:32]), reads=[], writes=bk(0, 1) + ["WsT_s"])
    s.op("dve", lambda h: h.memset(wsI[0:64, :, 64:128], 0.0), reads=[], writes=Tk(0, 1, 2))

    def f_ws(h):
        ins = None
        for g in range(8):
            ins = h.transpose(out=ps_w[:, g, :], in_=wsI[:, g, :], identity=identf)
        return ins
    s.op("pe", f_ws, reads=Tk(0, 1, 2) + ["identf"], writes=bk(0, 1))
    s.op("dve", lambda h: h.tensor_copy(out=WsT, in_=ps_w), reads=[], writes=bk(0, 1) + ["WsT"])
    def f_rs(h):
        ins = None
        for hf in range(2):
            ins = h.matmul(bank(2 + hf), lhsT=onesb, rhs=WsT[:, hf * 4:(hf + 1) * 4, :].rearrange("p a b -> p (a b)"), start=True, stop=True)
        return ins
    s.op("pe", f_rs, reads=["WsT", "onesb"], writes=bk(2, 3))
    rs_bc = v3(PS[:, 1024:2048], 8)

    def f_rs_s(h):
        return h.matmul(bank(4)[:, 0:256], lhsT=onesb[0:32, :], rhs=WsT_s[0:32, :, :].rearrange("p a b -> p (a b)"), start=True, stop=True)
    s.op("pe", f_rs_s, reads=["WsT_s", "onesb"], writes=bk(4))
    rs_bc_s = v3(bank(4)[:, 0:256], 8)
    bs_bc = v3(stage[:, 1024:2048], 8)
    s.dma("sp", "su_bs", bs_bc.rearrange("p a b -> p (a b)"), b_s_a.rearrange("g i -> (g i)").partition_broadcast(128), writes=Tk(1, 2))
    for cc in range(16):
        g = cc // 2
        s.op("dve", lambda h, cc=cc, g=g: h.scalar_tensor_tensor(out=biasT[:, cc, :], in0=rs_bc[:, g, :], scalar=prmT[:, cc, 8:9],
                                                                  in1=bs_bc[:, g, :], op0=ALU.mult, op1=ALU.add),
             reads=["prmT"] + Tk(1, 2), writes=bk(2, 3) + ["biasT"])
        s.op("dve", lambda h, cc=cc, g=g: h.scalar_tensor_tensor(out=biasT_s[:, cc, :], in0=rs_bc_s[:, g, :], scalar=prmT[:, cc, 8:9],
                                                                  in1=bs_bc[:, g, 0:32], op0=ALU.mult, op1=ALU.add),
             reads=["prmT"] + Tk(1, 2), writes=bk(4) + ["biasT_s"])

    s.op("dve", lambda h: h.tensor_copy(out=scTb.rearrange("p a b -> p (a b)"), in_=scT.rearrange("p a b -> p (a b)")), reads=["scT"], writes=["scTb"])
    ring_state = {"next": 0}

    ring_out = set()
    s.ring_out = ring_out

    def ring_slot():
        i = ring_state["next"]
        ring_state["next"] = (i + 1) % 16
        assert ("rs.%d" % i) not in ring_out, "ring slot %d re-allocated before its consumer was recorded" % i
        ring_out.add("rs.%d" % i)
        return i

    def slot_bf(i):
        return RING[:, i * 1024:(i + 1) * 1024].bitcast(BF16)

    def fetch_in(l, col0):
        i = ring_slot()
        v = v3(slot_bf(i), 8)
        s.dma("pool", "rs%d" % i, v, w_in[l].rearrange("(kc p) n -> p kc n", p=128)[:, :, col0:col0 + 256], writes=["rs.%d" % i])
        return i, v

    def fetch_out(l, e2):
        i = ring_slot()
        v = v3(slot_bf(i), 2)
        s.dma("pool", "rs%d" % i, v, w_out[l][e2 * 256:(e2 + 1) * 256, :].rearrange("(a p) n -> p a n", p=128), writes=["rs.%d" % i])
        return i, v

    def ada_fetch(l, part):
        sl = []
        for j in range(4):
            cb = part * 4 + j
            i = ring_slot()
            v = v3(slot_bf(i), 8)
            s.dma("pool", "rs%d" % i, v, w_ada[l].rearrange("(kc p) n -> p kc n", p=128)[:, :, cb * 256:(cb + 1) * 256], writes=["rs.%d" % i])
            sl.append((i, v))
        return sl

    def ada_compute(l, part, sl):
        psA = bank(7)

        def f(h):
            ins = None
            for j in range(4):
                for ec in range(2):
                    for kc in range(8):
                        col = (j * 2 + ec) * 2
                        ins = h.matmul(psA[:, col:col + 2], lhsT=sl[j][1][:, kc, ec * 128:(ec + 1) * 128], rhs=scTb[:, kc, :],
                                       start=(kc == 0), stop=(kc == 7))
            return ins
        s.op("pe", f, reads=["rs.%d" % i for i, _ in sl] + ["scTb"], writes=bk(7))
        psA3 = v3(psA[:, 0:16], 8)
        mv = modv(l)
        for sq in range(2):
            s.op("dve", lambda h, sq=sq: h.tensor_tensor(out=mv[:, 8 * part:8 * part + 8, sq], in0=psA3[:, :, sq], in1=badaT[:, 8 * part:8 * part + 8, l], op=ALU.add),
                 reads=["badaT"], writes=bk(7) + ["modT"])
            if part == 1:
                s.op("dve", lambda h, sq=sq: h.scalar_tensor_tensor(out=gsv(l)[:, :, sq], in0=mv[:, 8:16, sq], scalar=1.0,
                                                                     in1=gnT[:, :, l], op0=ALU.add, op1=ALU.mult),
                     reads=["modT", "gnT"], writes=["gsT"])

    def gate_rows(l, sq, dst, dkey):
        mv = modv(l)
        p = next_pair()

        def f(h):
            ins = None
            for kc in range(8):
                col = mv[:, 16 + kc, sq:sq + 1]
                lb = bass.AP(col.tensor, col.offset, [list(col.ap[0]), [0, 128]])
                ins = h.matmul(PS[:, p * 1024 + kc * 128:p * 1024 + (kc + 1) * 128], lhsT=lb, rhs=identf, start=True, stop=True)
            return ins
        s.op("pe", f, reads=["modT", "identf"], writes=bk(2 * p, 2 * p + 1))
        s.op("act", lambda h: h.activation(out=dst, in_=PS[:, p * 1024:(p + 1) * 1024], func=AF.Copy), reads=[], writes=bk(2 * p, 2 * p + 1) + [dkey])

    class Pref:
        def __init__(self, thunks, depth=1):
            self.th, self.res, self.n, self.depth = thunks, {}, 0, depth

        def get(self, g):
            while self.n < len(self.th) and self.n <= g + self.depth:
                self.res[self.n] = self.th[self.n]()
                self.n += 1
            return self.res.pop(g)

    def fetch_pool(g):
        i = ring_slot()
        v = v3(slot_bf(i), 4)
        s.dma("pool", "rs%d" % i, v, w_pool[g].rearrange("(a p) n -> p a n", p=128), writes=["rs.%d" % i])
        return i, v

    for part in range(2):
        ada_compute(0, part, ada_fetch(0, part))

    ST_DEFS = [
        dict(b0=0, nb=6, ncols=768, ep=768, sample=False),
        dict(b0=6, nb=6, ncols=768, ep=768, sample=False),
        dict(b0=12, nb=5, ncols=704, ep=640, sample=True),
    ]
    pair_rr = {"i": 0}

    def next_pair():
        p = pair_rr["i"]
        pair_rr["i"] = (p + 1) % 4
        return p

    def pairv(p, n):
        return PS[:, p * 1024:p * 1024 + n]

    cs_box = {"cs": 0}

    def ntiles(p, ncols):
        cs = cs_box["cs"]
        return [(cs, 512 - cs, 2 * p, cs), (512, ncols - 512, 2 * p + 1, 0)]

    def inproj(slot_i, wv, ci, p, ncols):
        nts = ntiles(p, ncols)

        def f(h):
            ins = None
            for kc in range(8):
                for (c0, n, b, bo) in nts:
                    ins = h.matmul(bank(b)[:, bo:bo + n], lhsT=wv[:, kc, ci * 128:(ci + 1) * 128], rhs=hT[:, kc, c0:c0 + n],
                                   start=(kc == 0), stop=(kc == 7))
            return ins
        s.op("pe", f, reads=["rs.%d" % slot_i] + ["hT.%d" % i for i in range(6)] + ["hTb.%d" % i for i in range(6)], writes=bk(2 * p, 2 * p + 1))

    def store_rows_T(src3, R, dst, okey, skey):
        pso = PS[0:R, 0:2048]

        def f(h):
            ins = None
            for c in range(16):
                ins = h.transpose(out=pso[:, c * 128:(c + 1) * 128], in_=src3[:, c, :], identity=identf)
            return ins
        s.op("pe", f, reads=[skey, "identf"], writes=bk(0, 1, 2, 3))
        s.op("act", lambda h: h.activation(out=stage[0:R, 0:1024], in_=pso[:, 0:1024], func=AF.Copy), reads=[], writes=bk(0, 1) + Tk(0, 1, 2))
        s.op("act", lambda h: h.activation(out=stage[0:R, 1024:2048], in_=pso[:, 1024:2048], func=AF.Copy), reads=[], writes=bk(2, 3) + Tk(0, 1, 2))
        s.dma("sp", "o_st", dst, stage[0:R, 0:2048], reads=Tk(0, 1, 2), writes=[okey])


    def emit_state_outputs(l):
        if l == 1:
            store_rows_T(hist[:, :, 0:2], 2, o_cb_p, "O.cbp", "hist")
            store_rows_T(outS[:, :, 0:2], 2, o_cb_s, "O.cbs", "outS")
        elif l == 2:
            store_rows_T(hist[:, :, 2:32], 30, o_cc_p, "O.ccp", "hist")
            store_rows_T(outS[:, :, 2:32], 30, o_cc_s, "O.ccs", "outS")
        elif l == 3:
            store_rows_T(hist[:, :, 32:47], 15, o_pd_p, "O.pdp", "hist")
            store_rows_T(outS[:, :, 32:47], 15, o_pd_s, "O.pds", "outS")

    def run_st(sti, sd):
        ncols, ep, nb = sd["ncols"], sd["ep"], sd["nb"]
        blocks = []
        for j in range(nb):
            blocks.append(dict(xt=X[:, j, :], n=128, col0=j * 128, sq=0, key="x.%d" % j, gb=sd["b0"] + j))
        if sd["sample"]:
            blocks.append(dict(xt=XS[0:32, :], n=32, col0=672, sq=1, key="x.5", gb=None))
        for j in range(nb):
            gb = sd["b0"] + j
            s.dma("sp", "x%d" % j, X[:, j, :], xp[gb * 128:(gb + 1) * 128, :], writes=["x.%d" % j])
        if sd["sample"]:
            s.dma("sp", "x5", XS[0:32, :], xsm, writes=["x.5"])

        def fix_hist(buf, H, hsl, key, c, first_mask_ok=True):
            h0, h1 = hsl
            if sti == 0:
                s.op("dve", lambda h: h.memset(buf[:, 0:H], 0.0), writes=[key])
                s.op("dve", lambda h: h.tensor_scalar_mul(out=buf[:, H + 128 - H:H + 128], in0=buf[:, H + 128 - H:H + 128], scalar1=maskc[:, 0:1]),
                     reads=["maskc", key], writes=[key])
            else:
                s.op("dve", lambda h: h.tensor_copy(out=buf[:, 0:H], in_=hist[:, c, h0:h1]), reads=["hist"], writes=[key])
            if sd["sample"]:
                s.op("dve", lambda h: h.tensor_copy(out=buf[:, H + 672 - H:H + 672], in_=stT[:, c, h0:h1]), reads=["stT", key], writes=[key])

        junkv = AUX_raw[:, 0:512].bitcast(BF16)

        def h_pre(bi, bl, lt):
            n, xt = bl["n"], bl["xt"]
            so = (bi % 4) * 4
            xi = bi % 3
            xn = AUX_raw[:, 3072 + xi * 512:3584 + xi * 512].bitcast(BF16)[0:n, 0:1024]
            s.op("dve", lambda h: h.memset(stat[0:n, so:so + 1], 0.0), reads=[], writes=["stat%d" % (bi % 4)])
            s.op("act", lambda h: h.activation(out=junkv[0:n, :], in_=xt, func=AF.Square, scale=1.0 / 32, accum_out=stat[0:n, so:so + 1]),
                 reads=[bl["key"]], writes=["junk", "stat%d" % (bi % 4)])
            s.op("act", lambda h: h.activation(out=stat[0:n, so + 1:so + 2], in_=stat[0:n, so:so + 1], func=AF.Sqrt, bias=EPS, scale=1.0),
                 reads=["stat%d" % (bi % 4)], writes=["stat%d" % (bi % 4)])
            s.op("dve", lambda h: h.reciprocal(out=stat[0:n, so + 2:so + 3], in_=stat[0:n, so + 1:so + 2]), reads=["stat%d" % (bi % 4)], writes=["stat%d" % (bi % 4)])
            s.op("act", lambda h: h.activation(out=xn, in_=xt, func=AF.Copy, scale=stat[0:n, so + 2:so + 3]),
                 reads=[bl["key"], "stat%d" % (bi % 4)], writes=["xn%d" % xi])

        def h_post(bi, bl, lt):
            n, sq, c0 = bl["n"], bl["sq"], bl["col0"]
            xi = bi % 3
            xn = AUX_raw[:, 3072 + xi * 512:3584 + xi * 512].bitcast(BF16)[0:n, 0:1024]
            mvt, gst = modv(lt), gsv(lt)
            p = next_pair()
            ptA = v3(bank(2 * p).bitcast(BF16)[:, 0:512], 4)
            ptB = v3(bank(2 * p + 1).bitcast(BF16)[:, 0:512], 4)

            def f(h):
                ins = None
                for kc in range(8):
                    dstp = ptA[:, kc, 0:n] if kc < 4 else ptB[:, kc - 4, 0:n]
                    ins = h.transpose(out=dstp, in_=xn[:, kc * 128:(kc + 1) * 128], identity=identb[0:n, 0:n])
                return ins
            s.op("pe", f, reads=["xn%d" % xi, "identb"], writes=bk(2 * p, 2 * p + 1))
            for kc in range(4):
                s.op("dve", lambda h, kc=kc: h.tensor_scalar(
                    out=hT[:, kc, c0:c0 + n], in0=ptA[:, kc, 0:n], scalar1=gst[:, kc, sq:sq + 1], scalar2=mvt[:, kc, sq:sq + 1],
                    op0=ALU.mult, op1=ALU.add), reads=["gsT", "modT"], writes=bk(2 * p) + ["hT.%d" % bi])
            for kc in range(4, 8):
                s.op("act", lambda h, kc=kc: h.activation(
                    out=hT[:, kc, c0:c0 + n], in_=ptB[:, kc - 4, 0:n], func=AF.Identity, scale=gst[:, kc, sq:sq + 1], bias=mvt[:, kc, sq:sq + 1]),
                    reads=["gsT", "modT"], writes=bk(2 * p + 1) + ["hTb.%d" % bi])

        hT_keys = ["hT.%d" % i for i in range(len(blocks))]

        def run_layer(l):
            mv = modv(l)
            last = (l == 3)
            cs_box["cs"] = 80 if (sti == 0 and l >= 1) else 0
            gP = gateP[l % 2]
            if l == 0:
                if sti > 0:
                    gate_rows(0, 0, gP, "gateP0")
                for bi, bl in enumerate(blocks):
                    h_pre(bi, bl, 0)
                    if bi >= 2:
                        h_post(bi - 2, blocks[bi - 2], 0)
                for bi in range(max(0, len(blocks) - 2), len(blocks)):
                    h_post(bi, blocks[bi], 0)
            if sd["sample"]:
                gate_rows(l, 1, gateS, "gateS")

            ada_sl = {}
            wo_box = {}

            def hoist_wo():
                if "wo" not in wo_box:
                    wo_box["wo"] = [fetch_out(l, e2) for e2 in range(8)]

            def mid_hook(grp):
                if l >= 3:
                    return
                if sti == 0:
                    if 2 <= grp <= 4:
                        ada_compute(l + 1, grp - 2, ada_sl[grp - 2])
                    if 1 <= grp <= 3:
                        ada_sl[grp - 1] = ada_fetch(l + 1, grp - 1)
                if grp == 5:
                    gate_rows(l + 1, 0, gateP[(l + 1) % 2], "gateP%d" % ((l + 1) % 2))

            if l == 0:
                wv_slots = [fetch_in(0, E + q * 256) for q in range(8)]

                def v_front(bi, bl):
                    n, c0 = bl["n"], bl["col0"]
                    smp = bl["sq"] == 1
                    gv = AUX_raw[0:n, (bi % 2) * 2048:(bi % 2) * 2048 + 2048]
                    vn = AUX_raw[:, 4096:6144].bitcast(BF16)[0:n, (bi % 2) * 2048:(bi % 2) * 2048 + 2048]
                    gk = ["aux.g%d" % (bi % 2)]
                    vk = ["aux.v%d" % (bi % 2)]
                    sk = ["statv%d" % (bi % 2)]

                    for vh in range(2):
                        def f(h, vh=vh):
                            ins = None
                            for kc in range(8):
                                for q in range(4 * vh, 4 * vh + 4):
                                    ins = h.matmul(bank(q // 2)[0:n, (q % 2) * 256:(q % 2) * 256 + 256], lhsT=hT[:, kc, c0:c0 + n],
                                                   rhs=wv_slots[q][1][:, kc, :], start=(kc == 0 and q % 2 == 0), stop=(kc == 7),
                                                   skip_group_check=True)
                            return ins
                        s.op("pe", f, reads=["hT.%d" % bi, "hTb.%d" % bi] + ["rs.%d" % i for i, _ in wv_slots[4 * vh:4 * vh + 4]], writes=bk(2 * vh, 2 * vh + 1))
                    so = 16 + (bi % 2) * 8
                    s.op("act", lambda h: h.activation(out=stat[0:n, so:so + 3], in_=stat[0:n, so:so + 3], func=AF.Copy, scale=0.0), reads=sk + ["stat_init"], writes=sk)
                    for hf in range(2):
                        s.op("act", lambda h, hf=hf: h.activation(out=gv[:, hf * 1024:(hf + 1) * 1024], in_=PS[0:n, hf * 1024:(hf + 1) * 1024],
                                                                  func=AF.Gelu, accum_out=stat[0:n, so + hf:so + hf + 1]),
                             reads=[], writes=bk(2 * hf, 2 * hf + 1) + gk + sk + (["xn0", "xn1"] if bi % 2 == 1 else []))
                    junk = Tpair01.bitcast(BF16)[0:n, 0:2048]
                    s.op("act", lambda h: h.activation(out=junk, in_=gv, func=AF.Square, accum_out=stat[0:n, so + 2:so + 3]),
                         reads=gk, writes=Tk(0, 1) + sk)
                    return gv, vn, gk, vk, sk, so, n, smp

                def v_stats(bi, bl, ctx):
                    gv, vn, gk, vk, sk, so, n, smp = ctx
                    s.op("dve", lambda h: h.tensor_tensor(out=stat[0:n, so + 3:so + 4], in0=stat[0:n, so:so + 1], in1=stat[0:n, so + 1:so + 2], op=ALU.add),
                         reads=sk, writes=sk)
                    s.op("dve", lambda h: h.tensor_scalar_mul(out=stat[0:n, so + 3:so + 4], in0=stat[0:n, so + 3:so + 4], scalar1=1.0 / E),
                         reads=sk, writes=sk)
                    s.op("dve", lambda h: h.tensor_tensor(out=stat[0:n, so + 4:so + 5], in0=stat[0:n, so + 3:so + 4], in1=stat[0:n, so + 3:so + 4], op=ALU.mult),
                         reads=sk, writes=sk)
                    s.op("dve", lambda h: h.scalar_tensor_tensor(out=stat[0:n, so + 5:so + 6], in0=stat[0:n, so + 2:so + 3], scalar=1.0 / E,
                                                                 in1=stat[0:n, so + 4:so + 5], op0=ALU.mult, op1=ALU.subtract),
                         reads=sk, writes=sk)
                    s.op("dve", lambda h: h.tensor_scalar_add(out=stat[0:n, so + 5:so + 6], in0=stat[0:n, so + 5:so + 6], scalar1=EPS), reads=sk, writes=sk)
                    s.op("pool", lambda h: h.tensor_tensor(out=stat[0:n, so + 6:so + 7], in0=stat[0:n, so + 5:so + 6], in1=stat[0:n, 63:64], op=ALU.pow),
                         reads=sk + ["neghalf"], writes=sk)
                    s.op("dve", lambda h: h.scalar_tensor_tensor(out=stat[0:n, so + 7:so + 8], in0=stat[0:n, so + 3:so + 4], scalar=-1.0,
                                                                 in1=stat[0:n, so + 6:so + 7], op0=ALU.mult, op1=ALU.mult),
                         reads=sk, writes=sk)
                    s.op("act", lambda h: h.activation(out=vn, in_=gv, func=AF.Identity, scale=stat[0:n, so + 6:so + 7], bias=stat[0:n, so + 7:so + 8]),
                         reads=gk + sk, writes=vk + (["xn2"] if bi % 2 == 0 else []))
                    if smp:
                        def vrows_out():
                            gx = gk + ["xn0", "xn1"]
                            s.op("dve", lambda h: h.tensor_scalar(out=gv, in0=gv, scalar1=stat[0:n, so + 6:so + 7], scalar2=stat[0:n, so + 7:so + 8],
                                                                  op0=ALU.mult, op1=ALU.add), reads=gx + vk + sk, writes=gx)
                            s.dma("sp", "su_st", stage[0:32, 0:2048], ln_a_gb[0:1, :].rearrange("a n -> (a n)").partition_broadcast(32), writes=Tk(0, 1, 2))
                            s.op("dve", lambda h: h.tensor_tensor(out=gv, in0=gv, in1=stage[0:32, 0:2048], op=ALU.mult), reads=Tk(0, 1, 2) + gx, writes=gx)
                            s.dma("sp", "su_st", stage[0:32, 0:2048], ln_a_gb[1:2, :].rearrange("a n -> (a n)").partition_broadcast(32), writes=Tk(0, 1, 2))
                            s.op("dve", lambda h: h.tensor_tensor(out=gv, in0=gv, in1=stage[0:32, 0:2048], op=ALU.add), reads=Tk(0, 1, 2) + gx, writes=gx)
                            s.dma("sp", "o_vr", vrows, gv, reads=gx, writes=["O.vrows"])
                        deferred_v.append(vrows_out)

                def v_back(bi, bl):
                    n, c0 = bl["n"], bl["col0"]
                    smp = bl["sq"] == 1
                    vn = AUX_raw[:, 4096:6144].bitcast(BF16)[0:n, (bi % 2) * 2048:(bi % 2) * 2048 + 2048]
                    vk = ["aux.v%d" % (bi % 2)]
                    wst = WsT_s if smp else WsT
                    bt = biasT_s if smp else biasT
                    mps = PS[:, 2048:2048 + 16 * n].rearrange("p (a b) -> p a b", a=16)

                    def f(h):
                        ins = None
                        for cc in range(16):
                            ins = h.matmul(mps[:, cc, :], lhsT=vn[:, cc * 128:(cc + 1) * 128], rhs=wst[0:n, cc // 2, 0:n], start=True, stop=True)
                        return ins
                    s.op("pe", f, reads=vk + ["WsT", "WsT_s"], writes=bk(4, 5, 6, 7))

                    def evac(lo, hi):
                        for cc in range(lo, hi):
                            s.op("dve", lambda h, cc=cc: h.scalar_tensor_tensor(
                                out=ysT[:, cc, c0:c0 + n], in0=mps[:, cc, :], scalar=prmT[:, cc, 7:8], in1=bt[:, cc, 0:n], op0=ALU.mult, op1=ALU.add),
                                reads=["prmT", "biasT", "biasT_s"], writes=bk(4, 5, 6, 7) + ["ysT.%d" % cc])
                    evac(0, 8)
                    return lambda: evac(8, 16)

                prevb = None
                deferred_v = []
                for bi, bl in enumerate(blocks):
                    ctx = v_front(bi, bl)
                    tail = v_back(*prevb) if prevb is not None else None
                    v_stats(bi, bl, ctx)
                    if tail is not None:
                        tail()
                    prevb = (bi, bl)
                v_back(*prevb)()
                for fdef in deferred_v:
                    fdef()
                pf = Pref([(lambda g=g: fetch_in(0, (g % 8) * 256 + (0 if g < 8 else 2 * E))) for g in range(16)], depth=3)
                for it in range(16):
                    if it == 0 and sti == 0:
                        ada_compute(0, 2, ada_fetch(0, 2))
                        gate_rows(0, 0, gP, "gateP0")
                    if it % 2 == 0:
                        mid_hook(it // 2)
                    sw = pf.get(it)
                    if it == 8:
                        hoist_wo()
                    isz = it >= 8
                    for ci in range(2):
                        c = (it % 8) * 2 + ci
                        pq = next_pair()
                        inproj(sw[0], sw[1], ci, pq, ncols)
                        tt_ = (2 if isz else 0) + (c % 2)
                        s.op("act", lambda h, pq=pq, tt_=tt_, isz=isz: h.activation(out=Tb(tt_)[:, 0:ncols], in_=pairv(pq, ncols), func=(AF.Silu if isz else AF.Gelu)),
                             reads=[], writes=bk(2 * pq, 2 * pq + 1) + Tk(tt_))
                        s.op("dve", lambda h, tt_=tt_, c=c: h.tensor_tensor(out=ysT[:, c, 0:ncols], in0=ysT[:, c, 0:ncols], in1=Tb(tt_)[:, 0:ncols], op=ALU.mult),
                             reads=Tk(tt_) + ["ysT.%d" % c], writes=["ysT.%d" % c])

            elif l == 1:
                pf = Pref([(lambda g=g: [fetch_in(1, b * E + g * 256) for b in range(4)]) for g in range(8)])
                for grp in range(8):
                    mid_hook(grp)
                    sl = pf.get(grp)
                    if grp == 4:
                        hoist_wo()
                    for ci in range(2):
                        c = grp * 2 + ci
                        pc_, ph_, pz_, pb_ = next_pair(), next_pair(), next_pair(), next_pair()
                        inproj(sl[1][0], sl[1][1], ci, pc_, ncols)
                        inproj(sl[2][0], sl[2][1], ci, ph_, ncols)
                        inproj(sl[3][0], sl[3][1], ci, pz_, ncols)
                        inproj(sl[0][0], sl[0][1], ci, pb_, ncols)
                        Pb = Tf(1)
                        s.op("act", lambda h, p=pc_: h.activation(out=Tf(0)[:, 0:ncols], in_=pairv(p, ncols), func=AF.Copy),
                             reads=[], writes=bk(2 * pc_, 2 * pc_ + 1) + Tk(0))
                        s.op("dve", lambda h, p=ph_: h.tensor_tensor(out=Tf(1)[:, 2:2 + ncols], in0=Tf(0)[:, 0:ncols], in1=pairv(p, ncols), op=ALU.mult),
                             reads=Tk(0), writes=bk(2 * ph_, 2 * ph_ + 1) + Tk(1))
                        s.op("act", lambda h, p=pz_, c=c: h.activation(out=ysT[:, c, 0:ncols], in_=pairv(p, ncols), func=AF.Silu),
                             reads=[], writes=bk(2 * pz_, 2 * pz_ + 1) + ["ysT.%d" % c])
                        s.op("act", lambda h, p=pb_: h.activation(out=Tf(3)[:, 0:ncols], in_=pairv(p, ncols), func=AF.Copy),
                             reads=[], writes=bk(2 * pb_, 2 * pb_ + 1) + Tk(3))
                        fix_hist(Pb, 2, (0, 2), "T.1", c)
                        s.op("dve", lambda h, c=c: h.tensor_copy(out=hist[:, c, 0:2], in_=Tf(1)[:, 2 + ep - 2:2 + ep]), reads=Tk(1), writes=["hist"])
                        if sd["sample"]:
                            s.op("dve", lambda h, c=c: h.tensor_copy(out=outS[:, c, 0:2], in_=Tf(1)[:, 2 + 702:2 + 704]), reads=Tk(1), writes=["outS"])
                        s.op("dve", lambda h, c=c: h.tensor_scalar_mul(out=Tf(2)[:, 0:ncols], in0=Tf(1)[:, 2:2 + ncols], scalar1=prmT[:, c, 2:3]),
                             reads=Tk(1) + ["prmT"], writes=Tk(2))
                        s.op("dve", lambda h, c=c: h.scalar_tensor_tensor(out=Tf(2)[:, 0:ncols], in0=Tf(1)[:, 1:1 + ncols], scalar=prmT[:, c, 1:2],
                                                                           in1=Tf(2)[:, 0:ncols], op0=ALU.mult, op1=ALU.add), reads=Tk(1, 2) + ["prmT"], writes=Tk(2))
                        s.op("dve", lambda h, c=c: h.scalar_tensor_tensor(out=Tf(2)[:, 0:ncols], in0=Tf(1)[:, 0:ncols], scalar=prmT[:, c, 0:1],
                                                                           in1=Tf(2)[:, 0:ncols], op0=ALU.mult, op1=ALU.add), reads=Tk(1, 2) + ["prmT"], writes=Tk(2))
                        s.op("dve", lambda h: h.tensor_tensor(out=Tf(2)[:, 0:ncols], in0=Tf(2)[:, 0:ncols], in1=Tf(3)[:, 0:ncols], op=ALU.mult),
                             reads=Tk(2, 3), writes=Tk(2))
                        s.op("dve", lambda h, c=c: h.tensor_tensor(out=ysT[:, c, 0:ncols], in0=ysT[:, c, 0:ncols], in1=Tf(2)[:, 0:ncols], op=ALU.mult),
                             reads=Tk(2) + ["ysT.%d" % c], writes=["ysT.%d" % c])

            elif l == 2:
                NTD = 3
                NTP = 31 - NTD

                def diag_view(c):
                    par = c % 2
                    dwords = ysT_raw[:, par * 2304:par * 2304 + NTP * 64].bitcast(BF16)
                    return dwords.rearrange("p (a b) -> p a b", a=NTP), ["ysT.%d" % i for i in range(par * 6, par * 6 + 6)]

                def build_diag(c):
                    dg, dkeys = diag_view(c)
                    id_bc = bass.AP(identb.tensor, identb.offset, [list(identb.ap[0]), [0, NTP], [1, 128]])
                    w_bc = wccT[:, c, NTD:31].unsqueeze(2).broadcast_to([128, NTP, 128])
                    s.op("pool", lambda h: h.tensor_tensor(out=dg, in0=id_bc, in1=w_bc, op=ALU.mult), reads=["identb", "wccT"], writes=dkeys)

                pf = Pref([(lambda g=g: (fetch_in(2, g * 256), fetch_in(2, E + g * 256))) for g in range(8)])
                for grp in range(8):
                    mid_hook(grp)
                    if grp >= 6:
                        build_diag(grp - 6)
                    sa, sg = pf.get(grp)
                    for ci in range(2):
                        c = grp * 2 + ci
                        pa, pg = next_pair(), next_pair()
                        inproj(sa[0], sa[1], ci, pa, ncols)
                        inproj(sg[0], sg[1], ci, pg, ncols)
                        tt = c % 2
                        G = AUX[:, c, :]
                        s.op("act", lambda h, pg=pg, tt=tt: h.activation(out=Tf(tt)[:, 0:ncols], in_=pairv(pg, ncols), func=AF.Sigmoid),
                             reads=[], writes=bk(2 * pg, 2 * pg + 1) + Tk(tt))
                        s.op("dve", lambda h, pa=pa, tt=tt, G=G: h.tensor_tensor(out=G[:, 30:30 + ncols], in0=Tf(tt)[:, 0:ncols], in1=pairv(pa, ncols), op=ALU.mult),
                             reads=Tk(tt), writes=bk(2 * pa, 2 * pa + 1) + ["aux.%d" % c])
                        fix_hist(G, 30, (2, 32), "aux.%d" % c, c)
                        s.op("dve", lambda h, pa=pa, tt=tt, c=c: h.tensor_tensor(out=hist[:, c, 2:32], in0=Tf(tt)[:, ep - 30:ep], in1=pairv(pa, ncols)[:, ep - 30:ep], op=ALU.mult),
                             reads=Tk(tt) + ["aux.%d" % c], writes=bk(2 * pa, 2 * pa + 1) + ["hist"])
                        if sd["sample"]:
                            s.op("dve", lambda h, pa=pa, tt=tt, c=c: h.tensor_tensor(out=outS[:, c, 2:32], in0=Tf(tt)[:, 674:704], in1=pairv(pa, ncols)[:, 674:704], op=ALU.mult),
                                 reads=Tk(tt), writes=bk(2 * pa, 2 * pa + 1) + ["outS"])
                zsl = [fetch_in(2, 2 * E + g * 256) for g in range(8)]
                hoist_wo()
                pend_stats = None
                for c in range(16):
                    par = c % 2
                    dg, dkeys = diag_view(c)
                    pc = c % 2
                    G = AUX[:, c, :]
                    nts = ntiles(pc, ncols)

                    acc = Tf(c % 2)
                    s.op("act", lambda h, G=G, acc=acc, c=c: h.activation(out=acc[:, 0:ncols], in_=G[:, 0:ncols], func=AF.Copy, scale=wccT[:, c, 0:1]),
                         reads=["aux.%d" % c, "wccT"], writes=Tk(c % 2))
                    for k in range(1, NTD):
                        s.op("dve", lambda h, G=G, acc=acc, c=c, k=k: h.scalar_tensor_tensor(out=acc[:, 0:ncols], in0=G[:, k:k + ncols], scalar=wccT[:, c, k:k + 1],
                                                                                        in1=acc[:, 0:ncols], op0=ALU.mult, op1=ALU.add),
                             reads=["aux.%d" % c, "wccT"] + Tk(c % 2), writes=Tk(c % 2))

                    def f(h, dg=dg, G=G, nts=nts):
                        ins = None
                        for (c0, n, b, bo) in nts:
                            for k in range(NTD, 31):
                                ins = h.matmul(bank(b)[:, bo:bo + n], lhsT=dg[:, k - NTD, :], rhs=G[:, c0 + k:c0 + k + n], start=(k == NTD), stop=(k == 30))
                        return ins
                    s.op("pe", f, reads=dkeys + ["aux.%d" % c], writes=bk(2 * pc, 2 * pc + 1))
                    if c + 2 < 16:
                        build_diag(c + 2)
                    s.op("dve", lambda h, pc=pc, G=G, c=c, acc=acc: h.scalar_tensor_tensor(out=G[:, 30:30 + ncols], in0=pairv(pc, ncols), scalar=prmT[:, c, 3:4],
                                                                                          in1=acc[:, 0:ncols], op0=ALU.add, op1=ALU.add),
                         reads=["prmT"] + Tk(c % 2), writes=bk(2 * pc, 2 * pc + 1) + ["aux.%d" % c])
                    sqt = 2 + (c % 2)
                    s.op("act", lambda h, G=G, sqt=sqt: h.activation(out=Tb(sqt)[:, 0:ncols], in_=G[:, 30:30 + ncols], func=AF.Square),
                         reads=["aux.%d" % c], writes=Tk(sqt))

                    def f2(h, G=G, sqt=sqt, c=c):
                        ins = None
                        for (c0, n, bq, bo) in ntiles(0, ncols):
                            ins = h.matmul(bank(4 + bq)[:, bo:bo + n], lhsT=onesb, rhs=G[:, 30 + c0:30 + c0 + n], start=(c == 0), stop=(c == 15))
                            ins = h.matmul(bank(6 + bq)[:, bo:bo + n], lhsT=onesb, rhs=Tb(sqt)[:, c0:c0 + n], start=(c == 0), stop=(c == 15))
                        return ins
                    if pend_stats is not None:
                        s.op("pe", pend_stats[0], reads=pend_stats[1], writes=bk(4, 5, 6, 7))
                    pend_stats = (f2, ["aux.%d" % c, "onesb"] + Tk(sqt))
                s.op("pe", pend_stats[0], reads=pend_stats[1], writes=bk(4, 5, 6, 7))
                s.op("dve", lambda h: h.tensor_scalar_mul(out=Tf(0)[:, 0:ncols], in0=pairv(2, ncols), scalar1=1.0 / E), reads=[], writes=bk(4, 5) + Tk(0))
                s.op("dve", lambda h: h.tensor_tensor(out=Tf(2)[:, 0:ncols], in0=Tf(0)[:, 0:ncols], in1=Tf(0)[:, 0:ncols], op=ALU.mult), reads=Tk(0), writes=Tk(2))
                s.op("dve", lambda h: h.scalar_tensor_tensor(out=Tf(1)[:, 0:ncols], in0=pairv(3, ncols), scalar=1.0 / E, in1=Tf(2)[:, 0:ncols],
                                                              op0=ALU.mult, op1=ALU.subtract), reads=Tk(2), writes=bk(6, 7) + Tk(1))
                s.op("dve", lambda h: h.tensor_scalar_max(out=Tf(1)[:, 0:ncols], in0=Tf(1)[:, 0:ncols], scalar1=0.0), reads=Tk(1), writes=Tk(1))
                s.op("act", lambda h: h.activation(out=Tf(1)[:, 0:ncols], in_=Tf(1)[:, 0:ncols], func=AF.Sqrt, bias=EPS, scale=1.0), reads=Tk(1), writes=Tk(1))
                s.op("dve", lambda h: h.reciprocal(out=Tf(1)[:, 0:ncols], in_=Tf(1)[:, 0:ncols]), reads=Tk(1), writes=Tk(1))
                pend_m = None
                pair_rr["i"] = 0
                for grp in range(8):
                    sz = zsl[grp]
                    for ci in range(2):
                        c = grp * 2 + ci
                        pz = next_pair()
                        inproj(sz[0], sz[1], ci, pz, ncols)
                        G = AUX[:, c, :]
                        tq = 2 + (c % 2)
                        s.op("act", lambda h, pz=pz, c=c: h.activation(out=ysT[:, c, 0:ncols], in_=pairv(pz, ncols), func=AF.Silu),
                             reads=[], writes=bk(2 * pz, 2 * pz + 1) + ["ysT.%d" % c])
                        s.op("dve", lambda h, G=G, tq=tq: h.tensor_tensor(out=Tf(tq)[:, 0:ncols], in0=G[:, 30:30 + ncols], in1=Tf(0)[:, 0:ncols], op=ALU.subtract),
                             reads=["aux.%d" % c] + Tk(0), writes=Tk(tq))
                        s.op("dve", lambda h, tq=tq: h.tensor_tensor(out=Tf(tq)[:, 0:ncols], in0=Tf(tq)[:, 0:ncols], in1=Tf(1)[:, 0:ncols], op=ALU.mult),
                             reads=Tk(1, tq), writes=Tk(tq))
                        s.op("act", lambda h, c=c, G=G, tq=tq: h.activation(out=G[:, 30:30 + ncols], in_=Tf(tq)[:, 0:ncols], func=AF.Silu, scale=prmT[:, c, 4:5], bias=prmT[:, c, 5:6]),
                             reads=Tk(tq) + ["prmT"], writes=["aux.%d" % c])
                        if pend_m is not None:
                            pend_m()
                        pend_m = (lambda c=c, G=G: s.op("pool", lambda h: h.tensor_tensor(out=ysT[:, c, 0:ncols], in0=ysT[:, c, 0:ncols], in1=G[:, 30:30 + ncols], op=ALU.mult),
                                                          reads=["aux.%d" % c, "ysT.%d" % c], writes=["ysT.%d" % c]))
                pend_m()

            else:
                Wd = 15 + ncols
                pf = Pref([(lambda g=g: (fetch_in(3, g * 256), fetch_in(3, E + g * 256))) for g in range(8)])
                wp_box = {}
                for grp in range(8):
                    sp_, sz = pf.get(grp)
                    if grp == 4:
                        wp_box["wp"] = [fetch_pool(g) for g in range(4)]
                        hoist_wo()
                    for ci in range(2):
                        c = grp * 2 + ci
                        g = c // 4
                        win = 2 << g
                        pp, pz = next_pair(), next_pair()
                        inproj(sp_[0], sp_[1], ci, pp, ncols)
                        inproj(sz[0], sz[1], ci, pz, ncols)
                        s.op("act", lambda h, pp=pp: h.activation(out=Tf(0)[:, 15:15 + ncols], in_=pairv(pp, ncols), func=AF.Copy),
                             reads=[], writes=bk(2 * pp, 2 * pp + 1) + Tk(0))
                        s.op("act", lambda h, pz=pz, c=c: h.activation(out=ysT[:, c, 0:ncols], in_=pairv(pz, ncols), func=AF.Silu),
                             reads=[], writes=bk(2 * pz, 2 * pz + 1) + ["ysT.%d" % c])
                        fix_hist(Tf(0), 15, (32, 47), "T.0", c)
                        s.op("dve", lambda h, c=c: h.tensor_copy(out=hist[:, c, 32:47], in_=Tf(0)[:, 15 + ep - 15:15 + ep]), reads=Tk(0), writes=["hist"])
                        if sd["sample"]:
                            s.op("dve", lambda h, c=c: h.tensor_copy(out=outS[:, c, 32:47], in_=Tf(0)[:, 15 + 689:15 + 704]), reads=Tk(0), writes=["outS"])
                        s.op("dve", lambda h: h.tensor_tensor(out=Tf(1)[:, 1:Wd], in0=Tf(0)[:, 1:Wd], in1=Tf(0)[:, 0:Wd - 1], op=ALU.add), reads=Tk(0), writes=Tk(1))
                        fin = 1
                        if win >= 4:
                            s.op("dve", lambda h: h.tensor_tensor(out=Tf(2)[:, 3:Wd], in0=Tf(1)[:, 3:Wd], in1=Tf(1)[:, 1:Wd - 2], op=ALU.add), reads=Tk(1), writes=Tk(2))
                            fin = 2
                        if win >= 8:
                            s.op("dve", lambda h: h.tensor_tensor(out=Tf(1)[:, 7:Wd], in0=Tf(2)[:, 7:Wd], in1=Tf(2)[:, 3:Wd - 4], op=ALU.add), reads=Tk(2), writes=Tk(1))
                            fin = 1
                        if win >= 16:
                            s.op("dve", lambda h: h.tensor_tensor(out=Tf(2)[:, 15:Wd], in0=Tf(1)[:, 15:Wd], in1=Tf(1)[:, 7:Wd - 8], op=ALU.add), reads=Tk(1), writes=Tk(2))
                            fin = 2
                        s.op("dve", lambda h, fin=fin, win=win, c=c: h.scalar_tensor_tensor(out=AUX[:, c, 0:ncols], in0=Tf(fin)[:, 15:Wd], scalar=1.0 / win,
                                                                                            in1=Tf(0)[:, 15:Wd], op0=ALU.mult, op1=ALU.subtract),
                             reads=Tk(0, fin), writes=["aux.%d" % c])
                        if sti == 0:
                            s.op("dve", lambda h, fin=fin, g=g: h.tensor_tensor(out=Tf(3)[:, 0:16], in0=Tf(fin)[:, 15 + 128:15 + 144], in1=invc[:, g * 16:(g + 1) * 16], op=ALU.mult),
                                 reads=Tk(fin) + ["invc"], writes=Tk(3))
                            s.op("dve", lambda h, c=c: h.tensor_tensor(out=AUX[:, c, 128:144], in0=Tf(3)[:, 0:16], in1=Tf(0)[:, 15 + 128:15 + 144], op=ALU.subtract),
                                 reads=Tk(0, 3) + ["aux.%d" % c], writes=["aux.%d" % c])
                wp = wp_box["wp"]
                for fc in range(16):
                    g, fi = fc // 4, fc % 4
                    p = next_pair()
                    nts = ntiles(p, ncols)

                    def f(h, g=g, fi=fi, nts=nts):
                        ins = None
                        for cc in range(4):
                            for (c0, n, b, bo) in nts:
                                ins = h.matmul(bank(b)[:, bo:bo + n], lhsT=wp[g][1][:, cc, fi * 128:(fi + 1) * 128], rhs=AUX[:, 4 * g + cc, c0:c0 + n],
                                               start=(cc == 0), stop=(cc == 3))
                        return ins
                    s.op("pe", f, reads=["rs.%d" % wp[g][0]] + ["aux.%d" % (4 * g + cc) for cc in range(4)], writes=bk(2 * p, 2 * p + 1))
                    s.op("dve", lambda h, p=p, fc=fc: h.scalar_tensor_tensor(out=ysT[:, fc, 0:ncols], in0=pairv(p, ncols), scalar=prmT[:, fc, 6:7],
                                                                              in1=ysT[:, fc, 0:ncols], op0=ALU.mult, op1=ALU.mult),
                         reads=["prmT", "ysT.%d" % fc], writes=bk(2 * p, 2 * p + 1) + ["ysT.%d" % fc])

            hoist_wo()
            wo = wo_box["wo"]

            pendq = []
            for bi, bl in enumerate(blocks):
                n, xt, sq, c0 = bl["n"], bl["xt"], bl["sq"], bl["col0"]
                p = next_pair()
                gsrc = gateS if sq == 1 else gP
                gkey = "gateS" if sq == 1 else "gateP%d" % (l % 2)

                def f(h, n=n, c0=c0, p=p):
                    ins = None
                    for ec in range(16):
                        for hf in range(2):
                            ins = h.matmul(bank(2 * p + hf)[0:n, :], lhsT=ysT[:, ec, c0:c0 + n], rhs=wo[ec // 2][1][:, ec % 2, hf * 512:(hf + 1) * 512],
                                           start=(ec == 0), stop=(ec == 15))
                    return ins
                s.op("pe", f, reads=["ysT.%d" % i for i in range(16)] + ["rs.%d" % i for i, _ in wo], writes=bk(2 * p, 2 * p + 1))
                is_tail = (bi == len(blocks) - 1)
                if len(pendq) >= 2 and not is_tail:
                    pb = pendq.pop(0)
                    h_post(pb[0], pb[1], l + 1)
                for hf in range(2):
                    tmp = Tf(hf)[0:n, 0:512]
                    s.op("dve", lambda h, hf=hf, p=p, n=n, tmp=tmp, gsrc=gsrc: h.tensor_tensor(out=tmp, in0=bank(2 * p + hf)[0:n, :], in1=gsrc[0:n, hf * 512:(hf + 1) * 512], op=ALU.mult),
                         reads=[gkey], writes=bk(2 * p + hf) + Tk(hf))
                    s.op("dve", lambda h, hf=hf, xt=xt, tmp=tmp: h.tensor_tensor(out=xt[:, hf * 512:(hf + 1) * 512], in0=xt[:, hf * 512:(hf + 1) * 512], in1=tmp, op=ALU.add),
                         reads=Tk(hf) + [bl["key"]], writes=[bl["key"]])
                if not last:
                    h_pre(bi, bl, l + 1)
                    pendq.append((bi, bl))
                elif bl["gb"] is None or bl["gb"] >= 1:
                    so = 32 + (bi % 4) * 4
                    s.op("dve", lambda h, so=so, n=n: h.memset(stat[0:n, so:so + 1], 0.0), reads=[], writes=["statf%d" % (bi % 4)])
                    s.op("act", lambda h, xt=xt, so=so, n=n: h.activation(out=junkv[0:n, :], in_=xt, func=AF.Square, scale=1.0 / 32, accum_out=stat[0:n, so:so + 1]),
                         reads=[bl["key"]], writes=["junk", "statf%d" % (bi % 4)])
                    s.op("act", lambda h, so=so, n=n: h.activation(out=stat[0:n, so + 1:so + 2], in_=stat[0:n, so:so + 1], func=AF.Sqrt, bias=EPS, scale=1.0),
                         reads=["statf%d" % (bi % 4)], writes=["statf%d" % (bi % 4)])
                    s.op("dve", lambda h, so=so, n=n: h.reciprocal(out=stat[0:n, so + 2:so + 3], in_=stat[0:n, so + 1:so + 2]), reads=["statf%d" % (bi % 4)], writes=["statf%d" % (bi % 4)])
                    yq = bi % 2
                    yt = AUX_raw[0:n, 1024 + yq * 1024:2048 + yq * 1024]
                    yk = ["aux.g%d" % yq]
                    s.op("dve", lambda h, xt=xt, so=so, n=n, yt=yt: h.scalar_tensor_tensor(out=yt, in0=xt, scalar=stat[0:n, so + 2:so + 3], in1=gf_bc[0:n, :],
                                                                                            op0=ALU.mult, op1=ALU.mult),
                         reads=[bl["key"], "statf%d" % (bi % 4), "gf_bc"], writes=yk)
                    if bl["gb"] is None:
                        s.dma("sp", "o_y%d" % yq, ysm, yt, reads=yk, writes=["O.ysm"])
                    else:
                        gb = bl["gb"] - 1
                        s.dma("sp", "o_y%d" % yq, yp[gb * 128:(gb + 1) * 128, :], yt, reads=yk, writes=["O.yp%d" % gb])
            for pb in pendq:
                h_post(pb[0], pb[1], l + 1)

        for l in range(4):
            run_layer(l)
            if sti == 2:
                emit_state_outputs(l)

    for sti, sd in enumerate(ST_DEFS):
        run_st(sti, sd)

    okeys = ["O.ysm", "O.vrows", "O.cbp", "O.cbs", "O.ccp", "O.ccs", "O.pdp", "O.pds"] + ["O.yp%d" % i for i in range(16)]
    s.wait_all("sp", okeys)
    with nc.Block() as block:
        s.emit_all(block)
    return nc


_WKEYS = ["w_ada", "b_ada", "g_norm", "w_in_a", "w_in_b", "w_in_c", "w_in_d", "w_out_a", "w_out_b", "w_out_c", "w_out_d",
          "w_conv_c", "w_s_a", "b_s_a", "w_pool_d"]


def kernel(**inp):
    f = lambda k: np.ascontiguousarray(np.asarray(inp[k], dtype=np.float32))
    xpr, xsa = f("x_prompt"), f("x_sample")
    shared = {k: f(k) for k in _WKEYS}
    shared["prm9"] = np.ascontiguousarray(np.concatenate(
        [f("w_conv_b"), f("b_conv_c")[None], f("ln_c_g")[None], f("ln_c_b")[None], f("scale_pool_d")[None], f("ln_a_g")[None], f("ln_a_b")[None]], 0))
    shared["ln_a_gb"] = np.ascontiguousarray(np.stack([f("ln_a_g"), f("ln_a_b")], 0))
    shared["g_final"] = f("g_final")[None]
    cpr, csa = f("c_prompt"), f("c_sample")
    sb, sc, sd_ = f("state_conv_b"), f("state_conv_c"), f("state_pool_d")
    wins = (2, 4, 8, 16)
    in_maps = []
    for i in range(8):
        b, half = i // 2, i % 2
        halo = xpr[b, 1920:2048] if half == 1 else np.zeros((128, D), np.float32)
        m = dict(shared)
        m["xp"] = np.ascontiguousarray(np.concatenate([halo, xpr[b, half * 2048:(half + 1) * 2048]], 0))
        m["xsm"] = xsa[i]
        m["st_b"], m["st_c"], m["st_d"] = sb[i], sc[i], sd_[i]
        m["cvec"] = np.ascontiguousarray(np.stack([cpr[b], csa[i]], 0))
        m["maskc"] = np.full((128, 1), float(half), np.float32)
        tab = np.zeros((4, 16), np.float32)
        for g, w in enumerate(wins):
            for j in range(16):
                tab[g, j] = 1.0 / (min(w, j + 1) if half == 0 else w)
        m["invc"] = np.ascontiguousarray(np.broadcast_to(tab.reshape(1, 64), (128, 64)))
        in_maps.append(m)
    nc = build_program()
    res = run_bass_kernel_spmd(nc, in_maps, core_ids=list(range(8)))
    R = res.results
    y_prompt = np.zeros((4, 4096, D), np.float32)
    for i in range(8):
        y_prompt[i // 2, (i % 2) * 2048:(i % 2 + 1) * 2048] = R[i]["yp"]
    stk = lambda k, ids: np.stack([np.asarray(R[i][k], np.float32) for i in ids], 0)
    allc = list(range(8))
    odd = [1, 3, 5, 7]
    return (y_prompt, stk("ysm", allc), stk("vrows", allc), stk("cb_p", odd), stk("cb_s", allc),
            stk("cc_p", odd), stk("cc_s", allc), stk("pd_p", odd), stk("pd_s", allc))
```

```python
import numpy as np
import concourse.bass as bass
import concourse.mybir as mybir
from concourse.bass_utils import run_bass_kernel_spmd

F32 = mybir.dt.float32
BF16 = mybir.dt.bfloat16
ALU = mybir.AluOpType
AF = mybir.ActivationFunctionType

ENGS = ("pe", "act", "dve", "pool", "sp")
D = 1024
E = 2048
EPS = 1e-6


class Sched:
    def __init__(self, nc):
        self.nc = nc
        self.streams = {e: [] for e in ENGS}
        self.cnt = {}
        self.seen = {e: {} for e in ENGS}
        self.res = {}
        self.sem = {}
        for e in ENGS:
            self.sem[e] = nc.alloc_semaphore("p_" + e)
            self.cnt[e] = 0

    def _deps(self, eng, reads, writes):
        waits = {}

        def add(sv):
            if sv is None:
                return
            sname, v = sv
            if v > waits.get(sname, 0):
                waits[sname] = v

        for k in reads:
            r = self.res.get(k)
            if r is not None:
                add(r[0])
        for k in writes:
            r = self.res.get(k)
            if r is not None:
                add(r[0])
                for sname, v in r[1].items():
                    add((sname, v))
        out = []
        for sname, v in waits.items():
            if self.seen[eng].get(sname, 0) >= v:
                continue
            self.seen[eng][sname] = v
            out.append((sname, v))
        return out

    def _mark(self, src, val, reads, writes):
        for k in writes:
            self.res[k] = [(src, val), {}]
        for k in reads:
            r = self.res.get(k)
            if r is None:
                r = self.res[k] = [None, {}]
            r[1][src] = val

    def op(self, eng, fn, reads=(), writes=()):
        ro = getattr(self, "ring_out", None)
        if ro:
            for k in reads:
                ro.discard(k)
        waits = self._deps(eng, reads, writes)
        if eng == "pe":
            waits = [(a, v) for (a, v) in waits if a != "pe"]
        self.cnt[eng] += 1
        val = self.cnt[eng]
        sems = self.sem
        wl = [(sems[a], v) for a, v in waits]
        mysem = sems[eng]

        def emit(h):
            for sm, v in wl:
                h.wait_ge(sm, v)
            ins = fn(h)
            ins.then_inc(mysem, 1)

        self.streams[eng].append(emit)
        self._mark(eng, val, reads, writes)

    def dma(self, q, slot, out, in_, reads=(), writes=(), **kw):
        if slot not in self.sem:
            self.sem[slot] = self.nc.alloc_semaphore("d_" + slot)
            self.cnt[slot] = 0
        waits = self._deps(q, reads, writes)
        self.cnt[slot] += 16
        val = self.cnt[slot]
        sems = self.sem
        wl = [(sems[a], v) for a, v in waits]
        dsm = sems[slot]

        def emit(h):
            for sm, v in wl:
                h.wait_ge(sm, v)
            h.dma_start(out=out, in_=in_, **kw).then_inc(dsm, 16)

        self.streams[q].append(emit)
        self._mark(slot, val, reads, writes)

    def wait_all(self, eng, keys):
        waits = self._deps(eng, keys, keys)
        sems = self.sem
        wl = [(sems[a], v) for a, v in waits]

        def emit(h):
            for sm, v in wl:
                h.wait_ge(sm, v)

        self.streams[eng].append(emit)

    def emit_all(self, block):
        st = self.streams

        @block.tensor
        def _(h):
            for f in st["pe"]:
                f(h)

        @block.scalar
        def _(h):
            for f in st["act"]:
                f(h)

        @block.vector
        def _(h):
            for f in st["dve"]:
                f(h)

        @block.gpsimd
        def _(h):
            for f in st["pool"]:
                f(h)

        @block.sync
        def _(h):
            for f in st["sp"]:
                f(h)


def build_program():
    nc = bass.Bass("TRN2", target_bir_lowering=False)

    def din(name, shape):
        return nc.dram_tensor(name, list(shape), F32, kind="ExternalInput").ap()

    def dout(name, shape):
        return nc.dram_tensor(name, list(shape), F32, kind="ExternalOutput").ap()

    xp = din("xp", [2176, D])
    xsm = din("xsm", [32, D])
    st_b = din("st_b", [2, E])
    st_c = din("st_c", [30, E])
    st_d = din("st_d", [15, E])
    cvec = din("cvec", [2, D])
    maskc_d = din("maskc", [128, 1])
    invc_d = din("invc", [128, 64])
    w_ada = din("w_ada", [4, D, 3 * D])
    b_ada = din("b_ada", [4, 3 * D])
    g_norm = din("g_norm", [4, D])
    w_in = [din("w_in_a", [D, 3 * E]), din("w_in_b", [D, 4 * E]), din("w_in_c", [D, 3 * E]), din("w_in_d", [D, 2 * E])]
    w_out = [din("w_out_a", [E, D]), din("w_out_b", [E, D]), din("w_out_c", [E, D]), din("w_out_d", [E, D])]
    prm9 = din("prm9", [9, E])
    w_conv_c = din("w_conv_c", [31, E])
    w_s_a = din("w_s_a", [8, 128, 128])
    b_s_a = din("b_s_a", [8, 128])
    w_pool = din("w_pool_d", [4, 512, 512])
    g_final = din("g_final", [1, D])
    ln_a_gb = din("ln_a_gb", [2, E])

    yp = dout("yp", [2048, D])
    ysm = dout("ysm", [32, D])
    vrows = dout("vrows", [32, E])
    o_cb_p = dout("cb_p", [2, E]); o_cb_s = dout("cb_s", [2, E])
    o_cc_p = dout("cc_p", [30, E]); o_cc_s = dout("cc_s", [30, E])
    o_pd_p = dout("pd_p", [15, E]); o_pd_s = dout("pd_s", [15, E])

    s = Sched(nc)
    TOTAL = 53000
    big = nc.alloc_sbuf_tensor("big", [128, TOTAL], F32)
    off = [0]

    def carve(nw):
        v = big[:, off[0]:off[0] + nw]
        off[0] += nw
        assert off[0] <= TOTAL, off[0]
        return v

    def v3(v, a):
        return v.rearrange("p (a b) -> p a b", a=a)

    X = v3(carve(6 * 1024), 6)
    XS = X[:, 5, :]
    hT = v3(carve(3072).bitcast(BF16), 8)
    ysT_raw = carve(6144)
    ysT = v3(ysT_raw.bitcast(BF16), 16)
    AUX_raw = carve(6400)
    AUX = v3(AUX_raw.bitcast(BF16), 16)
    RING = carve(16 * 1024)
    Traw = [carve(800) for _ in range(4)]
    gateP = [carve(1024), carve(1024)]
    gateS = carve(1024)
    dscr = [carve(128), carve(128)]
    scTb = v3(carve(8).bitcast(BF16), 8)
    gf_bc = carve(1024)
    biasT = v3(carve(2048), 16)
    biasT_s = v3(carve(512), 16)
    WsT = v3(carve(512).bitcast(BF16), 8)
    WsT_s = v3(carve(128).bitcast(BF16), 8)
    identb = carve(64).bitcast(BF16)
    identf = carve(128)
    onesf = carve(128)
    onesb = carve(64).bitcast(BF16)
    prmT = v3(carve(16 * 9), 16)
    wccT = v3(carve(16 * 32), 16)
    stT = v3(carve(16 * 48), 16)
    hist = v3(carve(16 * 48), 16)
    outS = v3(carve(16 * 48), 16)
    gnT = v3(carve(32), 8)
    badaT = v3(carve(96), 24)
    cT = v3(carve(16), 8)
    scT = v3(carve(16), 8)
    modT = carve(4 * 48)
    gsT = carve(4 * 16)
    stat = carve(64)
    maskc = carve(1)
    invc = carve(64)

    def modv(l):
        return v3(modT[:, l * 48:(l + 1) * 48], 24)

    def gsv(l):
        return v3(gsT[:, l * 16:(l + 1) * 16], 8)

    PS = nc.alloc_psum_tensor("ps", [128, 4096], F32)

    def bank(b):
        return PS[:, b * 512:(b + 1) * 512]

    def bk(*bs):
        return ["bk.%d" % b for b in bs]

    def Tf(i, n=800):
        return Traw[i][:, 0:n]

    def Tb(i, n=1600):
        return Traw[i].bitcast(BF16)[:, 0:n]

    def Tk(*idx):
        return ["T.%d" % i for i in idx]

    Tcat = big[:, Traw[0].offset - big.offset:Traw[0].offset - big.offset + 3200] if False else None

    t0_off = 6 * 1024 + 1024 + 3072 + 6144 + 6400 + 16 * 1024
    stage = big[:, t0_off:t0_off + 3200]
    Tpair01 = big[:, t0_off:t0_off + 1600]

    s.op("pool", lambda h: h.memset(identf, 0.0), writes=["identf"])
    s.op("pool", lambda h: h.affine_select(out=identf, in_=identf, pattern=[[-1, 128]], compare_op=ALU.not_equal,
                                           fill=1.0, base=0, channel_multiplier=1), reads=["identf"], writes=["identf"])
    s.op("dve", lambda h: h.tensor_copy(out=identb, in_=identf), reads=["identf"], writes=["identb"])
    s.op("dve", lambda h: h.memset(onesf, 1.0), writes=["onesf"])
    s.op("dve", lambda h: h.memset(onesb, 1.0), writes=["onesb"])
    s.op("dve", lambda h: h.memset(stat[:, 0:63], 0.0), writes=["stat_init"])
    s.op("dve", lambda h: h.memset(stat[:, 63:64], -0.5), writes=["neghalf"])
    s.op("dve", lambda h: h.memset(hist.rearrange("p a b -> p (a b)"), 0.0), writes=["hist"])
    s.dma("sp", "su_m", maskc, maskc_d, writes=["maskc"])
    s.dma("sp", "su_i", invc, invc_d, writes=["invc"])
    s.dma("sp", "su_g", gf_bc, g_final.partition_broadcast(128), writes=["gf_bc"])

    def load_rows_T(src, R, C, dst3, dkey, rp=32):
        nch = C // 128
        s.dma("sp", "su_st", stage[0:R, 0:C], src, writes=Tk(0, 1, 2, 3))
        pst = v3(bank(0)[:, 0:nch * rp], nch)

        def f(h):
            ins = None
            for c in range(nch):
                ins = h.transpose(out=pst[:, c, 0:R], in_=stage[0:R, c * 128:(c + 1) * 128], identity=identf[0:R, 0:R])
            return ins
        s.op("pe", f, reads=Tk(0, 1, 2, 3) + ["identf"], writes=bk(0))
        s.op("dve", lambda h: h.tensor_copy(out=dst3[:, :, 0:R], in_=pst[:, :, 0:R]), reads=[], writes=bk(0) + [dkey])

    load_rows_T(cvec, 2, D, cT, "cT")
    load_rows_T(g_norm, 4, D, gnT, "gnT")
    load_rows_T(b_ada, 4, 3 * D, badaT, "badaT", rp=4)
    load_rows_T(prm9, 9, E, prmT, "prmT")
    load_rows_T(w_conv_c, 31, E, wccT, "wccT")
    load_rows_T(st_b, 2, E, stT[:, :, 0:2], "stT")
    load_rows_T(st_c, 30, E, stT[:, :, 2:32], "stT")
    load_rows_T(st_d, 15, E, stT[:, :, 32:47], "stT")
    s.op("act", lambda h: h.activation(out=scT.rearrange("p a b -> p (a b)"), in_=cT.rearrange("p a b -> p (a b)"), func=AF.Silu),
         reads=["cT"], writes=["scT"])

    wsI = v3(stage[:, 0:1024], 8)
    s.dma("sp", "su_st", wsI, w_s_a.rearrange("g i j -> i g j"), writes=Tk(0, 1, 2))
    ps_w = v3(PS[:, 0:1024], 8)

    def f_ws_s(h):
        ins = None
        for g in range(8):
            ins = h.transpose(out=ps_w[0:32, g, 0:32], in_=wsI[0:32, g, 0:32], identity=identf[0:32, 0:32])
        return ins
    s.op("pe", f_ws_s, reads=Tk(0, 1, 2) + ["identf"], writes=bk(0, 1))
    s.op("dve", lambda h: h.tensor_copy(out=WsT_s[0:32, :, :], in_=ps_w[0:32, :, 0:32]), reads=[], writes=bk(0, 1) + ["WsT_s"])
    s.op("dve", lambda h: h.memset(wsI[0:64, :, 64:128], 0.0), reads=[], writes=Tk(0, 1, 2))

    def f_ws(h):
        ins = None
        for g in range(8):
            ins = h.transpose(out=ps_w[:, g, :], in_=wsI[:, g, :], identity=identf)
        return ins
    s.op("pe", f_ws, reads=Tk(0, 1, 2) + ["identf"], writes=bk(0, 1))
    s.op("dve", lambda h: h.tensor_copy(out=WsT, in_=ps_w), reads=[], writes=bk(0, 1) + ["WsT"])
    def f_rs(h):
        ins = None
        for hf in range(2):
            ins = h.matmul(bank(2 + hf), lhsT=onesb, rhs=WsT[:, hf * 4:(hf + 1) * 4, :].rearrange("p a b -> p (a b)"), start=True, stop=True)
        return ins
    s.op("pe", f_rs, reads=["WsT", "onesb"], writes=bk(2, 3))
    rs_bc = v3(PS[:, 1024:2048], 8)

    def f_rs_s(h):
        return h.matmul(bank(4)[:, 0:256], lhsT=onesb[0:32, :], rhs=WsT_s[0:32, :, :].rearrange("p a b -> p (a b)"), start=True, stop=True)
    s.op("pe", f_rs_s, reads=["WsT_s", "onesb"], writes=bk(4))
    rs_bc_s = v3(bank(4)[:, 0:256], 8)
    bs_bc = v3(stage[:, 1024:2048], 8)
    s.dma("sp", "su_bs", bs_bc.rearrange("p a b -> p (a b)"), b_s_a.rearrange("g i -> (g i)").partition_broadcast(128), writes=Tk(1, 2))
    for cc in range(16):
        g = cc // 2
        s.op("dve", lambda h, cc=cc, g=g: h.scalar_tensor_tensor(out=biasT[:, cc, :], in0=rs_bc[:, g, :], scalar=prmT[:, cc, 8:9],
                                                                  in1=bs_bc[:, g, :], op0=ALU.mult, op1=ALU.add),
             reads=["prmT"] + Tk(1, 2), writes=bk(2, 3) + ["biasT"])
        s.op("dve", lambda h, cc=cc, g=g: h.scalar_tensor_tensor(out=biasT_s[:, cc, :], in0=rs_bc_s[:, g, :], scalar=prmT[:, cc, 8:9],
                                                                  in1=bs_bc[:, g, 0:32], op0=ALU.mult, op1=ALU.add),
             reads=["prmT"] + Tk(1, 2), writes=bk(4) + ["biasT_s"])

    s.op("dve", lambda h: h.tensor_copy(out=scTb.rearrange("p a b -> p (a b)"), in_=scT.rearrange("p a b -> p (a b)")), reads=["scT"], writes=["scTb"])
    ring_state = {"next": 0}

    ring_out = set()
    s.ring_out = ring_out

    def ring_slot():
        i = ring_state["next"]
        ring_state["next"] = (i + 1) % 16
        assert ("rs.%d" % i) not in ring_out, "ring slot %d re-allocated before its consumer was recorded" % i
        ring_out.add("rs.%d" % i)
        return i

    def slot_bf(i):
        return RING[:, i * 1024:(i + 1) * 1024].bitcast(BF16)

    def fetch_in(l, col0):
        i = ring_slot()
        v = v3(slot_bf(i), 8)
        s.dma("pool", "rs%d" % i, v, w_in[l].rearrange("(kc p) n -> p kc n", p=128)[:, :, col0:col0 + 256], writes=["rs.%d" % i])
        return i, v

    def fetch_out(l, e2):
        i = ring_slot()
        v = v3(slot_bf(i), 2)
        s.dma("pool", "rs%d" % i, v, w_out[l][e2 * 256:(e2 + 1) * 256, :].rearrange("(a p) n -> p a n", p=128), writes=["rs.%d" % i])
        return i, v

    def ada_fetch(l, part):
        sl = []
        for j in range(4):
            cb = part * 4 + j
            i = ring_slot()
            v = v3(slot_bf(i), 8)
            s.dma("pool", "rs%d" % i, v, w_ada[l].rearrange("(kc p) n -> p kc n", p=128)[:, :, cb * 256:(cb + 1) * 256], writes=["rs.%d" % i])
            sl.append((i, v))
        return sl

    def ada_compute(l, part, sl):
        psA = bank(7)

        def f(h):
            ins = None
            for j in range(4):
                for ec in range(2):
                    for kc in range(8):
                        col = (j * 2 + ec) * 2
                        ins = h.matmul(psA[:, col:col + 2], lhsT=sl[j][1][:, kc, ec * 128:(ec + 1) * 128], rhs=scTb[:, kc, :],
                                       start=(kc == 0), stop=(kc == 7))
            return ins
        s.op("pe", f, reads=["rs.%d" % i for i, _ in sl] + ["scTb"], writes=bk(7))
        psA3 = v3(psA[:, 0:16], 8)
        mv = modv(l)
        for sq in range(2):
            s.op("dve", lambda h, sq=sq: h.tensor_tensor(out=mv[:, 8 * part:8 * part + 8, sq], in0=psA3[:, :, sq], in1=badaT[:, 8 * part:8 * part + 8, l], op=ALU.add),
                 reads=["badaT"], writes=bk(7) + ["modT"])
            if part == 1:
                s.op("dve", lambda h, sq=sq: h.scalar_tensor_tensor(out=gsv(l)[:, :, sq], in0=mv[:, 8:16, sq], scalar=1.0,
                                                                     in1=gnT[:, :, l], op0=ALU.add, op1=ALU.mult),
                     reads=["modT", "gnT"], writes=["gsT"])

    def gate_rows(l, sq, dst, dkey):
        mv = modv(l)
        p = next_pair()

        def f(h):
            ins = None
            for kc in range(8):
                col = mv[:, 16 + kc, sq:sq + 1]
                lb = bass.AP(col.tensor, col.offset, [list(col.ap[0]), [0, 128]])
                ins = h.matmul(PS[:, p * 1024 + kc * 128:p * 1024 + (kc + 1) * 128], lhsT=lb, rhs=identf, start=True, stop=True)
            return ins
        s.op("pe", f, reads=["modT", "identf"], writes=bk(2 * p, 2 * p + 1))
        s.op("act", lambda h: h.activation(out=dst, in_=PS[:, p * 1024:(p + 1) * 1024], func=AF.Copy), reads=[], writes=bk(2 * p, 2 * p + 1) + [dkey])

    class Pref:
        def __init__(self, thunks, depth=1):
            self.th, self.res, self.n, self.depth = thunks, {}, 0, depth

        def get(self, g):
            while self.n < len(self.th) and self.n <= g + self.depth:
                self.res[self.n] = self.th[self.n]()
                self.n += 1
            return self.res.pop(g)

    def fetch_pool(g):
        i = ring_slot()
        v = v3(slot_bf(i), 4)
        s.dma("pool", "rs%d" % i, v, w_pool[g].rearrange("(a p) n -> p a n", p=128), writes=["rs.%d" % i])
        return i, v

    for part in range(2):
        ada_compute(0, part, ada_fetch(0, part))

    ST_DEFS = [
        dict(b0=0, nb=6, ncols=768, ep=768, sample=False),
        dict(b0=6, nb=6, ncols=768, ep=768, sample=False),
        dict(b0=12, nb=5, ncols=704, ep=640, sample=True),
    ]
    pair_rr = {"i": 0}

    def next_pair():
        p = pair_rr["i"]
        pair_rr["i"] = (p + 1) % 4
        return p

    def pairv(p, n):
        return PS[:, p * 1024:p * 1024 + n]

    cs_box = {"cs": 0}

    def ntiles(p, ncols):
        cs = cs_box["cs"]
        return [(cs, 512 - cs, 2 * p, cs), (512, ncols - 512, 2 * p + 1, 0)]

    def inproj(slot_i, wv, ci, p, ncols):
        nts = ntiles(p, ncols)

        def f(h):
            ins = None
            for kc in range(8):
                for (c0, n, b, bo) in nts:
                    ins = h.matmul(bank(b)[:, bo:bo + n], lhsT=wv[:, kc, ci * 128:(ci + 1) * 128], rhs=hT[:, kc, c0:c0 + n],
                                   start=(kc == 0), stop=(kc == 7))
            return ins
        s.op("pe", f, reads=["rs.%d" % slot_i] + ["hT.%d" % i for i in range(6)] + ["hTb.%d" % i for i in range(6)], writes=bk(2 * p, 2 * p + 1))

    def store_rows_T(src3, R, dst, okey, skey):
        pso = PS[0:R, 0:2048]

        def f(h):
            ins = None
            for c in range(16):
                ins = h.transpose(out=pso[:, c * 128:(c + 1) * 128], in_=src3[:, c, :], identity=identf)
            return ins
        s.op("pe", f, reads=[skey, "identf"], writes=bk(0, 1, 2, 3))
        s.op("act", lambda h: h.activation(out=stage[0:R, 0:1024], in_=pso[:, 0:1024], func=AF.Copy), reads=[], writes=bk(0, 1) + Tk(0, 1, 2))
        s.op("act", lambda h: h.activation(out=stage[0:R, 1024:2048], in_=pso[:, 1024:2048], func=AF.Copy), reads=[], writes=bk(2, 3) + Tk(0, 1, 2))
        s.dma("sp", "o_st", dst, stage[0:R, 0:2048], reads=Tk(0, 1, 2), writes=[okey])


    def emit_state_outputs(l):
        if l == 1:
            store_rows_T(hist[:, :, 0:2], 2, o_cb_p, "O.cbp", "hist")
            store_rows_T(outS[:, :, 0:2], 2, o_cb_s, "O.cbs", "outS")
        elif l == 2:
            store_rows_T(hist[:, :, 2:32], 30, o_cc_p, "O.ccp", "hist")
            store_rows_T(outS[:, :, 2:32], 30, o_cc_s, "O.ccs", "outS")
        elif l == 3:
            store_rows_T(hist[:, :, 32:47], 15, o_pd_p, "O.pdp", "hist")
            store_rows_T(outS[:, :, 32:47], 15, o_pd_s, "O.pds", "outS")

    def run_st(sti, sd):
        ncols, ep, nb = sd["ncols"], sd["ep"], sd["nb"]
        blocks = []
        for j in range(nb):
            blocks.append(dict(xt=X[:, j, :], n=128, col0=j * 128, sq=0, key="x.%d" % j, gb=sd["b0"] + j))
        if sd["sample"]:
            blocks.append(dict(xt=XS[0:32, :], n=32, col0=672, sq=1, key="x.5", gb=None))
        for j in range(nb):
            gb = sd["b0"] + j
            s.dma("sp", "x%d" % j, X[:, j, :], xp[gb * 128:(gb + 1) * 128, :], writes=["x.%d" % j])
        if sd["sample"]:
            s.dma("sp", "x5", XS[0:32, :], xsm, writes=["x.5"])

        def fix_hist(buf, H, hsl, key, c, first_mask_ok=True):
            h0, h1 = hsl
            if sti == 0:
                s.op("dve", lambda h: h.memset(buf[:, 0:H], 0.0), writes=[key])
                s.op("dve", lambda h: h.tensor_scalar_mul(out=buf[:, H + 128 - H:H + 128], in0=buf[:, H + 128 - H:H + 128], scalar1=maskc[:, 0:1]),
                     reads=["maskc", key], writes=[key])
            else:
                s.op("dve", lambda h: h.tensor_copy(out=buf[:, 0:H], in_=hist[:, c, h0:h1]), reads=["hist"], writes=[key])
            if sd["sample"]:
                s.op("dve", lambda h: h.tensor_copy(out=buf[:, H + 672 - H:H + 672], in_=stT[:, c, h0:h1]), reads=["stT", key], writes=[key])

        junkv = AUX_raw[:, 0:512].bitcast(BF16)

        def h_pre(bi, bl, lt):
            n, xt = bl["n"], bl["xt"]
            so = (bi % 4) * 4
            xi = bi % 3
            xn = AUX_raw[:, 3072 + xi * 512:3584 + xi * 512].bitcast(BF16)[0:n, 0:1024]
            s.op("dve", lambda h: h.memset(stat[0:n, so:so + 1], 0.0), reads=[], writes=["stat%d" % (bi % 4)])
            s.op("act", lambda h: h.activation(out=junkv[0:n, :], in_=xt, func=AF.Square, scale=1.0 / 32, accum_out=stat[0:n, so:so + 1]),
                 reads=[bl["key"]], writes=["junk", "stat%d" % (bi % 4)])
            s.op("act", lambda h: h.activation(out=stat[0:n, so + 1:so + 2], in_=stat[0:n, so:so + 1], func=AF.Sqrt, bias=EPS, scale=1.0),
                 reads=["stat%d" % (bi % 4)], writes=["stat%d" % (bi % 4)])
            s.op("dve", lambda h: h.reciprocal(out=stat[0:n, so + 2:so + 3], in_=stat[0:n, so + 1:so + 2]), reads=["stat%d" % (bi % 4)], writes=["stat%d" % (bi % 4)])
            s.op("act", lambda h: h.activation(out=xn, in_=xt, func=AF.Copy, scale=stat[0:n, so + 2:so + 3]),
                 reads=[bl["key"], "stat%d" % (bi % 4)], writes=["xn%d" % xi])

        def h_post(bi, bl, lt):
            n, sq, c0 = bl["n"], bl["sq"], bl["col0"]
            xi = bi % 3
            xn = AUX_raw[:, 3072 + xi * 512:3584 + xi * 512].bitcast(BF16)[0:n, 0:1024]
            mvt, gst = modv(lt), gsv(lt)
            p = next_pair()
            ptA = v3(bank(2 * p).bitcast(BF16)[:, 0:512], 4)
            ptB = v3(bank(2 * p + 1).bitcast(BF16)[:, 0:512], 4)

            def f(h):
                ins = None
                for kc in range(8):
                    dstp = ptA[:, kc, 0:n] if kc < 4 else ptB[:, kc - 4, 0:n]
                    ins = h.transpose(out=dstp, in_=xn[:, kc * 128:(kc + 1) * 128], identity=identb[0:n, 0:n])
                return ins
            s.op("pe", f, reads=["xn%d" % xi, "identb"], writes=bk(2 * p, 2 * p + 1))
            for kc in range(4):
                s.op("dve", lambda h, kc=kc: h.tensor_scalar(
                    out=hT[:, kc, c0:c0 + n], in0=ptA[:, kc, 0:n], scalar1=gst[:, kc, sq:sq + 1], scalar2=mvt[:, kc, sq:sq + 1],
                    op0=ALU.mult, op1=ALU.add), reads=["gsT", "modT"], writes=bk(2 * p) + ["hT.%d" % bi])
            for kc in range(4, 8):
                s.op("act", lambda h, kc=kc: h.activation(
                    out=hT[:, kc, c0:c0 + n], in_=ptB[:, kc - 4, 0:n], func=AF.Identity, scale=gst[:, kc, sq:sq + 1], bias=mvt[:, kc, sq:sq + 1]),
                    reads=["gsT", "modT"], writes=bk(2 * p + 1) + ["hTb.%d" % bi])

        hT_keys = ["hT.%d" % i for i in range(len(blocks))]

        def run_layer(l):
            mv = modv(l)
            last = (l == 3)
            cs_box["cs"] = 80 if (sti == 0 and l >= 1) else 0
            gP = gateP[l % 2]
            if l == 0:
                if sti > 0:
                    gate_rows(0, 0, gP, "gateP0")
                for bi, bl in enumerate(blocks):
                    h_pre(bi, bl, 0)
                    if bi >= 2:
                        h_post(bi - 2, blocks[bi - 2], 0)
                for bi in range(max(0, len(blocks) - 2), len(blocks)):
                    h_post(bi, blocks[bi], 0)
            if sd["sample"]:
                gate_rows(l, 1, gateS, "gateS")

            ada_sl = {}
            wo_box = {}

            def hoist_wo():
                if "wo" not in wo_box:
                    wo_box["wo"] = [fetch_out(l, e2) for e2 in range(8)]

            def mid_hook(grp):
                if l >= 3:
                    return
                if sti == 0:
                    if 2 <= grp <= 4:
                        ada_compute(l + 1, grp - 2, ada_sl[grp - 2])
                    if 1 <= grp <= 3:
                        ada_sl[grp - 1] = ada_fetch(l + 1, grp - 1)
                if grp == 5:
                    gate_rows(l + 1, 0, gateP[(l + 1) % 2], "gateP%d" % ((l + 1) % 2))

            if l == 0:
                wv_slots = [fetch_in(0, E + q * 256) for q in range(8)]

                def v_front(bi, bl):
                    n, c0 = bl["n"], bl["col0"]
                    smp = bl["sq"] == 1
                    gv = AUX_raw[0:n, (bi % 2) * 2048:(bi % 2) * 2048 + 2048]
                    vn = AUX_raw[:, 4096:6144].bitcast(BF16)[0:n, (bi % 2) * 2048:(bi % 2) * 2048 + 2048]
                    gk = ["aux.g%d" % (bi % 2)]
                    vk = ["aux.v%d" % (bi % 2)]
                    sk = ["statv%d" % (bi % 2)]

                    for vh in range(2):
                        def f(h, vh=vh):
                            ins = None
                            for kc in range(8):
                                for q in range(4 * vh, 4 * vh + 4):
                                    ins = h.matmul(bank(q // 2)[0:n, (q % 2) * 256:(q % 2) * 256 + 256], lhsT=hT[:, kc, c0:c0 + n],
                                                   rhs=wv_slots[q][1][:, kc, :], start=(kc == 0 and q % 2 == 0), stop=(kc == 7),
                                                   skip_group_check=True)
                            return ins
                        s.op("pe", f, reads=["hT.%d" % bi, "hTb.%d" % bi] + ["rs.%d" % i for i, _ in wv_slots[4 * vh:4 * vh + 4]], writes=bk(2 * vh, 2 * vh + 1))
                    so = 16 + (bi % 2) * 8
                    s.op("act", lambda h: h.activation(out=stat[0:n, so:so + 3], in_=stat[0:n, so:so + 3], func=AF.Copy, scale=0.0), reads=sk + ["stat_init"], writes=sk)
                    for hf in range(2):
                        s.op("act", lambda h, hf=hf: h.activation(out=gv[:, hf * 1024:(hf + 1) * 1024], in_=PS[0:n, hf * 1024:(hf + 1) * 1024],
                                                                  func=AF.Gelu, accum_out=stat[0:n, so + hf:so + hf + 1]),
                             reads=[], writes=bk(2 * hf, 2 * hf + 1) + gk + sk + (["xn0", "xn1"] if bi % 2 == 1 else []))
                    junk = Tpair01.bitcast(BF16)[0:n, 0:2048]
                    s.op("act", lambda h: h.activation(out=junk, in_=gv, func=AF.Square, accum_out=stat[0:n, so + 2:so + 3]),
                         reads=gk, writes=Tk(0, 1) + sk)
                    return gv, vn, gk, vk, sk, so, n, smp

                def v_stats(bi, bl, ctx):
                    gv, vn, gk, vk, sk, so, n, smp = ctx
                    s.op("dve", lambda h: h.tensor_tensor(out=stat[0:n, so + 3:so + 4], in0=stat[0:n, so:so + 1], in1=stat[0:n, so + 1:so + 2], op=ALU.add),
                         reads=sk, writes=sk)
                    s.op("dve", lambda h: h.tensor_scalar_mul(out=stat[0:n, so + 3:so + 4], in0=stat[0:n, so + 3:so + 4], scalar1=1.0 / E),
                         reads=sk, writes=sk)
                    s.op("dve", lambda h: h.tensor_tensor(out=stat[0:n, so + 4:so + 5], in0=stat[0:n, so + 3:so + 4], in1=stat[0:n, so + 3:so + 4], op=ALU.mult),
                         reads=sk, writes=sk)
                    s.op("dve", lambda h: h.scalar_tensor_tensor(out=stat[0:n, so + 5:so + 6], in0=stat[0:n, so + 2:so + 3], scalar=1.0 / E,
                                                                 in1=stat[0:n, so + 4:so + 5], op0=ALU.mult, op1=ALU.subtract),
                         reads=sk, writes=sk)
                    s.op("dve", lambda h: h.tensor_scalar_add(out=stat[0:n, so + 5:so + 6], in0=stat[0:n, so + 5:so + 6], scalar1=EPS), reads=sk, writes=sk)
                    s.op("pool", lambda h: h.tensor_tensor(out=stat[0:n, so + 6:so + 7], in0=stat[0:n, so + 5:so + 6], in1=stat[0:n, 63:64], op=ALU.pow),
                         reads=sk + ["neghalf"], writes=sk)
                    s.op("dve", lambda h: h.scalar_tensor_tensor(out=stat[0:n, so + 7:so + 8], in0=stat[0:n, so + 3:so + 4], scalar=-1.0,
                                                                 in1=stat[0:n, so + 6:so + 7], op0=ALU.mult, op1=ALU.mult),
                         reads=sk, writes=sk)
                    s.op("act", lambda h: h.activation(out=vn, in_=gv, func=AF.Identity, scale=stat[0:n, so + 6:so + 7], bias=stat[0:n, so + 7:so + 8]),
                         reads=gk + sk, writes=vk + (["xn2"] if bi % 2 == 0 else []))
                    if smp:
                        def vrows_out():
                            gx = gk + ["xn0", "xn1"]
                            s.op("dve", lambda h: h.tensor_scalar(out=gv, in0=gv, scalar1=stat[0:n, so + 6:so + 7], scalar2=stat[0:n, so + 7:so + 8],
                                                                  op0=ALU.mult, op1=ALU.add), reads=gx + vk + sk, writes=gx)
                            s.dma("sp", "su_st", stage[0:32, 0:2048], ln_a_gb[0:1, :].rearrange("a n -> (a n)").partition_broadcast(32), writes=Tk(0, 1, 2))
                            s.op("dve", lambda h: h.tensor_tensor(out=gv, in0=gv, in1=stage[0:32, 0:2048], op=ALU.mult), reads=Tk(0, 1, 2) + gx, writes=gx)
                            s.dma("sp", "su_st", stage[0:32, 0:2048], ln_a_gb[1:2, :].rearrange("a n -> (a n)").partition_broadcast(32), writes=Tk(0, 1, 2))
                            s.op("dve", lambda h: h.tensor_tensor(out=gv, in0=gv, in1=stage[0:32, 0:2048], op=ALU.add), reads=Tk(0, 1, 2) + gx, writes=gx)
                            s.dma("sp", "o_vr", vrows, gv, reads=gx, writes=["O.vrows"])
                        deferred_v.append(vrows_out)

                def v_back(bi, bl):
                    n, c0 = bl["n"], bl["col0"]
                    smp = bl["sq"] == 1
                    vn = AUX_raw[:, 4096:6144].bitcast(BF16)[0:n, (bi % 2) * 2048:(bi % 2) * 2048 + 2048]
                    vk = ["aux.v%d" % (bi % 2)]
                    wst = WsT_s if smp else WsT
                    bt = biasT_s if smp else biasT
                    mps = PS[:, 2048:2048 + 16 * n].rearrange("p (a b) -> p a b", a=16)

                    def f(h):
                        ins = None
                        for cc in range(16):
                            ins = h.matmul(mps[:, cc, :], lhsT=vn[:, cc * 128:(cc + 1) * 128], rhs=wst[0:n, cc // 2, 0:n], start=True, stop=True)
                        return ins
                    s.op("pe", f, reads=vk + ["WsT", "WsT_s"], writes=bk(4, 5, 6, 7))

                    def evac(lo, hi):
                        for cc in range(lo, hi):
                            s.op("dve", lambda h, cc=cc: h.scalar_tensor_tensor(
                                out=ysT[:, cc, c0:c0 + n], in0=mps[:, cc, :], scalar=prmT[:, cc, 7:8], in1=bt[:, cc, 0:n], op0=ALU.mult, op1=ALU.add),
                                reads=["prmT", "biasT", "biasT_s"], writes=bk(4, 5, 6, 7) + ["ysT.%d" % cc])
                    evac(0, 8)
                    return lambda: evac(8, 16)

                prevb = None
                deferred_v = []
                for bi, bl in enumerate(blocks):
                    ctx = v_front(bi, bl)
                    tail = v_back(*prevb) if prevb is not None else None
                    v_stats(bi, bl, ctx)
                    if tail is not None:
                        tail()
                    prevb = (bi, bl)
                v_back(*prevb)()
                for fdef in deferred_v:
                    fdef()
                pf = Pref([(lambda g=g: fetch_in(0, (g % 8) * 256 + (0 if g < 8 else 2 * E))) for g in range(16)], depth=3)
                for it in range(16):
                    if it == 0 and sti == 0:
                        ada_compute(0, 2, ada_fetch(0, 2))
                        gate_rows(0, 0, gP, "gateP0")
                    if it % 2 == 0:
                        mid_hook(it // 2)
                    sw = pf.get(it)
                    if it == 8:
                        hoist_wo()
                    isz = it >= 8
                    for ci in range(2):
                        c = (it % 8) * 2 + ci
                        pq = next_pair()
                        inproj(sw[0], sw[1], ci, pq, ncols)
                        tt_ = (2 if isz else 0) + (c % 2)
                        s.op("act", lambda h, pq=pq, tt_=tt_, isz=isz: h.activation(out=Tb(tt_)[:, 0:ncols], in_=pairv(pq, ncols), func=(AF.Silu if isz else AF.Gelu)),
                             reads=[], writes=bk(2 * pq, 2 * pq + 1) + Tk(tt_))
                        s.op("dve", lambda h, tt_=tt_, c=c: h.tensor_tensor(out=ysT[:, c, 0:ncols], in0=ysT[:, c, 0:ncols], in1=Tb(tt_)[:, 0:ncols], op=ALU.mult),
                             reads=Tk(tt_) + ["ysT.%d" % c], writes=["ysT.%d" % c])

            elif l == 1:
                pf = Pref([(lambda g=g: [fetch_in(1, b * E + g * 256) for b in range(4)]) for g in range(8)])
                for grp in range(8):
                    mid_hook(grp)
                    sl = pf.get(grp)
                    if grp == 4:
                        hoist_wo()
                    for ci in range(2):
                        c = grp * 2 + ci
                        pc_, ph_, pz_, pb_ = next_pair(), next_pair(), next_pair(), next_pair()
                        inproj(sl[1][0], sl[1][1], ci, pc_, ncols)
                        inproj(sl[2][0], sl[2][1], ci, ph_, ncols)
                        inproj(sl[3][0], sl[3][1], ci, pz_, ncols)
                        inproj(sl[0][0], sl[0][1], ci, pb_, ncols)
                        Pb = Tf(1)
                        s.op("act", lambda h, p=pc_: h.activation(out=Tf(0)[:, 0:ncols], in_=pairv(p, ncols), func=AF.Copy),
                             reads=[], writes=bk(2 * pc_, 2 * pc_ + 1) + Tk(0))
                        s.op("dve", lambda h, p=ph_: h.tensor_tensor(out=Tf(1)[:, 2:2 + ncols], in0=Tf(0)[:, 0:ncols], in1=pairv(p, ncols), op=ALU.mult),
                             reads=Tk(0), writes=bk(2 * ph_, 2 * ph_ + 1) + Tk(1))
                        s.op("act", lambda h, p=pz_, c=c: h.activation(out=ysT[:, c, 0:ncols], in_=pairv(p, ncols), func=AF.Silu),
                             reads=[], writes=bk(2 * pz_, 2 * pz_ + 1) + ["ysT.%d" % c])
                        s.op("act", lambda h, p=pb_: h.activation(out=Tf(3)[:, 0:ncols], in_=pairv(p, ncols), func=AF.Copy),
                             reads=[], writes=bk(2 * pb_, 2 * pb_ + 1) + Tk(3))
                        fix_hist(Pb, 2, (0, 2), "T.1", c)
                        s.op("dve", lambda h, c=c: h.tensor_copy(out=hist[:, c, 0:2], in_=Tf(1)[:, 2 + ep - 2:2 + ep]), reads=Tk(1), writes=["hist"])
                        if sd["sample"]:
                            s.op("dve", lambda h, c=c: h.tensor_copy(out=outS[:, c, 0:2], in_=Tf(1)[:, 2 + 702:2 + 704]), reads=Tk(1), writes=["outS"])
                        s.op("dve", lambda h, c=c: h.tensor_scalar_mul(out=Tf(2)[:, 0:ncols], in0=Tf(1)[:, 2:2 + ncols], scalar1=prmT[:, c, 2:3]),
                             reads=Tk(1) + ["prmT"], writes=Tk(2))
                        s.op("dve", lambda h, c=c: h.scalar_tensor_tensor(out=Tf(2)[:, 0:ncols], in0=Tf(1)[:, 1:1 + ncols], scalar=prmT[:, c, 1:2],
                                                                           in1=Tf(2)[:, 0:ncols], op0=ALU.mult, op1=ALU.add), reads=Tk(1, 2) + ["prmT"], writes=Tk(2))
                        s.op("dve", lambda h, c=c: h.scalar_tensor_tensor(out=Tf(2)[:, 0:ncols], in0=Tf(1)[:, 0:ncols], scalar=prmT[:, c, 0:1],
                                                                           in1=Tf(2)[:, 0:ncols], op0=ALU.mult, op1=ALU.add), reads=Tk(1, 2) + ["prmT"], writes=Tk(2))
                        s.op("dve", lambda h: h.tensor_tensor(out=Tf(2)[:, 0:ncols], in0=Tf(2)[:, 0:ncols], in1=Tf(3)[:, 0:ncols], op=ALU.mult),
                             reads=Tk(2, 3), writes=Tk(2))
                        s.op("dve", lambda h, c=c: h.tensor_tensor(out=ysT[:, c, 0:ncols], in0=ysT[:, c, 0:ncols], in1=Tf(2)[:, 0:ncols], op=ALU.mult),
                             reads=Tk(2) + ["ysT.%d" % c], writes=["ysT.%d" % c])

            elif l == 2:
                NTD = 3
                NTP = 31 - NTD

                def diag_view(c):
                    par = c % 2
                    dwords = ysT_raw[:, par * 2304:par * 2304 + NTP * 64].bitcast(BF16)
                    return dwords.rearrange("p (a b) -> p a b", a=NTP), ["ysT.%d" % i for i in range(par * 6, par * 6 + 6)]

                def build_diag(c):
                    dg, dkeys = diag_view(c)
                    id_bc = bass.AP(identb.tensor, identb.offset, [list(identb.ap[0]), [0, NTP], [1, 128]])
                    w_bc = wccT[:, c, NTD:31].unsqueeze(2).broadcast_to([128, NTP, 128])
                    s.op("pool", lambda h: h.tensor_tensor(out=dg, in0=id_bc, in1=w_bc, op=ALU.mult), reads=["identb", "wccT"], writes=dkeys)

                pf = Pref([(lambda g=g: (fetch_in(2, g * 256), fetch_in(2, E + g * 256))) for g in range(8)])
                for grp in range(8):
                    mid_hook(grp)
                    if grp >= 6:
                        build_diag(grp - 6)
                    sa, sg = pf.get(grp)
                    for ci in range(2):
                        c = grp * 2 + ci
                        pa, pg = next_pair(), next_pair()
                        inproj(sa[0], sa[1], ci, pa, ncols)
                        inproj(sg[0], sg[1], ci, pg, ncols)
                        tt = c % 2
                        G = AUX[:, c, :]
                        s.op("act", lambda h, pg=pg, tt=tt: h.activation(out=Tf(tt)[:, 0:ncols], in_=pairv(pg, ncols), func=AF.Sigmoid),
                             reads=[], writes=bk(2 * pg, 2 * pg + 1) + Tk(tt))
                        s.op("dve", lambda h, pa=pa, tt=tt, G=G: h.tensor_tensor(out=G[:, 30:30 + ncols], in0=Tf(tt)[:, 0:ncols], in1=pairv(pa, ncols), op=ALU.mult),
                             reads=Tk(tt), writes=bk(2 * pa, 2 * pa + 1) + ["aux.%d" % c])
                        fix_hist(G, 30, (2, 32), "aux.%d" % c, c)
                        s.op("dve", lambda h, pa=pa, tt=tt, c=c: h.tensor_tensor(out=hist[:, c, 2:32], in0=Tf(tt)[:, ep - 30:ep], in1=pairv(pa, ncols)[:, ep - 30:ep], op=ALU.mult),
                             reads=Tk(tt) + ["aux.%d" % c], writes=bk(2 * pa, 2 * pa + 1) + ["hist"])
                        if sd["sample"]:
                            s.op("dve", lambda h, pa=pa, tt=tt, c=c: h.tensor_tensor(out=outS[:, c, 2:32], in0=Tf(tt)[:, 674:704], in1=pairv(pa, ncols)[:, 674:704], op=ALU.mult),
                                 reads=Tk(tt), writes=bk(2 * pa, 2 * pa + 1) + ["outS"])
                zsl = [fetch_in(2, 2 * E + g * 256) for g in range(8)]
                hoist_wo()
                pend_stats = None
                for c in range(16):
                    par = c % 2
                    dg, dkeys = diag_view(c)
                    pc = c % 2
                    G = AUX[:, c, :]
                    nts = ntiles(pc, ncols)

                    acc = Tf(c % 2)
                    s.op("act", lambda h, G=G, acc=acc, c=c: h.activation(out=acc[:, 0:ncols], in_=G[:, 0:ncols], func=AF.Copy, scale=wccT[:, c, 0:1]),
                         reads=["aux.%d" % c, "wccT"], writes=Tk(c % 2))
                    for k in range(1, NTD):
                        s.op("dve", lambda h, G=G, acc=acc, c=c, k=k: h.scalar_tensor_tensor(out=acc[:, 0:ncols], in0=G[:, k:k + ncols], scalar=wccT[:, c, k:k + 1],
                                                                                        in1=acc[:, 0:ncols], op0=ALU.mult, op1=ALU.add),
                             reads=["aux.%d" % c, "wccT"] + Tk(c % 2), writes=Tk(c % 2))

                    def f(h, dg=dg, G=G, nts=nts):
                        ins = None
                        for (c0, n, b, bo) in nts:
                            for k in range(NTD, 31):
                                ins = h.matmul(bank(b)[:, bo:bo + n], lhsT=dg[:, k - NTD, :], rhs=G[:, c0 + k:c0 + k + n], start=(k == NTD), stop=(k == 30))
                        return ins
                    s.op("pe", f, reads=dkeys + ["aux.%d" % c], writes=bk(2 * pc, 2 * pc + 1))
                    if c + 2 < 16:
                        build_diag(c + 2)
                    s.op("dve", lambda h, pc=pc, G=G, c=c, acc=acc: h.scalar_tensor_tensor(out=G[:, 30:30 + ncols], in0=pairv(pc, ncols), scalar=prmT[:, c, 3:4],
                                                                                          in1=acc[:, 0:ncols], op0=ALU.add, op1=ALU.add),
                         reads=["prmT"] + Tk(c % 2), writes=bk(2 * pc, 2 * pc + 1) + ["aux.%d" % c])
                    sqt = 2 + (c % 2)
                    s.op("act", lambda h, G=G, sqt=sqt: h.activation(out=Tb(sqt)[:, 0:ncols], in_=G[:, 30:30 + ncols], func=AF.Square),
                         reads=["aux.%d" % c], writes=Tk(sqt))

                    def f2(h, G=G, sqt=sqt, c=c):
                        ins = None
                        for (c0, n, bq, bo) in ntiles(0, ncols):
                            ins = h.matmul(bank(4 + bq)[:, bo:bo + n], lhsT=onesb, rhs=G[:, 30 + c0:30 + c0 + n], start=(c == 0), stop=(c == 15))
                            ins = h.matmul(bank(6 + bq)[:, bo:bo + n], lhsT=onesb, rhs=Tb(sqt)[:, c0:c0 + n], start=(c == 0), stop=(c == 15))
                        return ins
                    if pend_stats is not None:
                        s.op("pe", pend_stats[0], reads=pend_stats[1], writes=bk(4, 5, 6, 7))
                    pend_stats = (f2, ["aux.%d" % c, "onesb"] + Tk(sqt))
                s.op("pe", pend_stats[0], reads=pend_stats[1], writes=bk(4, 5, 6, 7))
                s.op("dve", lambda h: h.tensor_scalar_mul(out=Tf(0)[:, 0:ncols], in0=pairv(2, ncols), scalar1=1.0 / E), reads=[], writes=bk(4, 5) + Tk(0))
                s.op("dve", lambda h: h.tensor_tensor(out=Tf(2)[:, 0:ncols], in0=Tf(0)[:, 0:ncols], in1=Tf(0)[:, 0:ncols], op=ALU.mult), reads=Tk(0), writes=Tk(2))
                s.op("dve", lambda h: h.scalar_tensor_tensor(out=Tf(1)[:, 0:ncols], in0=pairv(3, ncols), scalar=1.0 / E, in1=Tf(2)[:, 0:ncols],
                                                              op0=ALU.mult, op1=ALU.subtract), reads=Tk(2), writes=bk(6, 7) + Tk(1))
                s.op("dve", lambda h: h.tensor_scalar_max(out=Tf(1)[:, 0:ncols], in0=Tf(1)[:, 0:ncols], scalar1=0.0), reads=Tk(1), writes=Tk(1))
                s.op("act", lambda h: h.activation(out=Tf(1)[:, 0:ncols], in_=Tf(1)[:, 0:ncols], func=AF.Sqrt, bias=EPS, scale=1.0), reads=Tk(1), writes=Tk(1))
                s.op("dve", lambda h: h.reciprocal(out=Tf(1)[:, 0:ncols], in_=Tf(1)[:, 0:ncols]), reads=Tk(1), writes=Tk(1))
                pend_m = None
                pair_rr["i"] = 0
                for grp in range(8):
                    sz = zsl[grp]
                    for ci in range(2):
                        c = grp * 2 + ci
                        pz = next_pair()
                        inproj(sz[0], sz[1], ci, pz, ncols)
                        G = AUX[:, c, :]
                        tq = 2 + (c % 2)
                        s.op("act", lambda h, pz=pz, c=c: h.activation(out=ysT[:, c, 0:ncols], in_=pairv(pz, ncols), func=AF.Silu),
                             reads=[], writes=bk(2 * pz, 2 * pz + 1) + ["ysT.%d" % c])
                        s.op("dve", lambda h, G=G, tq=tq: h.tensor_tensor(out=Tf(tq)[:, 0:ncols], in0=G[:, 30:30 + ncols], in1=Tf(0)[:, 0:ncols], op=ALU.subtract),
                             reads=["aux.%d" % c] + Tk(0), writes=Tk(tq))
                        s.op("dve", lambda h, tq=tq: h.tensor_tensor(out=Tf(tq)[:, 0:ncols], in0=Tf(tq)[:, 0:ncols], in1=Tf(1)[:, 0:ncols], op=ALU.mult),
                             reads=Tk(1, tq), writes=Tk(tq))
                        s.op("act", lambda h, c=c, G=G, tq=tq: h.activation(out=G[:, 30:30 + ncols], in_=Tf(tq)[:, 0:ncols], func=AF.Silu, scale=prmT[:, c, 4:5], bias=prmT[:, c, 5:6]),
                             reads=Tk(tq) + ["prmT"], writes=["aux.%d" % c])
                        if pend_m is not None:
                            pend_m()
                        pend_m = (lambda c=c, G=G: s.op("dve", lambda h: h.tensor_tensor(out=ysT[:, c, 0:ncols], in0=ysT[:, c, 0:ncols], in1=G[:, 30:30 + ncols], op=ALU.mult),
                                                          reads=["aux.%d" % c, "ysT.%d" % c], writes=["ysT.%d" % c]))
                pend_m()

            else:
                Wd = 15 + ncols
                pf = Pref([(lambda g=g: (fetch_in(3, g * 256), fetch_in(3, E + g * 256))) for g in range(8)])
                wp_box = {}
                for grp in range(8):
                    sp_, sz = pf.get(grp)
                    if grp == 4:
                        wp_box["wp"] = [fetch_pool(g) for g in range(4)]
                        hoist_wo()
                    for ci in range(2):
                        c = grp * 2 + ci
                        g = c // 4
                        win = 2 << g
                        pp, pz = next_pair(), next_pair()
                        inproj(sp_[0], sp_[1], ci, pp, ncols)
                        inproj(sz[0], sz[1], ci, pz, ncols)
                        s.op("act", lambda h, pp=pp: h.activation(out=Tf(0)[:, 15:15 + ncols], in_=pairv(pp, ncols), func=AF.Copy),
                             reads=[], writes=bk(2 * pp, 2 * pp + 1) + Tk(0))
                        s.op("act", lambda h, pz=pz, c=c: h.activation(out=ysT[:, c, 0:ncols], in_=pairv(pz, ncols), func=AF.Silu),
                             reads=[], writes=bk(2 * pz, 2 * pz + 1) + ["ysT.%d" % c])
                        fix_hist(Tf(0), 15, (32, 47), "T.0", c)
                        s.op("dve", lambda h, c=c: h.tensor_copy(out=hist[:, c, 32:47], in_=Tf(0)[:, 15 + ep - 15:15 + ep]), reads=Tk(0), writes=["hist"])
                        if sd["sample"]:
                            s.op("dve", lambda h, c=c: h.tensor_copy(out=outS[:, c, 32:47], in_=Tf(0)[:, 15 + 689:15 + 704]), reads=Tk(0), writes=["outS"])
                        s.op("dve", lambda h: h.tensor_tensor(out=Tf(1)[:, 1:Wd], in0=Tf(0)[:, 1:Wd], in1=Tf(0)[:, 0:Wd - 1], op=ALU.add), reads=Tk(0), writes=Tk(1))
                        fin = 1
                        if win >= 4:
                            s.op("dve", lambda h: h.tensor_tensor(out=Tf(2)[:, 3:Wd], in0=Tf(1)[:, 3:Wd], in1=Tf(1)[:, 1:Wd - 2], op=ALU.add), reads=Tk(1), writes=Tk(2))
                            fin = 2
                        if win >= 8:
                            s.op("dve", lambda h: h.tensor_tensor(out=Tf(1)[:, 7:Wd], in0=Tf(2)[:, 7:Wd], in1=Tf(2)[:, 3:Wd - 4], op=ALU.add), reads=Tk(2), writes=Tk(1))
                            fin = 1
                        if win >= 16:
                            s.op("dve", lambda h: h.tensor_tensor(out=Tf(2)[:, 15:Wd], in0=Tf(1)[:, 15:Wd], in1=Tf(1)[:, 7:Wd - 8], op=ALU.add), reads=Tk(1), writes=Tk(2))
                            fin = 2
                        s.op("dve", lambda h, fin=fin, win=win, c=c: h.scalar_tensor_tensor(out=AUX[:, c, 0:ncols], in0=Tf(fin)[:, 15:Wd], scalar=1.0 / win,
                                                                                            in1=Tf(0)[:, 15:Wd], op0=ALU.mult, op1=ALU.subtract),
                             reads=Tk(0, fin), writes=["aux.%d" % c])
                        if sti == 0:
                            s.op("dve", lambda h, fin=fin, g=g: h.tensor_tensor(out=Tf(3)[:, 0:16], in0=Tf(fin)[:, 15 + 128:15 + 144], in1=invc[:, g * 16:(g + 1) * 16], op=ALU.mult),
                                 reads=Tk(fin) + ["invc"], writes=Tk(3))
                            s.op("dve", lambda h, c=c: h.tensor_tensor(out=AUX[:, c, 128:144], in0=Tf(3)[:, 0:16], in1=Tf(0)[:, 15 + 128:15 + 144], op=ALU.subtract),
                                 reads=Tk(0, 3) + ["aux.%d" % c], writes=["aux.%d" % c])
                wp = wp_box["wp"]
                for fc in range(16):
                    g, fi = fc // 4, fc % 4
                    p = next_pair()
                    nts = ntiles(p, ncols)

                    def f(h, g=g, fi=fi, nts=nts):
                        ins = None
                        for cc in range(4):
                            for (c0, n, b, bo) in nts:
                                ins = h.matmul(bank(b)[:, bo:bo + n], lhsT=wp[g][1][:, cc, fi * 128:(fi + 1) * 128], rhs=AUX[:, 4 * g + cc, c0:c0 + n],
                                               start=(cc == 0), stop=(cc == 3))
                        return ins
                    s.op("pe", f, reads=["rs.%d" % wp[g][0]] + ["aux.%d" % (4 * g + cc) for cc in range(4)], writes=bk(2 * p, 2 * p + 1))
                    s.op("dve", lambda h, p=p, fc=fc: h.scalar_tensor_tensor(out=ysT[:, fc, 0:ncols], in0=pairv(p, ncols), scalar=prmT[:, fc, 6:7],
                                                                              in1=ysT[:, fc, 0:ncols], op0=ALU.mult, op1=ALU.mult),
                         reads=["prmT", "ysT.%d" % fc], writes=bk(2 * p, 2 * p + 1) + ["ysT.%d" % fc])

            hoist_wo()
            wo = wo_box["wo"]

            pendq = []
            for bi, bl in enumerate(blocks):
                n, xt, sq, c0 = bl["n"], bl["xt"], bl["sq"], bl["col0"]
                p = next_pair()
                gsrc = gateS if sq == 1 else gP
                gkey = "gateS" if sq == 1 else "gateP%d" % (l % 2)

                def f(h, n=n, c0=c0, p=p):
                    ins = None
                    for ec in range(16):
                        for hf in range(2):
                            ins = h.matmul(bank(2 * p + hf)[0:n, :], lhsT=ysT[:, ec, c0:c0 + n], rhs=wo[ec // 2][1][:, ec % 2, hf * 512:(hf + 1) * 512],
                                           start=(ec == 0), stop=(ec == 15))
                    return ins
                s.op("pe", f, reads=["ysT.%d" % i for i in range(16)] + ["rs.%d" % i for i, _ in wo], writes=bk(2 * p, 2 * p + 1))
                is_tail = (bi == len(blocks) - 1)
                if len(pendq) >= 2 and not is_tail:
                    pb = pendq.pop(0)
                    h_post(pb[0], pb[1], l + 1)
                for hf in range(2):
                    tmp = Tf(hf)[0:n, 0:512]
                    s.op("dve", lambda h, hf=hf, p=p, n=n, tmp=tmp, gsrc=gsrc: h.tensor_tensor(out=tmp, in0=bank(2 * p + hf)[0:n, :], in1=gsrc[0:n, hf * 512:(hf + 1) * 512], op=ALU.mult),
                         reads=[gkey], writes=bk(2 * p + hf) + Tk(hf))
                    s.op("dve", lambda h, hf=hf, xt=xt, tmp=tmp: h.tensor_tensor(out=xt[:, hf * 512:(hf + 1) * 512], in0=xt[:, hf * 512:(hf + 1) * 512], in1=tmp, op=ALU.add),
                         reads=Tk(hf) + [bl["key"]], writes=[bl["key"]])
                if not last:
                    h_pre(bi, bl, l + 1)
                    pendq.append((bi, bl))
                elif bl["gb"] is None or bl["gb"] >= 1:
                    so = 32 + (bi % 4) * 4
                    s.op("dve", lambda h, so=so, n=n: h.memset(stat[0:n, so:so + 1], 0.0), reads=[], writes=["statf%d" % (bi % 4)])
                    s.op("act", lambda h, xt=xt, so=so, n=n: h.activation(out=junkv[0:n, :], in_=xt, func=AF.Square, scale=1.0 / 32, accum_out=stat[0:n, so:so + 1]),
                         reads=[bl["key"]], writes=["junk", "statf%d" % (bi % 4)])
                    s.op("act", lambda h, so=so, n=n: h.activation(out=stat[0:n, so + 1:so + 2], in_=stat[0:n, so:so + 1], func=AF.Sqrt, bias=EPS, scale=1.0),
                         reads=["statf%d" % (bi % 4)], writes=["statf%d" % (bi % 4)])
                    s.op("dve", lambda h, so=so, n=n: h.reciprocal(out=stat[0:n, so + 2:so + 3], in_=stat[0:n, so + 1:so + 2]), reads=["statf%d" % (bi % 4)], writes=["statf%d" % (bi % 4)])
                    yq = bi % 2
                    yt = AUX_raw[0:n, 1024 + yq * 1024:2048 + yq * 1024]
                    yk = ["aux.g%d" % yq]
                    s.op("dve", lambda h, xt=xt, so=so, n=n, yt=yt: h.scalar_tensor_tensor(out=yt, in0=xt, scalar=stat[0:n, so + 2:so + 3], in1=gf_bc[0:n, :],
                                                                                            op0=ALU.mult, op1=ALU.mult),
                         reads=[bl["key"], "statf%d" % (bi % 4), "gf_bc"], writes=yk)
                    if bl["gb"] is None:
                        s.dma("sp", "o_y%d" % yq, ysm, yt, reads=yk, writes=["O.ysm"])
                    else:
                        gb = bl["gb"] - 1
                        s.dma("sp", "o_y%d" % yq, yp[gb * 128:(gb + 1) * 128, :], yt, reads=yk, writes=["O.yp%d" % gb])
            for pb in pendq:
                h_post(pb[0], pb[1], l + 1)

        for l in range(4):
            run_layer(l)
            if sti == 2:
                emit_state_outputs(l)

    for sti, sd in enumerate(ST_DEFS):
        run_st(sti, sd)

    okeys = ["O.ysm", "O.vrows", "O.cbp", "O.cbs", "O.ccp", "O.ccs", "O.pdp", "O.pds"] + ["O.yp%d" % i for i in range(16)]
    s.wait_all("sp", okeys)
    with nc.Block() as block:
        s.emit_all(block)
    return nc


_WKEYS = ["w_ada", "b_ada", "g_norm", "w_in_a", "w_in_b", "w_in_c", "w_in_d", "w_out_a", "w_out_b", "w_out_c", "w_out_d",
          "w_conv_c", "w_s_a", "b_s_a", "w_pool_d"]


def kernel(**inp):
    f = lambda k: np.ascontiguousarray(np.asarray(inp[k], dtype=np.float32))
    xpr, xsa = f("x_prompt"), f("x_sample")
    shared = {k: f(k) for k in _WKEYS}
    shared["prm9"] = np.ascontiguousarray(np.concatenate(
        [f("w_conv_b"), f("b_conv_c")[None], f("ln_c_g")[None], f("ln_c_b")[None], f("scale_pool_d")[None], f("ln_a_g")[None], f("ln_a_b")[None]], 0))
    shared["ln_a_gb"] = np.ascontiguousarray(np.stack([f("ln_a_g"), f("ln_a_b")], 0))
    shared["g_final"] = f("g_final")[None]
    cpr, csa = f("c_prompt"), f("c_sample")
    sb, sc, sd_ = f("state_conv_b"), f("state_conv_c"), f("state_pool_d")
    wins = (2, 4, 8, 16)
    in_maps = []
    for i in range(8):
        b, half = i // 2, i % 2
        halo = xpr[b, 1920:2048] if half == 1 else np.zeros((128, D), np.float32)
        m = dict(shared)
        m["xp"] = np.ascontiguousarray(np.concatenate([halo, xpr[b, half * 2048:(half + 1) * 2048]], 0))
        m["xsm"] = xsa[i]
        m["st_b"], m["st_c"], m["st_d"] = sb[i], sc[i], sd_[i]
        m["cvec"] = np.ascontiguousarray(np.stack([cpr[b], csa[i]], 0))
        m["maskc"] = np.full((128, 1), float(half), np.float32)
        tab = np.zeros((4, 16), np.float32)
        for g, w in enumerate(wins):
            for j in range(16):
                tab[g, j] = 1.0 / (min(w, j + 1) if half == 0 else w)
        m["invc"] = np.ascontiguousarray(np.broadcast_to(tab.reshape(1, 64), (128, 64)))
        in_maps.append(m)
    nc = build_program()
    res = run_bass_kernel_spmd(nc, in_maps, core_ids=list(range(8)))
    R = res.results
    y_prompt = np.zeros((4, 4096, D), np.float32)
    for i in range(8):
        y_prompt[i // 2, (i % 2) * 2048:(i % 2 + 1) * 2048] = R[i]["yp"]
    stk = lambda k, ids: np.stack([np.asarray(R[i][k], np.float32) for i in ids], 0)
    allc = list(range(8))
    odd = [1, 3, 5, 7]
    return (y_prompt, stk("ysm", allc), stk("vrows", allc), stk("cb_p", odd), stk("cb_s", allc),
            stk("cc_p", odd), stk("cc_s", allc), stk("pd_p", odd), stk("pd_s", allc))
```

```python
import numpy as np
import concourse.bass as bass
import concourse.mybir as mybir
from concourse.bass_utils import run_bass_kernel_spmd

F32 = mybir.dt.float32
BF16 = mybir.dt.bfloat16
ALU = mybir.AluOpType
AF = mybir.ActivationFunctionType

ENGS = ("pe", "act", "dve", "pool", "sp")
D = 1024
E = 2048
EPS = 1e-6


class Sched:
    def __init__(self, nc):
        self.nc = nc
        self.streams = {e: [] for e in ENGS}
        self.cnt = {}
        self.seen = {e: {} for e in ENGS}
        self.res = {}
        self.sem = {}
        for e in ENGS:
            self.sem[e] = nc.alloc_semaphore("p_" + e)
            self.cnt[e] = 0

    def _deps(self, eng, reads, writes):
        waits = {}

        def add(sv):
            if sv is None:
                return
            sname, v = sv
            if v > waits.get(sname, 0):
                waits[sname] = v

        for k in reads:
            r = self.res.get(k)
            if r is not None:
                add(r[0])
        for k in writes:
            r = self.res.get(k)
            if r is not None:
                add(r[0])
                for sname, v in r[1].items():
                    add((sname, v))
        out = []
        for sname, v in waits.items():
            if self.seen[eng].get(sname, 0) >= v:
                continue
            self.seen[eng][sname] = v
            out.append((sname, v))
        return out

    def _mark(self, src, val, reads, writes):
        for k in writes:
            self.res[k] = [(src, val), {}]
        for k in reads:
            r = self.res.get(k)
            if r is None:
                r = self.res[k] = [None, {}]
            r[1][src] = val

    def op(self, eng, fn, reads=(), writes=()):
        ro = getattr(self, "ring_out", None)
        if ro:
            for k in reads:
                ro.discard(k)
        waits = self._deps(eng, reads, writes)
        if eng == "pe":
            waits = [(a, v) for (a, v) in waits if a != "pe"]
        self.cnt[eng] += 1
        val = self.cnt[eng]
        sems = self.sem
        wl = [(sems[a], v) for a, v in waits]
        mysem = sems[eng]

        def emit(h):
            for sm, v in wl:
                h.wait_ge(sm, v)
            ins = fn(h)
            ins.then_inc(mysem, 1)

        self.streams[eng].append(emit)
        self._mark(eng, val, reads, writes)

    def dma(self, q, slot, out, in_, reads=(), writes=(), **kw):
        if slot not in self.sem:
            self.sem[slot] = self.nc.alloc_semaphore("d_" + slot)
            self.cnt[slot] = 0
        waits = self._deps(q, reads, writes)
        self.cnt[slot] += 16
        val = self.cnt[slot]
        sems = self.sem
        wl = [(sems[a], v) for a, v in waits]
        dsm = sems[slot]

        def emit(h):
            for sm, v in wl:
                h.wait_ge(sm, v)
            h.dma_start(out=out, in_=in_, **kw).then_inc(dsm, 16)

        self.streams[q].append(emit)
        self._mark(slot, val, reads, writes)

    def wait_all(self, eng, keys):
        waits = self._deps(eng, keys, keys)
        sems = self.sem
        wl = [(sems[a], v) for a, v in waits]

        def emit(h):
            for sm, v in wl:
                h.wait_ge(sm, v)

        self.streams[eng].append(emit)

    def emit_all(self, block):
        st = self.streams

        @block.tensor
        def _(h):
            for f in st["pe"]:
                f(h)

        @block.scalar
        def _(h):
            for f in st["act"]:
                f(h)

        @block.vector
        def _(h):
            for f in st["dve"]:
                f(h)

        @block.gpsimd
        def _(h):
            for f in st["pool"]:
                f(h)

        @block.sync
        def _(h):
            for f in st["sp"]:
                f(h)


def build_program():
    nc = bass.Bass("TRN2", target_bir_lowering=False)

    def din(name, shape):
        return nc.dram_tensor(name, list(shape), F32, kind="ExternalInput").ap()

    def dout(name, shape):
        return nc.dram_tensor(name, list(shape), F32, kind="ExternalOutput").ap()

    xp = din("xp", [2176, D])
    xsm = din("xsm", [32, D])
    st_b = din("st_b", [2, E])
    st_c = din("st_c", [30, E])
    st_d = din("st_d", [15, E])
    cvec = din("cvec", [2, D])
    maskc_d = din("maskc", [128, 1])
    invc_d = din("invc", [128, 64])
    w_ada = din("w_ada", [4, D, 3 * D])
    b_ada = din("b_ada", [4, 3 * D])
    g_norm = din("g_norm", [4, D])
    w_in = [din("w_in_a", [D, 3 * E]), din("w_in_b", [D, 4 * E]), din("w_in_c", [D, 3 * E]), din("w_in_d", [D, 2 * E])]
    w_out = [din("w_out_a", [E, D]), din("w_out_b", [E, D]), din("w_out_c", [E, D]), din("w_out_d", [E, D])]
    prm9 = din("prm9", [9, E])
    w_conv_c = din("w_conv_c", [31, E])
    w_s_a = din("w_s_a", [8, 128, 128])
    b_s_a = din("b_s_a", [8, 128])
    w_pool = din("w_pool_d", [4, 512, 512])
    g_final = din("g_final", [1, D])
    ln_a_gb = din("ln_a_gb", [2, E])

    yp = dout("yp", [2048, D])
    ysm = dout("ysm", [32, D])
    vrows = dout("vrows", [32, E])
    o_cb_p = dout("cb_p", [2, E]); o_cb_s = dout("cb_s", [2, E])
    o_cc_p = dout("cc_p", [30, E]); o_cc_s = dout("cc_s", [30, E])
    o_pd_p = dout("pd_p", [15, E]); o_pd_s = dout("pd_s", [15, E])

    s = Sched(nc)
    TOTAL = 53000
    big = nc.alloc_sbuf_tensor("big", [128, TOTAL], F32)
    off = [0]

    def carve(nw):
        v = big[:, off[0]:off[0] + nw]
        off[0] += nw
        assert off[0] <= TOTAL, off[0]
        return v

    def v3(v, a):
        return v.rearrange("p (a b) -> p a b", a=a)

    X = v3(carve(6 * 1024), 6)
    XS = X[:, 5, :]
    hT = v3(carve(3072).bitcast(BF16), 8)
    ysT_raw = carve(6144)
    ysT = v3(ysT_raw.bitcast(BF16), 16)
    AUX_raw = carve(6400)
    AUX = v3(AUX_raw.bitcast(BF16), 16)
    RING = carve(16 * 1024)
    Traw = [carve(800) for _ in range(4)]
    gateP = [carve(1024), carve(1024)]
    gateS = carve(1024)
    dscr = [carve(128), carve(128)]
    scTb = v3(carve(8).bitcast(BF16), 8)
    gf_bc = carve(1024)
    biasT = v3(carve(2048), 16)
    biasT_s = v3(carve(512), 16)
    WsT = v3(carve(512).bitcast(BF16), 8)
    WsT_s = v3(carve(128).bitcast(BF16), 8)
    identb = carve(64).bitcast(BF16)
    identf = carve(128)
    onesf = carve(128)
    onesb = carve(64).bitcast(BF16)
    prmT = v3(carve(16 * 9), 16)
    wccT = v3(carve(16 * 32), 16)
    stT = v3(carve(16 * 48), 16)
    hist = v3(carve(16 * 48), 16)
    outS = v3(carve(16 * 48), 16)
    gnT = v3(carve(32), 8)
    badaT = v3(carve(96), 24)
    cT = v3(carve(16), 8)
    scT = v3(carve(16), 8)
    modT = carve(4 * 48)
    gsT = carve(4 * 16)
    stat = carve(64)
    maskc = carve(1)
    invc = carve(64)

    def modv(l):
        return v3(modT[:, l * 48:(l + 1) * 48], 24)

    def gsv(l):
        return v3(gsT[:, l * 16:(l + 1) * 16], 8)

    PS = nc.alloc_psum_tensor("ps", [128, 4096], F32)

    def bank(b):
        return PS[:, b * 512:(b + 1) * 512]

    def bk(*bs):
        return ["bk.%d" % b for b in bs]

    def Tf(i, n=800):
        return Traw[i][:, 0:n]

    def Tb(i, n=1600):
        return Traw[i].bitcast(BF16)[:, 0:n]

    def Tk(*idx):
        return ["T.%d" % i for i in idx]

    Tcat = big[:, Traw[0].offset - big.offset:Traw[0].offset - big.offset + 3200] if False else None

    t0_off = 6 * 1024 + 1024 + 3072 + 6144 + 6400 + 16 * 1024
    stage = big[:, t0_off:t0_off + 3200]
    Tpair01 = big[:, t0_off:t0_off + 1600]

    s.op("pool", lambda h: h.memset(identf, 0.0), writes=["identf"])
    s.op("pool", lambda h: h.affine_select(out=identf, in_=identf, pattern=[[-1, 128]], compare_op=ALU.not_equal,
                                           fill=1.0, base=0, channel_multiplier=1), reads=["identf"], writes=["identf"])
    s.op("dve", lambda h: h.tensor_copy(out=identb, in_=identf), reads=["identf"], writes=["identb"])
    s.op("dve", lambda h: h.memset(onesf, 1.0), writes=["onesf"])
    s.op("dve", lambda h: h.memset(onesb, 1.0), writes=["onesb"])
    s.op("dve", lambda h: h.memset(stat[:, 0:63], 0.0), writes=["stat_init"])
    s.op("dve", lambda h: h.memset(stat[:, 63:64], -0.5), writes=["neghalf"])
    s.op("dve", lambda h: h.memset(hist.rearrange("p a b -> p (a b)"), 0.0), writes=["hist"])
    s.dma("sp", "su_m", maskc, maskc_d, writes=["maskc"])
    s.dma("sp", "su_i", invc, invc_d, writes=["invc"])
    s.dma("sp", "su_g", gf_bc, g_final.partition_broadcast(128), writes=["gf_bc"])

    def load_rows_T(src, R, C, dst3, dkey, rp=32):
        nch = C // 128
        s.dma("sp", "su_st", stage[0:R, 0:C], src, writes=Tk(0, 1, 2, 3))
        pst = v3(bank(0)[:, 0:nch * rp], nch)

        def f(h):
            ins = None
            for c in range(nch):
                ins = h.transpose(out=pst[:, c, 0:R], in_=stage[0:R, c * 128:(c + 1) * 128], identity=identf[0:R, 0:R])
            return ins
        s.op("pe", f, reads=Tk(0, 1, 2, 3) + ["identf"], writes=bk(0))
        s.op("dve", lambda h: h.tensor_copy(out=dst3[:, :, 0:R], in_=pst[:, :, 0:R]), reads=[], writes=bk(0) + [dkey])

    load_rows_T(cvec, 2, D, cT, "cT")
    load_rows_T(g_norm, 4, D, gnT, "gnT")
    load_rows_T(b_ada, 4, 3 * D, badaT, "badaT", rp=4)
    load_rows_T(prm9, 9, E, prmT, "prmT")
    load_rows_T(w_conv_c, 31, E, wccT, "wccT")
    load_rows_T(st_b, 2, E, stT[:, :, 0:2], "stT")
    load_rows_T(st_c, 30, E, stT[:, :, 2:32], "stT")
    load_rows_T(st_d, 15, E, stT[:, :, 32:47], "stT")
    s.op("act", lambda h: h.activation(out=scT.rearrange("p a b -> p (a b)"), in_=cT.rearrange("p a b -> p (a b)"), func=AF.Silu),
         reads=["cT"], writes=["scT"])

    wsI = v3(stage[:, 0:1024], 8)
    s.dma("sp", "su_st", wsI, w_s_a.rearrange("g i j -> i g j"), writes=Tk(0, 1, 2))
    ps_w = v3(PS[:, 0:1024], 8)

    def f_ws_s(h):
        ins = None
        for g in range(8):
            ins = h.transpose(out=ps_w[0:32, g, 0:32], in_=wsI[0:32, g, 0:32], identity=identf[0:32, 0:32])
        return ins
    s.op("pe", f_ws_s, reads=Tk(0, 1, 2) + ["identf"], writes=bk(0, 1))
    s.op("dve", lambda h: h.tensor_copy(out=WsT_s[0:32, :, :], in_=ps_w[0:32, :, 0:32]), reads=[], writes=bk(0, 1) + ["WsT_s"])
    s.op("dve", lambda h: h.memset(wsI[0:64, :, 64:128], 0.0), reads=[], writes=Tk(0, 1, 2))

    def f_ws(h):
        ins = None
        for g in range(8):
            ins = h.transpose(out=ps_w[:, g, :], in_=wsI[:, g, :], identity=identf)
        return ins
    s.op("pe", f_ws, reads=Tk(0, 1, 2) + ["identf"], writes=bk(0, 1))
    s.op("dve", lambda h: h.tensor_copy(out=WsT, in_=ps_w), reads=[], writes=bk(0, 1) + ["WsT"])
    def f_rs(h):
        ins = None
        for hf in range(2):
            ins = h.matmul(bank(2 + hf), lhsT=onesb, rhs=WsT[:, hf * 4:(hf + 1) * 4, :].rearrange("p a b -> p (a b)"), start=True, stop=True)
        return ins
    s.op("pe", f_rs, reads=["WsT", "onesb"], writes=bk(2, 3))
    rs_bc = v3(PS[:, 1024:2048], 8)

    def f_rs_s(h):
        return h.matmul(bank(4)[:, 0:256], lhsT=onesb[0:32, :], rhs=WsT_s[0:32, :, :].rearrange("p a b -> p (a b)"), start=True, stop=True)
    s.op("pe", f_rs_s, reads=["WsT_s", "onesb"], writes=bk(4))
    rs_bc_s = v3(bank(4)[:, 0:256], 8)
    bs_bc = v3(stage[:, 1024:2048], 8)
    s.dma("sp", "su_bs", bs_bc.rearrange("p a b -> p (a b)"), b_s_a.rearrange("g i -> (g i)").partition_broadcast(128), writes=Tk(1, 2))
    for cc in range(16):
        g = cc // 2
        s.op("dve", lambda h, cc=cc, g=g: h.scalar_tensor_tensor(out=biasT[:, cc, :], in0=rs_bc[:, g, :], scalar=prmT[:, cc, 8:9],
                                                                  in1=bs_bc[:, g, :], op0=ALU.mult, op1=ALU.add),
             reads=["prmT"] + Tk(1, 2), writes=bk(2, 3) + ["biasT"])
        s.op("dve", lambda h, cc=cc, g=g: h.scalar_tensor_tensor(out=biasT_s[:, cc, :], in0=rs_bc_s[:, g, :], scalar=prmT[:, cc, 8:9],
                                                                  in1=bs_bc[:, g, 0:32], op0=ALU.mult, op1=ALU.add),
             reads=["prmT"] + Tk(1, 2), writes=bk(4) + ["biasT_s"])

    s.op("dve", lambda h: h.tensor_copy(out=scTb.rearrange("p a b -> p (a b)"), in_=scT.rearrange("p a b -> p (a b)")), reads=["scT"], writes=["scTb"])
    ring_state = {"next": 0}

    ring_out = set()
    s.ring_out = ring_out

    def ring_slot():
        i = ring_state["next"]
        ring_state["next"] = (i + 1) % 16
        assert ("rs.%d" % i) not in ring_out, "ring slot %d re-allocated before its consumer was recorded" % i
        ring_out.add("rs.%d" % i)
        return i

    def slot_bf(i):
        return RING[:, i * 1024:(i + 1) * 1024].bitcast(BF16)

    def fetch_in(l, col0):
        i = ring_slot()
        v = v3(slot_bf(i), 8)
        s.dma("pool", "rs%d" % i, v, w_in[l].rearrange("(kc p) n -> p kc n", p=128)[:, :, col0:col0 + 256], writes=["rs.%d" % i])
        return i, v

    def fetch_out(l, e2):
        i = ring_slot()
        v = v3(slot_bf(i), 2)
        s.dma("pool", "rs%d" % i, v, w_out[l][e2 * 256:(e2 + 1) * 256, :].rearrange("(a p) n -> p a n", p=128), writes=["rs.%d" % i])
        return i, v

    def ada_fetch(l, part):
        sl = []
        for j in range(4):
            cb = part * 4 + j
            i = ring_slot()
            v = v3(slot_bf(i), 8)
            s.dma("pool", "rs%d" % i, v, w_ada[l].rearrange("(kc p) n -> p kc n", p=128)[:, :, cb * 256:(cb + 1) * 256], writes=["rs.%d" % i])
            sl.append((i, v))
        return sl

    def ada_compute(l, part, sl):
        psA = bank(7)

        def f(h):
            ins = None
            for j in range(4):
                for ec in range(2):
                    for kc in range(8):
                        col = (j * 2 + ec) * 2
                        ins = h.matmul(psA[:, col:col + 2], lhsT=sl[j][1][:, kc, ec * 128:(ec + 1) * 128], rhs=scTb[:, kc, :],
                                       start=(kc == 0), stop=(kc == 7))
            return ins
        s.op("pe", f, reads=["rs.%d" % i for i, _ in sl] + ["scTb"], writes=bk(7))
        psA3 = v3(psA[:, 0:16], 8)
        mv = modv(l)
        for sq in range(2):
            s.op("dve", lambda h, sq=sq: h.tensor_tensor(out=mv[:, 8 * part:8 * part + 8, sq], in0=psA3[:, :, sq], in1=badaT[:, 8 * part:8 * part + 8, l], op=ALU.add),
                 reads=["badaT"], writes=bk(7) + ["modT"])
            if part == 1:
                s.op("dve", lambda h, sq=sq: h.scalar_tensor_tensor(out=gsv(l)[:, :, sq], in0=mv[:, 8:16, sq], scalar=1.0,
                                                                     in1=gnT[:, :, l], op0=ALU.add, op1=ALU.mult),
                     reads=["modT", "gnT"], writes=["gsT"])

    def gate_rows(l, sq, dst, dkey):
        mv = modv(l)
        p = next_pair()

        def f(h):
            ins = None
            for kc in range(8):
                col = mv[:, 16 + kc, sq:sq + 1]
                lb = bass.AP(col.tensor, col.offset, [list(col.ap[0]), [0, 128]])
                ins = h.matmul(PS[:, p * 1024 + kc * 128:p * 1024 + (kc + 1) * 128], lhsT=lb, rhs=identf, start=True, stop=True)
            return ins
        s.op("pe", f, reads=["modT", "identf"], writes=bk(2 * p, 2 * p + 1))
        s.op("act", lambda h: h.activation(out=dst, in_=PS[:, p * 1024:(p + 1) * 1024], func=AF.Copy), reads=[], writes=bk(2 * p, 2 * p + 1) + [dkey])

    class Pref:
        def __init__(self, thunks, depth=1):
            self.th, self.res, self.n, self.depth = thunks, {}, 0, depth

        def get(self, g):
            while self.n < len(self.th) and self.n <= g + self.depth:
                self.res[self.n] = self.th[self.n]()
                self.n += 1
            return self.res.pop(g)

    def fetch_pool(g):
        i = ring_slot()
        v = v3(slot_bf(i), 4)
        s.dma("pool", "rs%d" % i, v, w_pool[g].rearrange("(a p) n -> p a n", p=128), writes=["rs.%d" % i])
        return i, v

    for part in range(2):
        ada_compute(0, part, ada_fetch(0, part))

    ST_DEFS = [
        dict(b0=0, nb=6, ncols=768, ep=768, sample=False),
        dict(b0=6, nb=6, ncols=768, ep=768, sample=False),
        dict(b0=12, nb=5, ncols=704, ep=640, sample=True),
    ]
    pair_rr = {"i": 0}

    def next_pair():
        p = pair_rr["i"]
        pair_rr["i"] = (p + 1) % 4
        return p

    def pairv(p, n):
        return PS[:, p * 1024:p * 1024 + n]

    cs_box = {"cs": 0}

    def ntiles(p, ncols):
        cs = cs_box["cs"]
        return [(cs, 512 - cs, 2 * p, cs), (512, ncols - 512, 2 * p + 1, 0)]

    def inproj(slot_i, wv, ci, p, ncols):
        nts = ntiles(p, ncols)

        def f(h):
            ins = None
            for kc in range(8):
                for (c0, n, b, bo) in nts:
                    ins = h.matmul(bank(b)[:, bo:bo + n], lhsT=wv[:, kc, ci * 128:(ci + 1) * 128], rhs=hT[:, kc, c0:c0 + n],
                                   start=(kc == 0), stop=(kc == 7))
            return ins
        s.op("pe", f, reads=["rs.%d" % slot_i] + ["hT.%d" % i for i in range(6)] + ["hTb.%d" % i for i in range(6)], writes=bk(2 * p, 2 * p + 1))

    def store_rows_T(src3, R, dst, okey, skey):
        pso = PS[0:R, 0:2048]

        def f(h):
            ins = None
            for c in range(16):
                ins = h.transpose(out=pso[:, c * 128:(c + 1) * 128], in_=src3[:, c, :], identity=identf)
            return ins
        s.op("pe", f, reads=[skey, "identf"], writes=bk(0, 1, 2, 3))
        s.op("act", lambda h: h.activation(out=stage[0:R, 0:1024], in_=pso[:, 0:1024], func=AF.Copy), reads=[], writes=bk(0, 1) + Tk(0, 1, 2))
        s.op("act", lambda h: h.activation(out=stage[0:R, 1024:2048], in_=pso[:, 1024:2048], func=AF.Copy), reads=[], writes=bk(2, 3) + Tk(0, 1, 2))
        s.dma("sp", "o_st", dst, stage[0:R, 0:2048], reads=Tk(0, 1, 2), writes=[okey])


    def emit_state_outputs(l):
        if l == 1:
            store_rows_T(hist[:, :, 0:2], 2, o_cb_p, "O.cbp", "hist")
            store_rows_T(outS[:, :, 0:2], 2, o_cb_s, "O.cbs", "outS")
        elif l == 2:
            store_rows_T(hist[:, :, 2:32], 30, o_cc_p, "O.ccp", "hist")
            store_rows_T(outS[:, :, 2:32], 30, o_cc_s, "O.ccs", "outS")
        elif l == 3:
            store_rows_T(hist[:, :, 32:47], 15, o_pd_p, "O.pdp", "hist")
            store_rows_T(outS[:, :, 32:47], 15, o_pd_s, "O.pds", "outS")

    def run_st(sti, sd):
        ncols, ep, nb = sd["ncols"], sd["ep"], sd["nb"]
        blocks = []
        for j in range(nb):
            blocks.append(dict(xt=X[:, j, :], n=128, col0=j * 128, sq=0, key="x.%d" % j, gb=sd["b0"] + j))
        if sd["sample"]:
            blocks.append(dict(xt=XS[0:32, :], n=32, col0=672, sq=1, key="x.5", gb=None))
        for j in range(nb):
            gb = sd["b0"] + j
            s.dma("sp", "x%d" % j, X[:, j, :], xp[gb * 128:(gb + 1) * 128, :], writes=["x.%d" % j])
        if sd["sample"]:
            s.dma("sp", "x5", XS[0:32, :], xsm, writes=["x.5"])

        def fix_hist(buf, H, hsl, key, c, first_mask_ok=True):
            h0, h1 = hsl
            if sti == 0:
                s.op("dve", lambda h: h.memset(buf[:, 0:H], 0.0), writes=[key])
                s.op("dve", lambda h: h.tensor_scalar_mul(out=buf[:, H + 128 - H:H + 128], in0=buf[:, H + 128 - H:H + 128], scalar1=maskc[:, 0:1]),
                     reads=["maskc", key], writes=[key])
            else:
                s.op("dve", lambda h: h.tensor_copy(out=buf[:, 0:H], in_=hist[:, c, h0:h1]), reads=["hist"], writes=[key])
            if sd["sample"]:
                s.op("dve", lambda h: h.tensor_copy(out=buf[:, H + 672 - H:H + 672], in_=stT[:, c, h0:h1]), reads=["stT", key], writes=[key])

        junkv = AUX_raw[:, 0:512].bitcast(BF16)

        def h_pre(bi, bl, lt):
            n, xt = bl["n"], bl["xt"]
            so = (bi % 4) * 4
            xi = bi % 3
            xn = AUX_raw[:, 3072 + xi * 512:3584 + xi * 512].bitcast(BF16)[0:n, 0:1024]
            s.op("dve", lambda h: h.memset(stat[0:n, so:so + 1], 0.0), reads=[], writes=["stat%d" % (bi % 4)])
            s.op("act", lambda h: h.activation(out=junkv[0:n, :], in_=xt, func=AF.Square, scale=1.0 / 32, accum_out=stat[0:n, so:so + 1]),
                 reads=[bl["key"]], writes=["junk", "stat%d" % (bi % 4)])
            s.op("act", lambda h: h.activation(out=stat[0:n, so + 1:so + 2], in_=stat[0:n, so:so + 1], func=AF.Sqrt, bias=EPS, scale=1.0),
                 reads=["stat%d" % (bi % 4)], writes=["stat%d" % (bi % 4)])
            s.op("dve", lambda h: h.reciprocal(out=stat[0:n, so + 2:so + 3], in_=stat[0:n, so + 1:so + 2]), reads=["stat%d" % (bi % 4)], writes=["stat%d" % (bi % 4)])
            s.op("act", lambda h: h.activation(out=xn, in_=xt, func=AF.Copy, scale=stat[0:n, so + 2:so + 3]),
                 reads=[bl["key"], "stat%d" % (bi % 4)], writes=["xn%d" % xi])

        def h_post(bi, bl, lt):
            n, sq, c0 = bl["n"], bl["sq"], bl["col0"]
            xi = bi % 3
            xn = AUX_raw[:, 3072 + xi * 512:3584 + xi * 512].bitcast(BF16)[0:n, 0:1024]
            mvt, gst = modv(lt), gsv(lt)
            p = next_pair()
            ptA = v3(bank(2 * p).bitcast(BF16)[:, 0:512], 4)
            ptB = v3(bank(2 * p + 1).bitcast(BF16)[:, 0:512], 4)

            def f(h):
                ins = None
                for kc in range(8):
                    dstp = ptA[:, kc, 0:n] if kc < 4 else ptB[:, kc - 4, 0:n]
                    ins = h.transpose(out=dstp, in_=xn[:, kc * 128:(kc + 1) * 128], identity=identb[0:n, 0:n])
                return ins
            s.op("pe", f, reads=["xn%d" % xi, "identb"], writes=bk(2 * p, 2 * p + 1))
            for kc in range(4):
                s.op("dve", lambda h, kc=kc: h.tensor_scalar(
                    out=hT[:, kc, c0:c0 + n], in0=ptA[:, kc, 0:n], scalar1=gst[:, kc, sq:sq + 1], scalar2=mvt[:, kc, sq:sq + 1],
                    op0=ALU.mult, op1=ALU.add), reads=["gsT", "modT"], writes=bk(2 * p) + ["hT.%d" % bi])
            for kc in range(4, 8):
                s.op("act", lambda h, kc=kc: h.activation(
                    out=hT[:, kc, c0:c0 + n], in_=ptB[:, kc - 4, 0:n], func=AF.Identity, scale=gst[:, kc, sq:sq + 1], bias=mvt[:, kc, sq:sq + 1]),
                    reads=["gsT", "modT"], writes=bk(2 * p + 1) + ["hTb.%d" % bi])

        hT_keys = ["hT.%d" % i for i in range(len(blocks))]

        def run_layer(l):
            mv = modv(l)
            last = (l == 3)
            cs_box["cs"] = 80 if (sti == 0 and l >= 1) else 0
            gP = gateP[l % 2]
            if l == 0:
                if sti > 0:
                    gate_rows(0, 0, gP, "gateP0")
                for bi, bl in enumerate(blocks):
                    h_pre(bi, bl, 0)
                    if bi >= 2:
                        h_post(bi - 2, blocks[bi - 2], 0)
                for bi in range(max(0, len(blocks) - 2), len(blocks)):
                    h_post(bi, blocks[bi], 0)
            if sd["sample"]:
                gate_rows(l, 1, gateS, "gateS")

            ada_sl = {}
            wo_box = {}

            def hoist_wo():
                if "wo" not in wo_box:
                    wo_box["wo"] = [fetch_out(l, e2) for e2 in range(8)]

            def mid_hook(grp):
                if l >= 3:
                    return
                if sti == 0:
                    if 2 <= grp <= 4:
                        ada_compute(l + 1, grp - 2, ada_sl[grp - 2])
                    if 1 <= grp <= 3:
                        ada_sl[grp - 1] = ada_fetch(l + 1, grp - 1)
                if grp == 5:
                    gate_rows(l + 1, 0, gateP[(l + 1) % 2], "gateP%d" % ((l + 1) % 2))

            if l == 0:
                wv_slots = [fetch_in(0, E + q * 256) for q in range(8)]

                def v_front(bi, bl):
                    n, c0 = bl["n"], bl["col0"]
                    smp = bl["sq"] == 1
                    gv = AUX_raw[0:n, (bi % 2) * 2048:(bi % 2) * 2048 + 2048]
                    vn = AUX_raw[:, 4096:6144].bitcast(BF16)[0:n, (bi % 2) * 2048:(bi % 2) * 2048 + 2048]
                    gk = ["aux.g%d" % (bi % 2)]
                    vk = ["aux.v%d" % (bi % 2)]
                    sk = ["statv%d" % (bi % 2)]

                    for vh in range(2):
                        def f(h, vh=vh):
                            ins = None
                            for kc in range(8):
                                for q in range(4 * vh, 4 * vh + 4):
                                    ins = h.matmul(bank(q // 2)[0:n, (q % 2) * 256:(q % 2) * 256 + 256], lhsT=hT[:, kc, c0:c0 + n],
                                                   rhs=wv_slots[q][1][:, kc, :], start=(kc == 0 and q % 2 == 0), stop=(kc == 7),
                                                   skip_group_check=True)
                            return ins
                        s.op("pe", f, reads=["hT.%d" % bi, "hTb.%d" % bi] + ["rs.%d" % i for i, _ in wv_slots[4 * vh:4 * vh + 4]], writes=bk(2 * vh, 2 * vh + 1))
                    so = 16 + (bi % 2) * 8
                    s.op("act", lambda h: h.activation(out=stat[0:n, so:so + 3], in_=stat[0:n, so:so + 3], func=AF.Copy, scale=0.0), reads=sk + ["stat_init"], writes=sk)
                    for hf in range(2):
                        s.op("act", lambda h, hf=hf: h.activation(out=gv[:, hf * 1024:(hf + 1) * 1024], in_=PS[0:n, hf * 1024:(hf + 1) * 1024],
                                                                  func=AF.Gelu, accum_out=stat[0:n, so + hf:so + hf + 1]),
                             reads=[], writes=bk(2 * hf, 2 * hf + 1) + gk + sk + (["xn0", "xn1"] if bi % 2 == 1 else []))
                    junk = Tpair01.bitcast(BF16)[0:n, 0:2048]
                    s.op("act", lambda h: h.activation(out=junk, in_=gv, func=AF.Square, accum_out=stat[0:n, so + 2:so + 3]),
                         reads=gk, writes=Tk(0, 1) + sk)
                    return gv, vn, gk, vk, sk, so, n, smp

                def v_stats(bi, bl, ctx):
                    gv, vn, gk, vk, sk, so, n, smp = ctx
                    s.op("dve", lambda h: h.tensor_tensor(out=stat[0:n, so + 3:so + 4], in0=stat[0:n, so:so + 1], in1=stat[0:n, so + 1:so + 2], op=ALU.add),
                         reads=sk, writes=sk)
                    s.op("dve", lambda h: h.tensor_scalar_mul(out=stat[0:n, so + 3:so + 4], in0=stat[0:n, so + 3:so + 4], scalar1=1.0 / E),
                         reads=sk, writes=sk)
                    s.op("dve", lambda h: h.tensor_tensor(out=stat[0:n, so + 4:so + 5], in0=stat[0:n, so + 3:so + 4], in1=stat[0:n, so + 3:so + 4], op=ALU.mult),
                         reads=sk, writes=sk)
                    s.op("dve", lambda h: h.scalar_tensor_tensor(out=stat[0:n, so + 5:so + 6], in0=stat[0:n, so + 2:so + 3], scalar=1.0 / E,
                                                                 in1=stat[0:n, so + 4:so + 5], op0=ALU.mult, op1=ALU.subtract),
                         reads=sk, writes=sk)
                    s.op("dve", lambda h: h.tensor_scalar_add(out=stat[0:n, so + 5:so + 6], in0=stat[0:n, so + 5:so + 6], scalar1=EPS), reads=sk, writes=sk)
                    s.op("pool", lambda h: h.tensor_tensor(out=stat[0:n, so + 6:so + 7], in0=stat[0:n, so + 5:so + 6], in1=stat[0:n, 63:64], op=ALU.pow),
                         reads=sk + ["neghalf"], writes=sk)
                    s.op("dve", lambda h: h.scalar_tensor_tensor(out=stat[0:n, so + 7:so + 8], in0=stat[0:n, so + 3:so + 4], scalar=-1.0,
                                                                 in1=stat[0:n, so + 6:so + 7], op0=ALU.mult, op1=ALU.mult),
                         reads=sk, writes=sk)
                    s.op("act", lambda h: h.activation(out=vn, in_=gv, func=AF.Identity, scale=stat[0:n, so + 6:so + 7], bias=stat[0:n, so + 7:so + 8]),
                         reads=gk + sk, writes=vk + (["xn2"] if bi % 2 == 0 else []))
                    if smp:
                        def vrows_out():
                            gx = gk + ["xn0", "xn1"]
                            s.op("dve", lambda h: h.tensor_scalar(out=gv, in0=gv, scalar1=stat[0:n, so + 6:so + 7], scalar2=stat[0:n, so + 7:so + 8],
                                                                  op0=ALU.mult, op1=ALU.add), reads=gx + vk + sk, writes=gx)
                            s.dma("sp", "su_st", stage[0:32, 0:2048], ln_a_gb[0:1, :].rearrange("a n -> (a n)").partition_broadcast(32), writes=Tk(0, 1, 2))
                            s.op("dve", lambda h: h.tensor_tensor(out=gv, in0=gv, in1=stage[0:32, 0:2048], op=ALU.mult), reads=Tk(0, 1, 2) + gx, writes=gx)
                            s.dma("sp", "su_st", stage[0:32, 0:2048], ln_a_gb[1:2, :].rearrange("a n -> (a n)").partition_broadcast(32), writes=Tk(0, 1, 2))
                            s.op("dve", lambda h: h.tensor_tensor(out=gv, in0=gv, in1=stage[0:32, 0:2048], op=ALU.add), reads=Tk(0, 1, 2) + gx, writes=gx)
                            s.dma("sp", "o_vr", vrows, gv, reads=gx, writes=["O.vrows"])
                        deferred_v.append(vrows_out)

                def v_back(bi, bl):
                    n, c0 = bl["n"], bl["col0"]
                    smp = bl["sq"] == 1
                    vn = AUX_raw[:, 4096:6144].bitcast(BF16)[0:n, (bi % 2) * 2048:(bi % 2) * 2048 + 2048]
                    vk = ["aux.v%d" % (bi % 2)]
                    wst = WsT_s if smp else WsT
                    bt = biasT_s if smp else biasT
                    mps = PS[:, 2048:2048 + 16 * n].rearrange("p (a b) -> p a b", a=16)

                    def f(h):
                        ins = None
                        for cc in range(16):
                            ins = h.matmul(mps[:, cc, :], lhsT=vn[:, cc * 128:(cc + 1) * 128], rhs=wst[0:n, cc // 2, 0:n], start=True, stop=True)
                        return ins
                    s.op("pe", f, reads=vk + ["WsT", "WsT_s"], writes=bk(4, 5, 6, 7))

                    def evac(lo, hi):
                        for cc in range(lo, hi):
                            s.op("dve", lambda h, cc=cc: h.scalar_tensor_tensor(
                                out=ysT[:, cc, c0:c0 + n], in0=mps[:, cc, :], scalar=prmT[:, cc, 7:8], in1=bt[:, cc, 0:n], op0=ALU.mult, op1=ALU.add),
                                reads=["prmT", "biasT", "biasT_s"], writes=bk(4, 5, 6, 7) + ["ysT.%d" % cc])
                    evac(0, 8)
                    return lambda: evac(8, 16)

                prevb = None
                deferred_v = []
                for bi, bl in enumerate(blocks):
                    ctx = v_front(bi, bl)
                    tail = v_back(*prevb) if prevb is not None else None
                    v_stats(bi, bl, ctx)
                    if tail is not None:
                        tail()
                    prevb = (bi, bl)
                v_back(*prevb)()
                for fdef in deferred_v:
                    fdef()
                pf = Pref([(lambda g=g: fetch_in(0, (g % 8) * 256 + (0 if g < 8 else 2 * E))) for g in range(16)], depth=3)
                for it in range(16):
                    if it == 0 and sti == 0:
                        ada_compute(0, 2, ada_fetch(0, 2))
                        gate_rows(0, 0, gP, "gateP0")
                    if it % 2 == 0:
                        mid_hook(it // 2)
                    sw = pf.get(it)
                    if it == 8:
                        hoist_wo()
                    isz = it >= 8
                    for ci in range(2):
                        c = (it % 8) * 2 + ci
                        pq = next_pair()
                        inproj(sw[0], sw[1], ci, pq, ncols)
                        tt_ = (2 if isz else 0) + (c % 2)
                        s.op("act", lambda h, pq=pq, tt_=tt_, isz=isz: h.activation(out=Tb(tt_)[:, 0:ncols], in_=pairv(pq, ncols), func=(AF.Silu if isz else AF.Gelu)),
                             reads=[], writes=bk(2 * pq, 2 * pq + 1) + Tk(tt_))
                        s.op("dve", lambda h, tt_=tt_, c=c: h.tensor_tensor(out=ysT[:, c, 0:ncols], in0=ysT[:, c, 0:ncols], in1=Tb(tt_)[:, 0:ncols], op=ALU.mult),
                             reads=Tk(tt_) + ["ysT.%d" % c], writes=["ysT.%d" % c])

            elif l == 1:
                pf = Pref([(lambda g=g: [fetch_in(1, b * E + g * 256) for b in range(4)]) for g in range(8)])
                for grp in range(8):
                    mid_hook(grp)
                    sl = pf.get(grp)
                    if grp == 4:
                        hoist_wo()
                    for ci in range(2):
                        c = grp * 2 + ci
                        pc_, ph_, pz_, pb_ = next_pair(), next_pair(), next_pair(), next_pair()
                        inproj(sl[1][0], sl[1][1], ci, pc_, ncols)
                        inproj(sl[2][0], sl[2][1], ci, ph_, ncols)
                        inproj(sl[3][0], sl[3][1], ci, pz_, ncols)
                        inproj(sl[0][0], sl[0][1], ci, pb_, ncols)
                        Pb = Tf(1)
                        s.op("act", lambda h, p=pc_: h.activation(out=Tf(0)[:, 0:ncols], in_=pairv(p, ncols), func=AF.Copy),
                             reads=[], writes=bk(2 * pc_, 2 * pc_ + 1) + Tk(0))
                        s.op("dve", lambda h, p=ph_: h.tensor_tensor(out=Tf(1)[:, 2:2 + ncols], in0=Tf(0)[:, 0:ncols], in1=pairv(p, ncols), op=ALU.mult),
                             reads=Tk(0), writes=bk(2 * ph_, 2 * ph_ + 1) + Tk(1))
                        s.op("act", lambda h, p=pz_, c=c: h.activation(out=ysT[:, c, 0:ncols], in_=pairv(p, ncols), func=AF.Silu),
                             reads=[], writes=bk(2 * pz_, 2 * pz_ + 1) + ["ysT.%d" % c])
                        s.op("act", lambda h, p=pb_: h.activation(out=Tf(3)[:, 0:ncols], in_=pairv(p, ncols), func=AF.Copy),
                             reads=[], writes=bk(2 * pb_, 2 * pb_ + 1) + Tk(3))
                        fix_hist(Pb, 2, (0, 2), "T.1", c)
                        s.op("dve", lambda h, c=c: h.tensor_copy(out=hist[:, c, 0:2], in_=Tf(1)[:, 2 + ep - 2:2 + ep]), reads=Tk(1), writes=["hist"])
                        if sd["sample"]:
                            s.op("dve", lambda h, c=c: h.tensor_copy(out=outS[:, c, 0:2], in_=Tf(1)[:, 2 + 702:2 + 704]), reads=Tk(1), writes=["outS"])
                        s.op("dve", lambda h, c=c: h.tensor_scalar_mul(out=Tf(2)[:, 0:ncols], in0=Tf(1)[:, 2:2 + ncols], scalar1=prmT[:, c, 2:3]),
                             reads=Tk(1) + ["prmT"], writes=Tk(2))
                        s.op("dve", lambda h, c=c: h.scalar_tensor_tensor(out=Tf(2)[:, 0:ncols], in0=Tf(1)[:, 1:1 + ncols], scalar=prmT[:, c, 1:2],
                                                                           in1=Tf(2)[:, 0:ncols], op0=ALU.mult, op1=ALU.add), reads=Tk(1, 2) + ["prmT"], writes=Tk(2))
                        s.op("dve", lambda h, c=c: h.scalar_tensor_tensor(out=Tf(2)[:, 0:ncols], in0=Tf(1)[:, 0:ncols], scalar=prmT[:, c, 0:1],
                                                                           in1=Tf(2)[:, 0:ncols], op0=ALU.mult, op1=ALU.add), reads=Tk(1, 2) + ["prmT"], writes=Tk(2))
                        s.op("dve", lambda h: h.tensor_tensor(out=Tf(2)[:, 0:ncols], in0=Tf(2)[:, 0:ncols], in1=Tf(3)[:, 0:ncols], op=ALU.mult),
                             reads=Tk(2, 3), writes=Tk(2))
                        s.op("dve", lambda h, c=c: h.tensor_tensor(out=ysT[:, c, 0:ncols], in0=ysT[:, c, 0:ncols], in1=Tf(2)[:, 0:ncols], op=ALU.mult),
                             reads=Tk(2) + ["ysT.%d" % c], writes=["ysT.%d" % c])

            elif l == 2:
                NTD = 3
                NTP = 31 - NTD

                def diag_view(c):
                    par = c % 2
                    dwords = ysT_raw[:, par * 2304:par * 2304 + NTP * 64].bitcast(BF16)
                    return dwords.rearrange("p (a b) -> p a b", a=NTP), ["ysT.%d" % i for i in range(par * 6, par * 6 + 6)]

                def build_diag(c, eng="pool"):
                    dg, dkeys = diag_view(c)
                    id_bc = bass.AP(identb.tensor, identb.offset, [list(identb.ap[0]), [0, NTP], [1, 128]])
                    w_bc = wccT[:, c, NTD:31].unsqueeze(2).broadcast_to([128, NTP, 128])
                    s.op(eng, lambda h: h.tensor_tensor(out=dg, in0=id_bc, in1=w_bc, op=ALU.mult), reads=["identb", "wccT"], writes=dkeys)

                pf = Pref([(lambda g=g: (fetch_in(2, g * 256), fetch_in(2, E + g * 256))) for g in range(8)])
                for grp in range(8):
                    mid_hook(grp)
                    if grp >= 6:
                        build_diag(grp - 6, "dve")
                    sa, sg = pf.get(grp)
                    for ci in range(2):
                        c = grp * 2 + ci
                        pa, pg = next_pair(), next_pair()
                        inproj(sa[0], sa[1], ci, pa, ncols)
                        inproj(sg[0], sg[1], ci, pg, ncols)
                        tt = c % 2
                        G = AUX[:, c, :]
                        s.op("act", lambda h, pg=pg, tt=tt: h.activation(out=Tf(tt)[:, 0:ncols], in_=pairv(pg, ncols), func=AF.Sigmoid),
                             reads=[], writes=bk(2 * pg, 2 * pg + 1) + Tk(tt))
                        s.op("dve", lambda h, pa=pa, tt=tt, G=G: h.tensor_tensor(out=G[:, 30:30 + ncols], in0=Tf(tt)[:, 0:ncols], in1=pairv(pa, ncols), op=ALU.mult),
                             reads=Tk(tt), writes=bk(2 * pa, 2 * pa + 1) + ["aux.%d" % c])
                        fix_hist(G, 30, (2, 32), "aux.%d" % c, c)
                        s.op("dve", lambda h, pa=pa, tt=tt, c=c: h.tensor_tensor(out=hist[:, c, 2:32], in0=Tf(tt)[:, ep - 30:ep], in1=pairv(pa, ncols)[:, ep - 30:ep], op=ALU.mult),
                             reads=Tk(tt) + ["aux.%d" % c], writes=bk(2 * pa, 2 * pa + 1) + ["hist"])
                        if sd["sample"]:
                            s.op("dve", lambda h, pa=pa, tt=tt, c=c: h.tensor_tensor(out=outS[:, c, 2:32], in0=Tf(tt)[:, 674:704], in1=pairv(pa, ncols)[:, 674:704], op=ALU.mult),
                                 reads=Tk(tt), writes=bk(2 * pa, 2 * pa + 1) + ["outS"])
                zsl = [fetch_in(2, 2 * E + g * 256) for g in range(8)]
                hoist_wo()
                pend_stats = None
                for c in range(16):
                    par = c % 2
                    dg, dkeys = diag_view(c)
                    pc = c % 2
                    G = AUX[:, c, :]
                    nts = ntiles(pc, ncols)

                    acc = Tf(c % 2)
                    s.op("act", lambda h, G=G, acc=acc, c=c: h.activation(out=acc[:, 0:ncols], in_=G[:, 0:ncols], func=AF.Copy, scale=wccT[:, c, 0:1]),
                         reads=["aux.%d" % c, "wccT"], writes=Tk(c % 2))
                    for k in range(1, NTD):
                        s.op("dve", lambda h, G=G, acc=acc, c=c, k=k: h.scalar_tensor_tensor(out=acc[:, 0:ncols], in0=G[:, k:k + ncols], scalar=wccT[:, c, k:k + 1],
                                                                                        in1=acc[:, 0:ncols], op0=ALU.mult, op1=ALU.add),
                             reads=["aux.%d" % c, "wccT"] + Tk(c % 2), writes=Tk(c % 2))

                    def f(h, dg=dg, G=G, nts=nts):
                        ins = None
                        for (c0, n, b, bo) in nts:
                            for k in range(NTD, 31):
                                ins = h.matmul(bank(b)[:, bo:bo + n], lhsT=dg[:, k - NTD, :], rhs=G[:, c0 + k:c0 + k + n], start=(k == NTD), stop=(k == 30))
                        return ins
                    s.op("pe", f, reads=dkeys + ["aux.%d" % c], writes=bk(2 * pc, 2 * pc + 1))
                    if c + 2 < 16:
                        build_diag(c + 2)
                    s.op("dve", lambda h, pc=pc, G=G, c=c, acc=acc: h.scalar_tensor_tensor(out=G[:, 30:30 + ncols], in0=pairv(pc, ncols), scalar=prmT[:, c, 3:4],
                                                                                          in1=acc[:, 0:ncols], op0=ALU.add, op1=ALU.add),
                         reads=["prmT"] + Tk(c % 2), writes=bk(2 * pc, 2 * pc + 1) + ["aux.%d" % c])
                    sqt = 2 + (c % 2)
                    s.op("act", lambda h, G=G, sqt=sqt: h.activation(out=Tb(sqt)[:, 0:ncols], in_=G[:, 30:30 + ncols], func=AF.Square),
                         reads=["aux.%d" % c], writes=Tk(sqt))

                    def f2(h, G=G, sqt=sqt, c=c):
                        ins = None
                        for (c0, n, bq, bo) in ntiles(0, ncols):
                            ins = h.matmul(bank(4 + bq)[:, bo:bo + n], lhsT=onesb, rhs=G[:, 30 + c0:30 + c0 + n], start=(c == 0), stop=(c == 15))
                            ins = h.matmul(bank(6 + bq)[:, bo:bo + n], lhsT=onesb, rhs=Tb(sqt)[:, c0:c0 + n], start=(c == 0), stop=(c == 15))
                        return ins
                    if pend_stats is not None:
                        s.op("pe", pend_stats[0], reads=pend_stats[1], writes=bk(4, 5, 6, 7))
                    pend_stats = (f2, ["aux.%d" % c, "onesb"] + Tk(sqt))
                s.op("pe", pend_stats[0], reads=pend_stats[1], writes=bk(4, 5, 6, 7))
                s.op("dve", lambda h: h.tensor_scalar_mul(out=Tf(0)[:, 0:ncols], in0=pairv(2, ncols), scalar1=1.0 / E), reads=[], writes=bk(4, 5) + Tk(0))
                s.op("dve", lambda h: h.tensor_tensor(out=Tf(2)[:, 0:ncols], in0=Tf(0)[:, 0:ncols], in1=Tf(0)[:, 0:ncols], op=ALU.mult), reads=Tk(0), writes=Tk(2))
                s.op("dve", lambda h: h.scalar_tensor_tensor(out=Tf(1)[:, 0:ncols], in0=pairv(3, ncols), scalar=1.0 / E, in1=Tf(2)[:, 0:ncols],
                                                              op0=ALU.mult, op1=ALU.subtract), reads=Tk(2), writes=bk(6, 7) + Tk(1))
                s.op("dve", lambda h: h.tensor_scalar_max(out=Tf(1)[:, 0:ncols], in0=Tf(1)[:, 0:ncols], scalar1=0.0), reads=Tk(1), writes=Tk(1))
                s.op("act", lambda h: h.activation(out=Tf(1)[:, 0:ncols], in_=Tf(1)[:, 0:ncols], func=AF.Sqrt, bias=EPS, scale=1.0), reads=Tk(1), writes=Tk(1))
                s.op("dve", lambda h: h.reciprocal(out=Tf(1)[:, 0:ncols], in_=Tf(1)[:, 0:ncols]), reads=Tk(1), writes=Tk(1))
                pend_m = None
                pair_rr["i"] = 0
                for grp in range(8):
                    sz = zsl[grp]
                    for ci in range(2):
                        c = grp * 2 + ci
                        pz = next_pair()
                        inproj(sz[0], sz[1], ci, pz, ncols)
                        G = AUX[:, c, :]
                        tq = 2 + (c % 2)
                        s.op("act", lambda h, pz=pz, c=c: h.activation(out=ysT[:, c, 0:ncols], in_=pairv(pz, ncols), func=AF.Silu),
                             reads=[], writes=bk(2 * pz, 2 * pz + 1) + ["ysT.%d" % c])
                        s.op("dve", lambda h, G=G, tq=tq: h.tensor_tensor(out=Tf(tq)[:, 0:ncols], in0=G[:, 30:30 + ncols], in1=Tf(0)[:, 0:ncols], op=ALU.subtract),
                             reads=["aux.%d" % c] + Tk(0), writes=Tk(tq))
                        s.op("dve", lambda h, tq=tq: h.tensor_tensor(out=Tf(tq)[:, 0:ncols], in0=Tf(tq)[:, 0:ncols], in1=Tf(1)[:, 0:ncols], op=ALU.mult),
                             reads=Tk(1, tq), writes=Tk(tq))
                        s.op("act", lambda h, c=c, G=G, tq=tq: h.activation(out=G[:, 30:30 + ncols], in_=Tf(tq)[:, 0:ncols], func=AF.Silu, scale=prmT[:, c, 4:5], bias=prmT[:, c, 5:6]),
                             reads=Tk(tq) + ["prmT"], writes=["aux.%d" % c])
                        if pend_m is not None:
                            pend_m()
                        pend_m = (lambda c=c, G=G: s.op("dve", lambda h: h.tensor_tensor(out=ysT[:, c, 0:ncols], in0=ysT[:, c, 0:ncols], in1=G[:, 30:30 + ncols], op=ALU.mult),
                                                          reads=["aux.%d" % c, "ysT.%d" % c], writes=["ysT.%d" % c]))
                pend_m()

            else:
                Wd = 15 + ncols
                pf = Pref([(lambda g=g: (fetch_in(3, g * 256), fetch_in(3, E + g * 256))) for g in range(8)])
                wp_box = {}
                for grp in range(8):
                    sp_, sz = pf.get(grp)
                    if grp == 4:
                        wp_box["wp"] = [fetch_pool(g) for g in range(4)]
                        hoist_wo()
                    for ci in range(2):
                        c = grp * 2 + ci
                        g = c // 4
                        win = 2 << g
                        pp, pz = next_pair(), next_pair()
                        inproj(sp_[0], sp_[1], ci, pp, ncols)
                        inproj(sz[0], sz[1], ci, pz, ncols)
                        s.op("act", lambda h, pp=pp: h.activation(out=Tf(0)[:, 15:15 + ncols], in_=pairv(pp, ncols), func=AF.Copy),
                             reads=[], writes=bk(2 * pp, 2 * pp + 1) + Tk(0))
                        s.op("act", lambda h, pz=pz, c=c: h.activation(out=ysT[:, c, 0:ncols], in_=pairv(pz, ncols), func=AF.Silu),
                             reads=[], writes=bk(2 * pz, 2 * pz + 1) + ["ysT.%d" % c])
                        fix_hist(Tf(0), 15, (32, 47), "T.0", c)
                        s.op("dve", lambda h, c=c: h.tensor_copy(out=hist[:, c, 32:47], in_=Tf(0)[:, 15 + ep - 15:15 + ep]), reads=Tk(0), writes=["hist"])
                        if sd["sample"]:
                            s.op("dve", lambda h, c=c: h.tensor_copy(out=outS[:, c, 32:47], in_=Tf(0)[:, 15 + 689:15 + 704]), reads=Tk(0), writes=["outS"])
                        s.op("dve", lambda h: h.tensor_tensor(out=Tf(1)[:, 1:Wd], in0=Tf(0)[:, 1:Wd], in1=Tf(0)[:, 0:Wd - 1], op=ALU.add), reads=Tk(0), writes=Tk(1))
                        fin = 1
                        if win >= 4:
                            s.op("dve", lambda h: h.tensor_tensor(out=Tf(2)[:, 3:Wd], in0=Tf(1)[:, 3:Wd], in1=Tf(1)[:, 1:Wd - 2], op=ALU.add), reads=Tk(1), writes=Tk(2))
                            fin = 2
                        if win >= 8:
                            s.op("dve", lambda h: h.tensor_tensor(out=Tf(1)[:, 7:Wd], in0=Tf(2)[:, 7:Wd], in1=Tf(2)[:, 3:Wd - 4], op=ALU.add), reads=Tk(2), writes=Tk(1))
                            fin = 1
                        if win >= 16:
                            s.op("dve", lambda h: h.tensor_tensor(out=Tf(2)[:, 15:Wd], in0=Tf(1)[:, 15:Wd], in1=Tf(1)[:, 7:Wd - 8], op=ALU.add), reads=Tk(1), writes=Tk(2))
                            fin = 2
                        s.op("dve", lambda h, fin=fin, win=win, c=c: h.scalar_tensor_tensor(out=AUX[:, c, 0:ncols], in0=Tf(fin)[:, 15:Wd], scalar=1.0 / win,
                                                                                            in1=Tf(0)[:, 15:Wd], op0=ALU.mult, op1=ALU.subtract),
                             reads=Tk(0, fin), writes=["aux.%d" % c])
                        if sti == 0:
                            s.op("dve", lambda h, fin=fin, g=g: h.tensor_tensor(out=Tf(3)[:, 0:16], in0=Tf(fin)[:, 15 + 128:15 + 144], in1=invc[:, g * 16:(g + 1) * 16], op=ALU.mult),
                                 reads=Tk(fin) + ["invc"], writes=Tk(3))
                            s.op("dve", lambda h, c=c: h.tensor_tensor(out=AUX[:, c, 128:144], in0=Tf(3)[:, 0:16], in1=Tf(0)[:, 15 + 128:15 + 144], op=ALU.subtract),
                                 reads=Tk(0, 3) + ["aux.%d" % c], writes=["aux.%d" % c])
                wp = wp_box["wp"]
                for fc in range(16):
                    g, fi = fc // 4, fc % 4
                    p = next_pair()
                    nts = ntiles(p, ncols)

                    def f(h, g=g, fi=fi, nts=nts):
                        ins = None
                        for cc in range(4):
                            for (c0, n, b, bo) in nts:
                                ins = h.matmul(bank(b)[:, bo:bo + n], lhsT=wp[g][1][:, cc, fi * 128:(fi + 1) * 128], rhs=AUX[:, 4 * g + cc, c0:c0 + n],
                                               start=(cc == 0), stop=(cc == 3))
                        return ins
                    s.op("pe", f, reads=["rs.%d" % wp[g][0]] + ["aux.%d" % (4 * g + cc) for cc in range(4)], writes=bk(2 * p, 2 * p + 1))
                    s.op("dve", lambda h, p=p, fc=fc: h.scalar_tensor_tensor(out=ysT[:, fc, 0:ncols], in0=pairv(p, ncols), scalar=prmT[:, fc, 6:7],
                                                                              in1=ysT[:, fc, 0:ncols], op0=ALU.mult, op1=ALU.mult),
                         reads=["prmT", "ysT.%d" % fc], writes=bk(2 * p, 2 * p + 1) + ["ysT.%d" % fc])

            hoist_wo()
            wo = wo_box["wo"]

            pendq = []
            for bi, bl in enumerate(blocks):
                n, xt, sq, c0 = bl["n"], bl["xt"], bl["sq"], bl["col0"]
                p = next_pair()
                gsrc = gateS if sq == 1 else gP
                gkey = "gateS" if sq == 1 else "gateP%d" % (l % 2)

                def f(h, n=n, c0=c0, p=p):
                    ins = None
                    for ec in range(16):
                        for hf in range(2):
                            ins = h.matmul(bank(2 * p + hf)[0:n, :], lhsT=ysT[:, ec, c0:c0 + n], rhs=wo[ec // 2][1][:, ec % 2, hf * 512:(hf + 1) * 512],
                                           start=(ec == 0), stop=(ec == 15))
                    return ins
                s.op("pe", f, reads=["ysT.%d" % i for i in range(16)] + ["rs.%d" % i for i, _ in wo], writes=bk(2 * p, 2 * p + 1))
                is_tail = (bi == len(blocks) - 1)
                if len(pendq) >= 2 and not is_tail:
                    pb = pendq.pop(0)
                    h_post(pb[0], pb[1], l + 1)
                for hf in range(2):
                    tmp = Tf(hf)[0:n, 0:512]
                    s.op("dve", lambda h, hf=hf, p=p, n=n, tmp=tmp, gsrc=gsrc: h.tensor_tensor(out=tmp, in0=bank(2 * p + hf)[0:n, :], in1=gsrc[0:n, hf * 512:(hf + 1) * 512], op=ALU.mult),
                         reads=[gkey], writes=bk(2 * p + hf) + Tk(hf))
                    s.op("dve", lambda h, hf=hf, xt=xt, tmp=tmp: h.tensor_tensor(out=xt[:, hf * 512:(hf + 1) * 512], in0=xt[:, hf * 512:(hf + 1) * 512], in1=tmp, op=ALU.add),
                         reads=Tk(hf) + [bl["key"]], writes=[bl["key"]])
                if not last:
                    h_pre(bi, bl, l + 1)
                    pendq.append((bi, bl))
                elif bl["gb"] is None or bl["gb"] >= 1:
                    so = 32 + (bi % 4) * 4
                    s.op("dve", lambda h, so=so, n=n: h.memset(stat[0:n, so:so + 1], 0.0), reads=[], writes=["statf%d" % (bi % 4)])
                    s.op("act", lambda h, xt=xt, so=so, n=n: h.activation(out=junkv[0:n, :], in_=xt, func=AF.Square, scale=1.0 / 32, accum_out=stat[0:n, so:so + 1]),
                         reads=[bl["key"]], writes=["junk", "statf%d" % (bi % 4)])
                    s.op("act", lambda h, so=so, n=n: h.activation(out=stat[0:n, so + 1:so + 2], in_=stat[0:n, so:so + 1], func=AF.Sqrt, bias=EPS, scale=1.0),
                         reads=["statf%d" % (bi % 4)], writes=["statf%d" % (bi % 4)])
                    s.op("dve", lambda h, so=so, n=n: h.reciprocal(out=stat[0:n, so + 2:so + 3], in_=stat[0:n, so + 1:so + 2]), reads=["statf%d" % (bi % 4)], writes=["statf%d" % (bi % 4)])
                    yq = bi % 2
                    yt = AUX_raw[0:n, 1024 + yq * 1024:2048 + yq * 1024]
                    yk = ["aux.g%d" % yq]
                    s.op("dve", lambda h, xt=xt, so=so, n=n, yt=yt: h.scalar_tensor_tensor(out=yt, in0=xt, scalar=stat[0:n, so + 2:so + 3], in1=gf_bc[0:n, :],
                                                                                            op0=ALU.mult, op1=ALU.mult),
                         reads=[bl["key"], "statf%d" % (bi % 4), "gf_bc"], writes=yk)
                    if bl["gb"] is None:
                        s.dma("sp", "o_y%d" % yq, ysm, yt, reads=yk, writes=["O.ysm"])
                    else:
                        gb = bl["gb"] - 1
                        s.dma("sp", "o_y%d" % yq, yp[gb * 128:(gb + 1) * 128, :], yt, reads=yk, writes=["O.yp%d" % gb])
            for pb in pendq:
                h_post(pb[0], pb[1], l + 1)

        for l in range(4):
            run_layer(l)
            if sti == 2:
                emit_state_outputs(l)

    for sti, sd in enumerate(ST_DEFS):
        run_st(sti, sd)

    okeys = ["O.ysm", "O.vrows", "O.cbp", "O.cbs", "O.ccp", "O.ccs", "O.pdp", "O.pds"] + ["O.yp%d" % i for i in range(16)]
    s.wait_all("sp", okeys)
    with nc.Block() as block:
        s.emit_all(block)
    return nc


_WKEYS = ["w_ada", "b_ada", "g_norm", "w_in_a", "w_in_b", "w_in_c", "w_in_d", "w_out_a", "w_out_b", "w_out_c", "w_out_d",
          "w_conv_c", "w_s_a", "b_s_a", "w_pool_d"]


def kernel(**inp):
    f = lambda k: np.ascontiguousarray(np.asarray(inp[k], dtype=np.float32))
    xpr, xsa = f("x_prompt"), f("x_sample")
    shared = {k: f(k) for k in _WKEYS}
    shared["prm9"] = np.ascontiguousarray(np.concatenate(
        [f("w_conv_b"), f("b_conv_c")[None], f("ln_c_g")[None], f("ln_c_b")[None], f("scale_pool_d")[None], f("ln_a_g")[None], f("ln_a_b")[None]], 0))
    shared["ln_a_gb"] = np.ascontiguousarray(np.stack([f("ln_a_g"), f("ln_a_b")], 0))
    shared["g_final"] = f("g_final")[None]
    cpr, csa = f("c_prompt"), f("c_sample")
    sb, sc, sd_ = f("state_conv_b"), f("state_conv_c"), f("state_pool_d")
    wins = (2, 4, 8, 16)
    in_maps = []
    for i in range(8):
        b, half = i // 2, i % 2
        halo = xpr[b, 1920:2048] if half == 1 else np.zeros((128, D), np.float32)
        m = dict(shared)
        m["xp"] = np.ascontiguousarray(np.concatenate([halo, xpr[b, half * 2048:(half + 1) * 2048]], 0))
        m["xsm"] = xsa[i]
        m["st_b"], m["st_c"], m["st_d"] = sb[i], sc[i], sd_[i]
        m["cvec"] = np.ascontiguousarray(np.stack([cpr[b], csa[i]], 0))
        m["maskc"] = np.full((128, 1), float(half), np.float32)
        tab = np.zeros((4, 16), np.float32)
        for g, w in enumerate(wins):
            for j in range(16):
                tab[g, j] = 1.0 / (min(w, j + 1) if half == 0 else w)
        m["invc"] = np.ascontiguousarray(np.broadcast_to(tab.reshape(1, 64), (128, 64)))
        in_maps.append(m)
    nc = build_program()
    res = run_bass_kernel_spmd(nc, in_maps, core_ids=list(range(8)))
    R = res.results
    y_prompt = np.zeros((4, 4096, D), np.float32)
    for i in range(8):
        y_prompt[i // 2, (i % 2) * 2048:(i % 2 + 1) * 2048] = R[i]["yp"]
    stk = lambda k, ids: np.stack([np.asarray(R[i][k], np.float32) for i in ids], 0)
    allc = list(range(8))
    odd = [1, 3, 5, 7]
    return (y_prompt, stk("ysm", allc), stk("vrows", allc), stk("cb_p", odd), stk("cb_s", allc),
            stk("cc_p", odd), stk("cc_s", allc), stk("pd_p", odd), stk("pd_s", allc))
```

```python
import numpy as np
import concourse.bass as bass
import concourse.mybir as mybir
from concourse.bass_utils import run_bass_kernel_spmd

F32 = mybir.dt.float32
BF16 = mybir.dt.bfloat16
ALU = mybir.AluOpType
AF = mybir.ActivationFunctionType

ENGS = ("pe", "act", "dve", "pool", "sp")
D = 1024
E = 2048
EPS = 1e-6


class Sched:
    def __init__(self, nc):
        self.nc = nc
        self.streams = {e: [] for e in ENGS}
        self.cnt = {}
        self.seen = {e: {} for e in ENGS}
        self.res = {}
        self.sem = {}
        for e in ENGS:
            self.sem[e] = nc.alloc_semaphore("p_" + e)
            self.cnt[e] = 0

    def _deps(self, eng, reads, writes):
        waits = {}

        def add(sv):
            if sv is None:
                return
            sname, v = sv
            if v > waits.get(sname, 0):
                waits[sname] = v

        for k in reads:
            r = self.res.get(k)
            if r is not None:
                add(r[0])
        for k in writes:
            r = self.res.get(k)
            if r is not None:
                add(r[0])
                for sname, v in r[1].items():
                    add((sname, v))
        out = []
        for sname, v in waits.items():
            if self.seen[eng].get(sname, 0) >= v:
                continue
            self.seen[eng][sname] = v
            out.append((sname, v))
        return out

    def _mark(self, src, val, reads, writes):
        for k in writes:
            self.res[k] = [(src, val), {}]
        for k in reads:
            r = self.res.get(k)
            if r is None:
                r = self.res[k] = [None, {}]
            r[1][src] = val

    def op(self, eng, fn, reads=(), writes=()):
        ro = getattr(self, "ring_out", None)
        if ro:
            for k in reads:
                ro.discard(k)
        waits = self._deps(eng, reads, writes)
        if eng == "pe":
            waits = [(a, v) for (a, v) in waits if a != "pe"]
        self.cnt[eng] += 1
        val = self.cnt[eng]
        sems = self.sem
        wl = [(sems[a], v) for a, v in waits]
        mysem = sems[eng]

        def emit(h):
            for sm, v in wl:
                h.wait_ge(sm, v)
            ins = fn(h)
            ins.then_inc(mysem, 1)

        self.streams[eng].append(emit)
        self._mark(eng, val, reads, writes)

    def dma(self, q, slot, out, in_, reads=(), writes=(), **kw):
        if slot not in self.sem:
            self.sem[slot] = self.nc.alloc_semaphore("d_" + slot)
            self.cnt[slot] = 0
        waits = self._deps(q, reads, writes)
        self.cnt[slot] += 16
        val = self.cnt[slot]
        sems = self.sem
        wl = [(sems[a], v) for a, v in waits]
        dsm = sems[slot]

        def emit(h):
            for sm, v in wl:
                h.wait_ge(sm, v)
            h.dma_start(out=out, in_=in_, **kw).then_inc(dsm, 16)

        self.streams[q].append(emit)
        self._mark(slot, val, reads, writes)

    def wait_all(self, eng, keys):
        waits = self._deps(eng, keys, keys)
        sems = self.sem
        wl = [(sems[a], v) for a, v in waits]

        def emit(h):
            for sm, v in wl:
                h.wait_ge(sm, v)

        self.streams[eng].append(emit)

    def emit_all(self, block):
        st = self.streams

        @block.tensor
        def _(h):
            for f in st["pe"]:
                f(h)

        @block.scalar
        def _(h):
            for f in st["act"]:
                f(h)

        @block.vector
        def _(h):
            for f in st["dve"]:
                f(h)

        @block.gpsimd
        def _(h):
            for f in st["pool"]:
                f(h)

        @block.sync
        def _(h):
            for f in st["sp"]:
                f(h)


def build_program():
    nc = bass.Bass("TRN2", target_bir_lowering=False)

    def din(name, shape):
        return nc.dram_tensor(name, list(shape), F32, kind="ExternalInput").ap()

    def dout(name, shape):
        return nc.dram_tensor(name, list(shape), F32, kind="ExternalOutput").ap()

    xp = din("xp", [2176, D])
    xsm = din("xsm", [32, D])
    st_b = din("st_b", [2, E])
    st_c = din("st_c", [30, E])
    st_d = din("st_d", [15, E])
    cvec = din("cvec", [2, D])
    maskc_d = din("maskc", [128, 1])
    invc_d = din("invc", [128, 64])
    w_ada = din("w_ada", [4, D, 3 * D])
    b_ada = din("b_ada", [4, 3 * D])
    g_norm = din("g_norm", [4, D])
    w_in = [din("w_in_a", [D, 3 * E]), din("w_in_b", [D, 4 * E]), din("w_in_c", [D, 3 * E]), din("w_in_d", [D, 2 * E])]
    w_out = [din("w_out_a", [E, D]), din("w_out_b", [E, D]), din("w_out_c", [E, D]), din("w_out_d", [E, D])]
    prm9 = din("prm9", [9, E])
    w_conv_c = din("w_conv_c", [31, E])
    w_s_a = din("w_s_a", [8, 128, 128])
    b_s_a = din("b_s_a", [8, 128])
    w_pool = din("w_pool_d", [4, 512, 512])
    g_final = din("g_final", [1, D])
    ln_a_gb = din("ln_a_gb", [2, E])

    yp = dout("yp", [2048, D])
    ysm = dout("ysm", [32, D])
    vrows = dout("vrows", [32, E])
    o_cb_p = dout("cb_p", [2, E]); o_cb_s = dout("cb_s", [2, E])
    o_cc_p = dout("cc_p", [30, E]); o_cc_s = dout("cc_s", [30, E])
    o_pd_p = dout("pd_p", [15, E]); o_pd_s = dout("pd_s", [15, E])

    s = Sched(nc)
    TOTAL = 53000
    big = nc.alloc_sbuf_tensor("big", [128, TOTAL], F32)
    off = [0]

    def carve(nw):
        v = big[:, off[0]:off[0] + nw]
        off[0] += nw
        assert off[0] <= TOTAL, off[0]
        return v

    def v3(v, a):
        return v.rearrange("p (a b) -> p a b", a=a)

    X = v3(carve(6 * 1024), 6)
    XS = X[:, 5, :]
    hT = v3(carve(3072).bitcast(BF16), 8)
    ysT_raw = carve(6144)
    ysT = v3(ysT_raw.bitcast(BF16), 16)
    AUX_raw = carve(6400)
    AUX = v3(AUX_raw.bitcast(BF16), 16)
    RING = carve(16 * 1024)
    Traw = [carve(800) for _ in range(4)]
    gateP = [carve(1024), carve(1024)]
    gateS = carve(1024)
    dscr = [carve(128), carve(128)]
    scTb = v3(carve(8).bitcast(BF16), 8)
    gf_bc = carve(1024)
    biasT = v3(carve(2048), 16)
    biasT_s = v3(carve(512), 16)
    WsT = v3(carve(512).bitcast(BF16), 8)
    WsT_s = v3(carve(128).bitcast(BF16), 8)
    identb = carve(64).bitcast(BF16)
    identf = carve(128)
    onesf = carve(128)
    onesb = carve(64).bitcast(BF16)
    prmT = v3(carve(16 * 9), 16)
    wccT = v3(carve(16 * 32), 16)
    stT = v3(carve(16 * 48), 16)
    hist = v3(carve(16 * 48), 16)
    outS = v3(carve(16 * 48), 16)
    gnT = v3(carve(32), 8)
    badaT = v3(carve(96), 24)
    cT = v3(carve(16), 8)
    scT = v3(carve(16), 8)
    modT = carve(4 * 48)
    gsT = carve(4 * 16)
    stat = carve(64)
    maskc = carve(1)
    invc = carve(64)

    def modv(l):
        return v3(modT[:, l * 48:(l + 1) * 48], 24)

    def gsv(l):
        return v3(gsT[:, l * 16:(l + 1) * 16], 8)

    PS = nc.alloc_psum_tensor("ps", [128, 4096], F32)

    def bank(b):
        return PS[:, b * 512:(b + 1) * 512]

    def bk(*bs):
        return ["bk.%d" % b for b in bs]

    def Tf(i, n=800):
        return Traw[i][:, 0:n]

    def Tb(i, n=1600):
        return Traw[i].bitcast(BF16)[:, 0:n]

    def Tk(*idx):
        return ["T.%d" % i for i in idx]

    Tcat = big[:, Traw[0].offset - big.offset:Traw[0].offset - big.offset + 3200] if False else None

    t0_off = 6 * 1024 + 1024 + 3072 + 6144 + 6400 + 16 * 1024
    stage = big[:, t0_off:t0_off + 3200]
    Tpair01 = big[:, t0_off:t0_off + 1600]

    s.op("pool", lambda h: h.memset(identf, 0.0), writes=["identf"])
    s.op("pool", lambda h: h.affine_select(out=identf, in_=identf, pattern=[[-1, 128]], compare_op=ALU.not_equal,
                                           fill=1.0, base=0, channel_multiplier=1), reads=["identf"], writes=["identf"])
    s.op("dve", lambda h: h.tensor_copy(out=identb, in_=identf), reads=["identf"], writes=["identb"])
    s.op("dve", lambda h: h.memset(onesf, 1.0), writes=["onesf"])
    s.op("dve", lambda h: h.memset(onesb, 1.0), writes=["onesb"])
    s.op("dve", lambda h: h.memset(stat[:, 0:63], 0.0), writes=["stat_init"])
    s.op("dve", lambda h: h.memset(stat[:, 63:64], -0.5), writes=["neghalf"])
    s.op("dve", lambda h: h.memset(hist.rearrange("p a b -> p (a b)"), 0.0), writes=["hist"])
    s.dma("sp", "su_m", maskc, maskc_d, writes=["maskc"])
    s.dma("sp", "su_i", invc, invc_d, writes=["invc"])
    s.dma("sp", "su_g", gf_bc, g_final.partition_broadcast(128), writes=["gf_bc"])

    def load_rows_T(src, R, C, dst3, dkey, rp=32):
        nch = C // 128
        s.dma("sp", "su_st", stage[0:R, 0:C], src, writes=Tk(0, 1, 2, 3))
        pst = v3(bank(0)[:, 0:nch * rp], nch)

        def f(h):
            ins = None
            for c in range(nch):
                ins = h.transpose(out=pst[:, c, 0:R], in_=stage[0:R, c * 128:(c + 1) * 128], identity=identf[0:R, 0:R])
            return ins
        s.op("pe", f, reads=Tk(0, 1, 2, 3) + ["identf"], writes=bk(0))
        s.op("dve", lambda h: h.tensor_copy(out=dst3[:, :, 0:R], in_=pst[:, :, 0:R]), reads=[], writes=bk(0) + [dkey])

    load_rows_T(cvec, 2, D, cT, "cT")
    load_rows_T(g_norm, 4, D, gnT, "gnT")
    load_rows_T(b_ada, 4, 3 * D, badaT, "badaT", rp=4)
    load_rows_T(prm9, 9, E, prmT, "prmT")
    load_rows_T(w_conv_c, 31, E, wccT, "wccT")
    load_rows_T(st_b, 2, E, stT[:, :, 0:2], "stT")
    load_rows_T(st_c, 30, E, stT[:, :, 2:32], "stT")
    load_rows_T(st_d, 15, E, stT[:, :, 32:47], "stT")
    s.op("act", lambda h: h.activation(out=scT.rearrange("p a b -> p (a b)"), in_=cT.rearrange("p a b -> p (a b)"), func=AF.Silu),
         reads=["cT"], writes=["scT"])

    wsI = v3(stage[:, 0:1024], 8)
    s.dma("sp", "su_st", wsI, w_s_a.rearrange("g i j -> i g j"), writes=Tk(0, 1, 2))
    ps_w = v3(PS[:, 0:1024], 8)

    def f_ws_s(h):
        ins = None
        for g in range(8):
            ins = h.transpose(out=ps_w[0:32, g, 0:32], in_=wsI[0:32, g, 0:32], identity=identf[0:32, 0:32])
        return ins
    s.op("pe", f_ws_s, reads=Tk(0, 1, 2) + ["identf"], writes=bk(0, 1))
    s.op("dve", lambda h: h.tensor_copy(out=WsT_s[0:32, :, :], in_=ps_w[0:32, :, 0:32]), reads=[], writes=bk(0, 1) + ["WsT_s"])
    s.op("dve", lambda h: h.memset(wsI[0:64, :, 64:128], 0.0), reads=[], writes=Tk(0, 1, 2))

    def f_ws(h):
        ins = None
        for g in range(8):
            ins = h.transpose(out=ps_w[:, g, :], in_=wsI[:, g, :], identity=identf)
        return ins
    s.op("pe", f_ws, reads=Tk(0, 1, 2) + ["identf"], writes=bk(0, 1))
    s.op("dve", lambda h: h.tensor_copy(out=WsT, in_=ps_w), reads=[], writes=bk(0, 1) + ["WsT"])
    def f_rs(h):
        ins = None
        for hf in range(2):
            ins = h.matmul(bank(2 + hf), lhsT=onesb, rhs=WsT[:, hf * 4:(hf + 1) * 4, :].rearrange("p a b -> p (a b)"), start=True, stop=True)
        return ins
    s.op("pe", f_rs, reads=["WsT", "onesb"], writes=bk(2, 3))
    rs_bc = v3(PS[:, 1024:2048], 8)

    def f_rs_s(h):
        return h.matmul(bank(4)[:, 0:256], lhsT=onesb[0:32, :], rhs=WsT_s[0:32, :, :].rearrange("p a b -> p (a b)"), start=True, stop=True)
    s.op("pe", f_rs_s, reads=["WsT_s", "onesb"], writes=bk(4))
    rs_bc_s = v3(bank(4)[:, 0:256], 8)
    bs_bc = v3(stage[:, 1024:2048], 8)
    s.dma("sp", "su_bs", bs_bc.rearrange("p a b -> p (a b)"), b_s_a.rearrange("g i -> (g i)").partition_broadcast(128), writes=Tk(1, 2))
    for cc in range(16):
        g = cc // 2
        s.op("dve", lambda h, cc=cc, g=g: h.scalar_tensor_tensor(out=biasT[:, cc, :], in0=rs_bc[:, g, :], scalar=prmT[:, cc, 8:9],
                                                                  in1=bs_bc[:, g, :], op0=ALU.mult, op1=ALU.add),
             reads=["prmT"] + Tk(1, 2), writes=bk(2, 3) + ["biasT"])
        s.op("dve", lambda h, cc=cc, g=g: h.scalar_tensor_tensor(out=biasT_s[:, cc, :], in0=rs_bc_s[:, g, :], scalar=prmT[:, cc, 8:9],
                                                                  in1=bs_bc[:, g, 0:32], op0=ALU.mult, op1=ALU.add),
             reads=["prmT"] + Tk(1, 2), writes=bk(4) + ["biasT_s"])

    s.op("dve", lambda h: h.tensor_copy(out=scTb.rearrange("p a b -> p (a b)"), in_=scT.rearrange("p a b -> p (a b)")), reads=["scT"], writes=["scTb"])
    ring_state = {"next": 0}

    ring_out = set()
    s.ring_out = ring_out

    def ring_slot():
        i = ring_state["next"]
        ring_state["next"] = (i + 1) % 16
        assert ("rs.%d" % i) not in ring_out, "ring slot %d re-allocated before its consumer was recorded" % i
        ring_out.add("rs.%d" % i)
        return i

    def slot_bf(i):
        return RING[:, i * 1024:(i + 1) * 1024].bitcast(BF16)

    def fetch_in(l, col0):
        i = ring_slot()
        v = v3(slot_bf(i), 8)
        s.dma("pool", "rs%d" % i, v, w_in[l].rearrange("(kc p) n -> p kc n", p=128)[:, :, col0:col0 + 256], writes=["rs.%d" % i])
        return i, v

    def fetch_out(l, e2):
        i = ring_slot()
        v = v3(slot_bf(i), 2)
        s.dma("pool", "rs%d" % i, v, w_out[l][e2 * 256:(e2 + 1) * 256, :].rearrange("(a p) n -> p a n", p=128), writes=["rs.%d" % i])
        return i, v

    def ada_fetch(l, part):
        sl = []
        for j in range(4):
            cb = part * 4 + j
            i = ring_slot()
            v = v3(slot_bf(i), 8)
            s.dma("pool", "rs%d" % i, v, w_ada[l].rearrange("(kc p) n -> p kc n", p=128)[:, :, cb * 256:(cb + 1) * 256], writes=["rs.%d" % i])
            sl.append((i, v))
        return sl

    def ada_compute(l, part, sl):
        psA = bank(7)

        def f(h):
            ins = None
            for j in range(4):
                for ec in range(2):
                    for kc in range(8):
                        col = (j * 2 + ec) * 2
                        ins = h.matmul(psA[:, col:col + 2], lhsT=sl[j][1][:, kc, ec * 128:(ec + 1) * 128], rhs=scTb[:, kc, :],
                                       start=(kc == 0), stop=(kc == 7))
            return ins
        s.op("pe", f, reads=["rs.%d" % i for i, _ in sl] + ["scTb"], writes=bk(7))
        psA3 = v3(psA[:, 0:16], 8)
        mv = modv(l)
        for sq in range(2):
            s.op("dve", lambda h, sq=sq: h.tensor_tensor(out=mv[:, 8 * part:8 * part + 8, sq], in0=psA3[:, :, sq], in1=badaT[:, 8 * part:8 * part + 8, l], op=ALU.add),
                 reads=["badaT"], writes=bk(7) + ["modT"])
            if part == 1:
                s.op("dve", lambda h, sq=sq: h.scalar_tensor_tensor(out=gsv(l)[:, :, sq], in0=mv[:, 8:16, sq], scalar=1.0,
                                                                     in1=gnT[:, :, l], op0=ALU.add, op1=ALU.mult),
                     reads=["modT", "gnT"], writes=["gsT"])

    def gate_rows(l, sq, dst, dkey):
        mv = modv(l)
        p = next_pair()

        def f(h):
            ins = None
            for kc in range(8):
                col = mv[:, 16 + kc, sq:sq + 1]
                lb = bass.AP(col.tensor, col.offset, [list(col.ap[0]), [0, 128]])
                ins = h.matmul(PS[:, p * 1024 + kc * 128:p * 1024 + (kc + 1) * 128], lhsT=lb, rhs=identf, start=True, stop=True)
            return ins
        s.op("pe", f, reads=["modT", "identf"], writes=bk(2 * p, 2 * p + 1))
        s.op("act", lambda h: h.activation(out=dst, in_=PS[:, p * 1024:(p + 1) * 1024], func=AF.Copy), reads=[], writes=bk(2 * p, 2 * p + 1) + [dkey])

    class Pref:
        def __init__(self, thunks, depth=1):
            self.th, self.res, self.n, self.depth = thunks, {}, 0, depth

        def get(self, g):
            while self.n < len(self.th) and self.n <= g + self.depth:
                self.res[self.n] = self.th[self.n]()
                self.n += 1
            return self.res.pop(g)

    def fetch_pool(g):
        i = ring_slot()
        v = v3(slot_bf(i), 4)
        s.dma("pool", "rs%d" % i, v, w_pool[g].rearrange("(a p) n -> p a n", p=128), writes=["rs.%d" % i])
        return i, v

    for part in range(2):
        ada_compute(0, part, ada_fetch(0, part))

    ST_DEFS = [
        dict(b0=0, nb=6, ncols=768, ep=768, sample=False),
        dict(b0=6, nb=6, ncols=768, ep=768, sample=False),
        dict(b0=12, nb=5, ncols=704, ep=640, sample=True),
    ]
    pair_rr = {"i": 0}

    def next_pair():
        p = pair_rr["i"]
        pair_rr["i"] = (p + 1) % 4
        return p

    def pairv(p, n):
        return PS[:, p * 1024:p * 1024 + n]

    cs_box = {"cs": 0}

    def ntiles(p, ncols):
        cs = cs_box["cs"]
        return [(cs, 512 - cs, 2 * p, cs), (512, ncols - 512, 2 * p + 1, 0)]

    def inproj(slot_i, wv, ci, p, ncols):
        nts = ntiles(p, ncols)

        def f(h):
            ins = None
            for kc in range(8):
                for (c0, n, b, bo) in nts:
                    ins = h.matmul(bank(b)[:, bo:bo + n], lhsT=wv[:, kc, ci * 128:(ci + 1) * 128], rhs=hT[:, kc, c0:c0 + n],
                                   start=(kc == 0), stop=(kc == 7))
            return ins
        s.op("pe", f, reads=["rs.%d" % slot_i] + ["hT.%d" % i for i in range(6)] + ["hTb.%d" % i for i in range(6)], writes=bk(2 * p, 2 * p + 1))

    def store_rows_T(src3, R, dst, okey, skey):
        pso = PS[0:R, 0:2048]

        def f(h):
            ins = None
            for c in range(16):
                ins = h.transpose(out=pso[:, c * 128:(c + 1) * 128], in_=src3[:, c, :], identity=identf)
            return ins
        s.op("pe", f, reads=[skey, "identf"], writes=bk(0, 1, 2, 3))
        s.op("act", lambda h: h.activation(out=stage[0:R, 0:1024], in_=pso[:, 0:1024], func=AF.Copy), reads=[], writes=bk(0, 1) + Tk(0, 1, 2))
        s.op("act", lambda h: h.activation(out=stage[0:R, 1024:2048], in_=pso[:, 1024:2048], func=AF.Copy), reads=[], writes=bk(2, 3) + Tk(0, 1, 2))
        s.dma("sp", "o_st", dst, stage[0:R, 0:2048], reads=Tk(0, 1, 2), writes=[okey])


    def emit_state_outputs(l):
        if l == 1:
            store_rows_T(hist[:, :, 0:2], 2, o_cb_p, "O.cbp", "hist")
            store_rows_T(outS[:, :, 0:2], 2, o_cb_s, "O.cbs", "outS")
        elif l == 2:
            store_rows_T(hist[:, :, 2:32], 30, o_cc_p, "O.ccp", "hist")
            store_rows_T(outS[:, :, 2:32], 30, o_cc_s, "O.ccs", "outS")
        elif l == 3:
            store_rows_T(hist[:, :, 32:47], 15, o_pd_p, "O.pdp", "hist")
            store_rows_T(outS[:, :, 32:47], 15, o_pd_s, "O.pds", "outS")

    def run_st(sti, sd):
        ncols, ep, nb = sd["ncols"], sd["ep"], sd["nb"]
        blocks = []
        for j in range(nb):
            blocks.append(dict(xt=X[:, j, :], n=128, col0=j * 128, sq=0, key="x.%d" % j, gb=sd["b0"] + j))
        if sd["sample"]:
            blocks.append(dict(xt=XS[0:32, :], n=32, col0=672, sq=1, key="x.5", gb=None))
        for j in range(nb):
            gb = sd["b0"] + j
            s.dma("sp", "x%d" % j, X[:, j, :], xp[gb * 128:(gb + 1) * 128, :], writes=["x.%d" % j])
        if sd["sample"]:
            s.dma("sp", "x5", XS[0:32, :], xsm, writes=["x.5"])

        def fix_hist(buf, H, hsl, key, c, first_mask_ok=True):
            h0, h1 = hsl
            if sti == 0:
                s.op("dve", lambda h: h.memset(buf[:, 0:H], 0.0), writes=[key])
                s.op("dve", lambda h: h.tensor_scalar_mul(out=buf[:, H + 128 - H:H + 128], in0=buf[:, H + 128 - H:H + 128], scalar1=maskc[:, 0:1]),
                     reads=["maskc", key], writes=[key])
            else:
                s.op("dve", lambda h: h.tensor_copy(out=buf[:, 0:H], in_=hist[:, c, h0:h1]), reads=["hist"], writes=[key])
            if sd["sample"]:
                s.op("dve", lambda h: h.tensor_copy(out=buf[:, H + 672 - H:H + 672], in_=stT[:, c, h0:h1]), reads=["stT", key], writes=[key])

        junkv = AUX_raw[:, 0:512].bitcast(BF16)

        def h_pre(bi, bl, lt):
            n, xt = bl["n"], bl["xt"]
            so = (bi % 4) * 4
            xi = bi % 3
            xn = AUX_raw[:, 3072 + xi * 512:3584 + xi * 512].bitcast(BF16)[0:n, 0:1024]
            s.op("dve", lambda h: h.memset(stat[0:n, so:so + 1], 0.0), reads=[], writes=["stat%d" % (bi % 4)])
            s.op("act", lambda h: h.activation(out=junkv[0:n, :], in_=xt, func=AF.Square, scale=1.0 / 32, accum_out=stat[0:n, so:so + 1]),
                 reads=[bl["key"]], writes=["junk", "stat%d" % (bi % 4)])
            s.op("act", lambda h: h.activation(out=stat[0:n, so + 1:so + 2], in_=stat[0:n, so:so + 1], func=AF.Sqrt, bias=EPS, scale=1.0),
                 reads=["stat%d" % (bi % 4)], writes=["stat%d" % (bi % 4)])
            s.op("dve", lambda h: h.reciprocal(out=stat[0:n, so + 2:so + 3], in_=stat[0:n, so + 1:so + 2]), reads=["stat%d" % (bi % 4)], writes=["stat%d" % (bi % 4)])
            s.op("act", lambda h: h.activation(out=xn, in_=xt, func=AF.Copy, scale=stat[0:n, so + 2:so + 3]),
                 reads=[bl["key"], "stat%d" % (bi % 4)], writes=["xn%d" % xi])

        def h_post(bi, bl, lt):
            n, sq, c0 = bl["n"], bl["sq"], bl["col0"]
            xi = bi % 3
            xn = AUX_raw[:, 3072 + xi * 512:3584 + xi * 512].bitcast(BF16)[0:n, 0:1024]
            mvt, gst = modv(lt), gsv(lt)
            p = next_pair()
            ptA = v3(bank(2 * p).bitcast(BF16)[:, 0:512], 4)
            ptB = v3(bank(2 * p + 1).bitcast(BF16)[:, 0:512], 4)

            def f(h):
                ins = None
                for kc in range(8):
                    dstp = ptA[:, kc, 0:n] if kc < 4 else ptB[:, kc - 4, 0:n]
                    ins = h.transpose(out=dstp, in_=xn[:, kc * 128:(kc + 1) * 128], identity=identb[0:n, 0:n])
                return ins
            s.op("pe", f, reads=["xn%d" % xi, "identb"], writes=bk(2 * p, 2 * p + 1))
            for kc in range(4):
                s.op("dve", lambda h, kc=kc: h.tensor_scalar(
                    out=hT[:, kc, c0:c0 + n], in0=ptA[:, kc, 0:n], scalar1=gst[:, kc, sq:sq + 1], scalar2=mvt[:, kc, sq:sq + 1],
                    op0=ALU.mult, op1=ALU.add), reads=["gsT", "modT"], writes=bk(2 * p) + ["hT.%d" % bi])
            for kc in range(4, 8):
                s.op("act", lambda h, kc=kc: h.activation(
                    out=hT[:, kc, c0:c0 + n], in_=ptB[:, kc - 4, 0:n], func=AF.Identity, scale=gst[:, kc, sq:sq + 1], bias=mvt[:, kc, sq:sq + 1]),
                    reads=["gsT", "modT"], writes=bk(2 * p + 1) + ["hTb.%d" % bi])

        hT_keys = ["hT.%d" % i for i in range(len(blocks))]

        def run_layer(l):
            mv = modv(l)
            last = (l == 3)
            cs_box["cs"] = 80 if (sti == 0 and l >= 1) else 0
            gP = gateP[l % 2]
            if l == 0:
                if sti > 0:
                    gate_rows(0, 0, gP, "gateP0")
                for bi, bl in enumerate(blocks):
                    h_pre(bi, bl, 0)
                    if bi >= 2:
                        h_post(bi - 2, blocks[bi - 2], 0)
                for bi in range(max(0, len(blocks) - 2), len(blocks)):
                    h_post(bi, blocks[bi], 0)
            if sd["sample"]:
                gate_rows(l, 1, gateS, "gateS")

            ada_sl = {}
            wo_box = {}

            def hoist_wo():
                if "wo" not in wo_box:
                    wo_box["wo"] = [fetch_out(l, e2) for e2 in range(8)]

            def mid_hook(grp):
                if l >= 3:
                    return
                if sti == 0:
                    if 2 <= grp <= 4:
                        ada_compute(l + 1, grp - 2, ada_sl[grp - 2])
                    if 1 <= grp <= 3:
                        ada_sl[grp - 1] = ada_fetch(l + 1, grp - 1)
                if grp == 5:
                    gate_rows(l + 1, 0, gateP[(l + 1) % 2], "gateP%d" % ((l + 1) % 2))

            if l == 0:
                wv_slots = [fetch_in(0, E + q * 256) for q in range(8)]

                def v_front(bi, bl):
                    n, c0 = bl["n"], bl["col0"]
                    smp = bl["sq"] == 1
                    gv = AUX_raw[0:n, (bi % 2) * 2048:(bi % 2) * 2048 + 2048]
                    vn = AUX_raw[:, 4096:6144].bitcast(BF16)[0:n, (bi % 2) * 2048:(bi % 2) * 2048 + 2048]
                    gk = ["aux.g%d" % (bi % 2)]
                    vk = ["aux.v%d" % (bi % 2)]
                    sk = ["statv%d" % (bi % 2)]

                    for vh in range(2):
                        def f(h, vh=vh):
                            ins = None
                            for kc in range(8):
                                for q in range(4 * vh, 4 * vh + 4):
                                    ins = h.matmul(bank(q // 2)[0:n, (q % 2) * 256:(q % 2) * 256 + 256], lhsT=hT[:, kc, c0:c0 + n],
                                                   rhs=wv_slots[q][1][:, kc, :], start=(kc == 0 and q % 2 == 0), stop=(kc == 7),
                                                   skip_group_check=True)
                            return ins
                        s.op("pe", f, reads=["hT.%d" % bi, "hTb.%d" % bi] + ["rs.%d" % i for i, _ in wv_slots[4 * vh:4 * vh + 4]], writes=bk(2 * vh, 2 * vh + 1))
                    so = 16 + (bi % 2) * 8
                    s.op("act", lambda h: h.activation(out=stat[0:n, so:so + 3], in_=stat[0:n, so:so + 3], func=AF.Copy, scale=0.0), reads=sk + ["stat_init"], writes=sk)
                    for hf in range(2):
                        s.op("act", lambda h, hf=hf: h.activation(out=gv[:, hf * 1024:(hf + 1) * 1024], in_=PS[0:n, hf * 1024:(hf + 1) * 1024],
                                                                  func=AF.Gelu, accum_out=stat[0:n, so + hf:so + hf + 1]),
                             reads=[], writes=bk(2 * hf, 2 * hf + 1) + gk + sk + (["xn0", "xn1"] if bi % 2 == 1 else []))
                    junk = Tpair01.bitcast(BF16)[0:n, 0:2048]
                    s.op("act", lambda h: h.activation(out=junk, in_=gv, func=AF.Square, accum_out=stat[0:n, so + 2:so + 3]),
                         reads=gk, writes=Tk(0, 1) + sk)
                    return gv, vn, gk, vk, sk, so, n, smp

                def v_stats(bi, bl, ctx):
                    gv, vn, gk, vk, sk, so, n, smp = ctx
                    s.op("dve", lambda h: h.tensor_tensor(out=stat[0:n, so + 3:so + 4], in0=stat[0:n, so:so + 1], in1=stat[0:n, so + 1:so + 2], op=ALU.add),
                         reads=sk, writes=sk)
                    s.op("dve", lambda h: h.tensor_scalar_mul(out=stat[0:n, so + 3:so + 4], in0=stat[0:n, so + 3:so + 4], scalar1=1.0 / E),
                         reads=sk, writes=sk)
                    s.op("dve", lambda h: h.tensor_tensor(out=stat[0:n, so + 4:so + 5], in0=stat[0:n, so + 3:so + 4], in1=stat[0:n, so + 3:so + 4], op=ALU.mult),
                         reads=sk, writes=sk)
                    s.op("dve", lambda h: h.scalar_tensor_tensor(out=stat[0:n, so + 5:so + 6], in0=stat[0:n, so + 2:so + 3], scalar=1.0 / E,
                                                                 in1=stat[0:n, so + 4:so + 5], op0=ALU.mult, op1=ALU.subtract),
                         reads=sk, writes=sk)
                    s.op("dve", lambda h: h.tensor_scalar_add(out=stat[0:n, so + 5:so + 6], in0=stat[0:n, so + 5:so + 6], scalar1=EPS), reads=sk, writes=sk)
                    s.op("pool", lambda h: h.tensor_tensor(out=stat[0:n, so + 6:so + 7], in0=stat[0:n, so + 5:so + 6], in1=stat[0:n, 63:64], op=ALU.pow),
                         reads=sk + ["neghalf"], writes=sk)
                    s.op("dve", lambda h: h.scalar_tensor_tensor(out=stat[0:n, so + 7:so + 8], in0=stat[0:n, so + 3:so + 4], scalar=-1.0,
                                                                 in1=stat[0:n, so + 6:so + 7], op0=ALU.mult, op1=ALU.mult),
                         reads=sk, writes=sk)
                    s.op("act", lambda h: h.activation(out=vn, in_=gv, func=AF.Identity, scale=stat[0:n, so + 6:so + 7], bias=stat[0:n, so + 7:so + 8]),
                         reads=gk + sk, writes=vk + (["xn2"] if bi % 2 == 0 else []))
                    if smp:
                        def vrows_out():
                            gx = gk + ["xn0", "xn1"]
                            s.op("dve", lambda h: h.tensor_scalar(out=gv, in0=gv, scalar1=stat[0:n, so + 6:so + 7], scalar2=stat[0:n, so + 7:so + 8],
                                                                  op0=ALU.mult, op1=ALU.add), reads=gx + vk + sk, writes=gx)
                            s.dma("sp", "su_st", stage[0:32, 0:2048], ln_a_gb[0:1, :].rearrange("a n -> (a n)").partition_broadcast(32), writes=Tk(0, 1, 2))
                            s.op("dve", lambda h: h.tensor_tensor(out=gv, in0=gv, in1=stage[0:32, 0:2048], op=ALU.mult), reads=Tk(0, 1, 2) + gx, writes=gx)
                            s.dma("sp", "su_st", stage[0:32, 0:2048], ln_a_gb[1:2, :].rearrange("a n -> (a n)").partition_broadcast(32), writes=Tk(0, 1, 2))
                            s.op("dve", lambda h: h.tensor_tensor(out=gv, in0=gv, in1=stage[0:32, 0:2048], op=ALU.add), reads=Tk(0, 1, 2) + gx, writes=gx)
                            s.dma("sp", "o_vr", vrows, gv, reads=gx, writes=["O.vrows"])
                        deferred_v.append(vrows_out)

                def v_back(bi, bl):
                    n, c0 = bl["n"], bl["col0"]
                    smp = bl["sq"] == 1
                    vn = AUX_raw[:, 4096:6144].bitcast(BF16)[0:n, (bi % 2) * 2048:(bi % 2) * 2048 + 2048]
                    vk = ["aux.v%d" % (bi % 2)]
                    wst = WsT_s if smp else WsT
                    bt = biasT_s if smp else biasT
                    mps = PS[:, 2048:2048 + 16 * n].rearrange("p (a b) -> p a b", a=16)

                    def f(h):
                        ins = None
                        for cc in range(16):
                            ins = h.matmul(mps[:, cc, :], lhsT=vn[:, cc * 128:(cc + 1) * 128], rhs=wst[0:n, cc // 2, 0:n], start=True, stop=True)
                        return ins
                    s.op("pe", f, reads=vk + ["WsT", "WsT_s"], writes=bk(4, 5, 6, 7))

                    def evac(lo, hi):
                        for cc in range(lo, hi):
                            s.op("dve", lambda h, cc=cc: h.scalar_tensor_tensor(
                                out=ysT[:, cc, c0:c0 + n], in0=mps[:, cc, :], scalar=prmT[:, cc, 7:8], in1=bt[:, cc, 0:n], op0=ALU.mult, op1=ALU.add),
                                reads=["prmT", "biasT", "biasT_s"], writes=bk(4, 5, 6, 7) + ["ysT.%d" % cc])
                    evac(0, 8)
                    return lambda: evac(8, 16)

                prevb = None
                deferred_v = []
                for bi, bl in enumerate(blocks):
                    ctx = v_front(bi, bl)
                    tail = v_back(*prevb) if prevb is not None else None
                    v_stats(bi, bl, ctx)
                    if tail is not None:
                        tail()
                    prevb = (bi, bl)
                v_back(*prevb)()
                for fdef in deferred_v:
                    fdef()
                pf = Pref([(lambda g=g: fetch_in(0, (g % 8) * 256 + (0 if g < 8 else 2 * E))) for g in range(16)], depth=3)
                for it in range(16):
                    if it == 0 and sti == 0:
                        ada_compute(0, 2, ada_fetch(0, 2))
                        gate_rows(0, 0, gP, "gateP0")
                    if it % 2 == 0:
                        mid_hook(it // 2)
                    sw = pf.get(it)
                    if it == 8:
                        hoist_wo()
                    isz = it >= 8
                    for ci in range(2):
                        c = (it % 8) * 2 + ci
                        pq = next_pair()
                        inproj(sw[0], sw[1], ci, pq, ncols)
                        tt_ = (2 if isz else 0) + (c % 2)
                        s.op("act", lambda h, pq=pq, tt_=tt_, isz=isz: h.activation(out=Tb(tt_)[:, 0:ncols], in_=pairv(pq, ncols), func=(AF.Silu if isz else AF.Gelu)),
                             reads=[], writes=bk(2 * pq, 2 * pq + 1) + Tk(tt_))
                        s.op("dve", lambda h, tt_=tt_, c=c: h.tensor_tensor(out=ysT[:, c, 0:ncols], in0=ysT[:, c, 0:ncols], in1=Tb(tt_)[:, 0:ncols], op=ALU.mult),
                             reads=Tk(tt_) + ["ysT.%d" % c], writes=["ysT.%d" % c])

            elif l == 1:
                pf = Pref([(lambda g=g: [fetch_in(1, b * E + g * 256) for b in range(4)]) for g in range(8)])
                for grp in range(8):
                    mid_hook(grp)
                    sl = pf.get(grp)
                    if grp == 4:
                        hoist_wo()
                    for ci in range(2):
                        c = grp * 2 + ci
                        pc_, ph_, pz_, pb_ = next_pair(), next_pair(), next_pair(), next_pair()
                        inproj(sl[1][0], sl[1][1], ci, pc_, ncols)
                        inproj(sl[2][0], sl[2][1], ci, ph_, ncols)
                        inproj(sl[3][0], sl[3][1], ci, pz_, ncols)
                        inproj(sl[0][0], sl[0][1], ci, pb_, ncols)
                        Pb = Tf(1)
                        s.op("act", lambda h, p=pc_: h.activation(out=Tf(0)[:, 0:ncols], in_=pairv(p, ncols), func=AF.Copy),
                             reads=[], writes=bk(2 * pc_, 2 * pc_ + 1) + Tk(0))
                        s.op("dve", lambda h, p=ph_: h.tensor_tensor(out=Tf(1)[:, 2:2 + ncols], in0=Tf(0)[:, 0:ncols], in1=pairv(p, ncols), op=ALU.mult),
                             reads=Tk(0), writes=bk(2 * ph_, 2 * ph_ + 1) + Tk(1))
                        s.op("act", lambda h, p=pz_, c=c: h.activation(out=ysT[:, c, 0:ncols], in_=pairv(p, ncols), func=AF.Silu),
                             reads=[], writes=bk(2 * pz_, 2 * pz_ + 1) + ["ysT.%d" % c])
                        s.op("act", lambda h, p=pb_: h.activation(out=Tf(3)[:, 0:ncols], in_=pairv(p, ncols), func=AF.Copy),
                             reads=[], writes=bk(2 * pb_, 2 * pb_ + 1) + Tk(3))
                        fix_hist(Pb, 2, (0, 2), "T.1", c)
                        s.op("dve", lambda h, c=c: h.tensor_copy(out=hist[:, c, 0:2], in_=Tf(1)[:, 2 + ep - 2:2 + ep]), reads=Tk(1), writes=["hist"])
                        if sd["sample"]:
                            s.op("dve", lambda h, c=c: h.tensor_copy(out=outS[:, c, 0:2], in_=Tf(1)[:, 2 + 702:2 + 704]), reads=Tk(1), writes=["outS"])
                        s.op("dve", lambda h, c=c: h.tensor_scalar_mul(out=Tf(2)[:, 0:ncols], in0=Tf(1)[:, 2:2 + ncols], scalar1=prmT[:, c, 2:3]),
                             reads=Tk(1) + ["prmT"], writes=Tk(2))
                        s.op("dve", lambda h, c=c: h.scalar_tensor_tensor(out=Tf(2)[:, 0:ncols], in0=Tf(1)[:, 1:1 + ncols], scalar=prmT[:, c, 1:2],
                                                                           in1=Tf(2)[:, 0:ncols], op0=ALU.mult, op1=ALU.add), reads=Tk(1, 2) + ["prmT"], writes=Tk(2))
                        s.op("dve", lambda h, c=c: h.scalar_tensor_tensor(out=Tf(2)[:, 0:ncols], in0=Tf(1)[:, 0:ncols], scalar=prmT[:, c, 0:1],
                                                                           in1=Tf(2)[:, 0:ncols], op0=ALU.mult, op1=ALU.add), reads=Tk(1, 2) + ["prmT"], writes=Tk(2))
                        s.op("dve", lambda h: h.tensor_tensor(out=Tf(2)[:, 0:ncols], in0=Tf(2)[:, 0:ncols], in1=Tf(3)[:, 0:ncols], op=ALU.mult),
                             reads=Tk(2, 3), writes=Tk(2))
                        s.op("dve", lambda h, c=c: h.tensor_tensor(out=ysT[:, c, 0:ncols], in0=ysT[:, c, 0:ncols], in1=Tf(2)[:, 0:ncols], op=ALU.mult),
                             reads=Tk(2) + ["ysT.%d" % c], writes=["ysT.%d" % c])

            elif l == 2:
                NTD = 4
                NTP = 31 - NTD

                def diag_view(c):
                    par = c % 2
                    dwords = ysT_raw[:, par * 2304:par * 2304 + NTP * 64].bitcast(BF16)
                    return dwords.rearrange("p (a b) -> p a b", a=NTP), ["ysT.%d" % i for i in range(par * 6, par * 6 + 6)]

                def build_diag(c, eng="pool"):
                    dg, dkeys = diag_view(c)
                    id_bc = bass.AP(identb.tensor, identb.offset, [list(identb.ap[0]), [0, NTP], [1, 128]])
                    w_bc = wccT[:, c, NTD:31].unsqueeze(2).broadcast_to([128, NTP, 128])
                    s.op(eng, lambda h: h.tensor_tensor(out=dg, in0=id_bc, in1=w_bc, op=ALU.mult), reads=["identb", "wccT"], writes=dkeys)

                pf = Pref([(lambda g=g: (fetch_in(2, g * 256), fetch_in(2, E + g * 256))) for g in range(8)])
                for grp in range(8):
                    mid_hook(grp)
                    if grp >= 6:
                        build_diag(grp - 6, "dve")
                    sa, sg = pf.get(grp)
                    for ci in range(2):
                        c = grp * 2 + ci
                        pa, pg = next_pair(), next_pair()
                        inproj(sa[0], sa[1], ci, pa, ncols)
                        inproj(sg[0], sg[1], ci, pg, ncols)
                        tt = c % 2
                        G = AUX[:, c, :]
                        s.op("act", lambda h, pg=pg, tt=tt: h.activation(out=Tf(tt)[:, 0:ncols], in_=pairv(pg, ncols), func=AF.Sigmoid),
                             reads=[], writes=bk(2 * pg, 2 * pg + 1) + Tk(tt))
                        s.op("dve", lambda h, pa=pa, tt=tt, G=G: h.tensor_tensor(out=G[:, 30:30 + ncols], in0=Tf(tt)[:, 0:ncols], in1=pairv(pa, ncols), op=ALU.mult),
                             reads=Tk(tt), writes=bk(2 * pa, 2 * pa + 1) + ["aux.%d" % c])
                        fix_hist(G, 30, (2, 32), "aux.%d" % c, c)
                        s.op("dve", lambda h, pa=pa, tt=tt, c=c: h.tensor_tensor(out=hist[:, c, 2:32], in0=Tf(tt)[:, ep - 30:ep], in1=pairv(pa, ncols)[:, ep - 30:ep], op=ALU.mult),
                             reads=Tk(tt) + ["aux.%d" % c], writes=bk(2 * pa, 2 * pa + 1) + ["hist"])
                        if sd["sample"]:
                            s.op("dve", lambda h, pa=pa, tt=tt, c=c: h.tensor_tensor(out=outS[:, c, 2:32], in0=Tf(tt)[:, 674:704], in1=pairv(pa, ncols)[:, 674:704], op=ALU.mult),
                                 reads=Tk(tt), writes=bk(2 * pa, 2 * pa + 1) + ["outS"])
                zsl = [fetch_in(2, 2 * E + g * 256) for g in range(8)]
                hoist_wo()
                pend_stats = None
                for c in range(16):
                    par = c % 2
                    dg, dkeys = diag_view(c)
                    pc = c % 2
                    G = AUX[:, c, :]
                    nts = ntiles(pc, ncols)

                    acc = Tf(c % 2)
                    s.op("act", lambda h, G=G, acc=acc, c=c: h.activation(out=acc[:, 0:ncols], in_=G[:, 0:ncols], func=AF.Copy, scale=wccT[:, c, 0:1]),
                         reads=["aux.%d" % c, "wccT"], writes=Tk(c % 2))
                    for k in range(1, NTD):
                        s.op("dve", lambda h, G=G, acc=acc, c=c, k=k: h.scalar_tensor_tensor(out=acc[:, 0:ncols], in0=G[:, k:k + ncols], scalar=wccT[:, c, k:k + 1],
                                                                                        in1=acc[:, 0:ncols], op0=ALU.mult, op1=ALU.add),
                             reads=["aux.%d" % c, "wccT"] + Tk(c % 2), writes=Tk(c % 2))

                    def f(h, dg=dg, G=G, nts=nts):
                        ins = None
                        for (c0, n, b, bo) in nts:
                            for k in range(NTD, 31):
                                ins = h.matmul(bank(b)[:, bo:bo + n], lhsT=dg[:, k - NTD, :], rhs=G[:, c0 + k:c0 + k + n], start=(k == NTD), stop=(k == 30))
                        return ins
                    s.op("pe", f, reads=dkeys + ["aux.%d" % c], writes=bk(2 * pc, 2 * pc + 1))
                    if c + 2 < 16:
                        build_diag(c + 2)
                    s.op("dve", lambda h, pc=pc, G=G, c=c, acc=acc: h.scalar_tensor_tensor(out=G[:, 30:30 + ncols], in0=pairv(pc, ncols), scalar=prmT[:, c, 3:4],
                                                                                          in1=acc[:, 0:ncols], op0=ALU.add, op1=ALU.add),
                         reads=["prmT"] + Tk(c % 2), writes=bk(2 * pc, 2 * pc + 1) + ["aux.%d" % c])
                    sqt = 2 + (c % 2)
                    s.op("act", lambda h, G=G, sqt=sqt: h.activation(out=Tb(sqt)[:, 0:ncols], in_=G[:, 30:30 + ncols], func=AF.Square),
                         reads=["aux.%d" % c], writes=Tk(sqt))

                    def f2(h, G=G, sqt=sqt, c=c):
                        ins = None
                        for (c0, n, bq, bo) in ntiles(0, ncols):
                            ins = h.matmul(bank(4 + bq)[:, bo:bo + n], lhsT=onesb, rhs=G[:, 30 + c0:30 + c0 + n], start=(c == 0), stop=(c == 15))
                            ins = h.matmul(bank(6 + bq)[:, bo:bo + n], lhsT=onesb, rhs=Tb(sqt)[:, c0:c0 + n], start=(c == 0), stop=(c == 15))
                        return ins
                    if pend_stats is not None:
                        s.op("pe", pend_stats[0], reads=pend_stats[1], writes=bk(4, 5, 6, 7))
                    pend_stats = (f2, ["aux.%d" % c, "onesb"] + Tk(sqt))
                s.op("pe", pend_stats[0], reads=pend_stats[1], writes=bk(4, 5, 6, 7))
                s.op("dve", lambda h: h.tensor_scalar_mul(out=Tf(0)[:, 0:ncols], in0=pairv(2, ncols), scalar1=1.0 / E), reads=[], writes=bk(4, 5) + Tk(0))
                s.op("dve", lambda h: h.tensor_tensor(out=Tf(2)[:, 0:ncols], in0=Tf(0)[:, 0:ncols], in1=Tf(0)[:, 0:ncols], op=ALU.mult), reads=Tk(0), writes=Tk(2))
                s.op("dve", lambda h: h.scalar_tensor_tensor(out=Tf(1)[:, 0:ncols], in0=pairv(3, ncols), scalar=1.0 / E, in1=Tf(2)[:, 0:ncols],
                                                              op0=ALU.mult, op1=ALU.subtract), reads=Tk(2), writes=bk(6, 7) + Tk(1))
                s.op("dve", lambda h: h.tensor_scalar_max(out=Tf(1)[:, 0:ncols], in0=Tf(1)[:, 0:ncols], scalar1=0.0), reads=Tk(1), writes=Tk(1))
                s.op("act", lambda h: h.activation(out=Tf(1)[:, 0:ncols], in_=Tf(1)[:, 0:ncols], func=AF.Sqrt, bias=EPS, scale=1.0), reads=Tk(1), writes=Tk(1))
                s.op("dve", lambda h: h.reciprocal(out=Tf(1)[:, 0:ncols], in_=Tf(1)[:, 0:ncols]), reads=Tk(1), writes=Tk(1))
                pend_m = None
                pair_rr["i"] = 0
                for grp in range(8):
                    sz = zsl[grp]
                    for ci in range(2):
                        c = grp * 2 + ci
                        pz = next_pair()
                        inproj(sz[0], sz[1], ci, pz, ncols)
                        G = AUX[:, c, :]
                        tq = 2 + (c % 2)
                        s.op("act", lambda h, pz=pz, c=c: h.activation(out=ysT[:, c, 0:ncols], in_=pairv(pz, ncols), func=AF.Silu),
                             reads=[], writes=bk(2 * pz, 2 * pz + 1) + ["ysT.%d" % c])
                        s.op("dve", lambda h, G=G, tq=tq: h.tensor_tensor(out=Tf(tq)[:, 0:ncols], in0=G[:, 30:30 + ncols], in1=Tf(0)[:, 0:ncols], op=ALU.subtract),
                             reads=["aux.%d" % c] + Tk(0), writes=Tk(tq))
                        s.op("dve", lambda h, tq=tq: h.tensor_tensor(out=Tf(tq)[:, 0:ncols], in0=Tf(tq)[:, 0:ncols], in1=Tf(1)[:, 0:ncols], op=ALU.mult),
                             reads=Tk(1, tq), writes=Tk(tq))
                        s.op("act", lambda h, c=c, G=G, tq=tq: h.activation(out=G[:, 30:30 + ncols], in_=Tf(tq)[:, 0:ncols], func=AF.Silu, scale=prmT[:, c, 4:5], bias=prmT[:, c, 5:6]),
                             reads=Tk(tq) + ["prmT"], writes=["aux.%d" % c])
                        if pend_m is not None:
                            pend_m()
                        pend_m = (lambda c=c, G=G: s.op("dve", lambda h: h.tensor_tensor(out=ysT[:, c, 0:ncols], in0=ysT[:, c, 0:ncols], in1=G[:, 30:30 + ncols], op=ALU.mult),
                                                          reads=["aux.%d" % c, "ysT.%d" % c], writes=["ysT.%d" % c]))
                pend_m()

            else:
                Wd = 15 + ncols
                pf = Pref([(lambda g=g: (fetch_in(3, g * 256), fetch_in(3, E + g * 256))) for g in range(8)])
                wp_box = {}
                for grp in range(8):
                    sp_, sz = pf.get(grp)
                    if grp == 4:
                        wp_box["wp"] = [fetch_pool(g) for g in range(4)]
                        hoist_wo()
                    for ci in range(2):
                        c = grp * 2 + ci
                        g = c // 4
                        win = 2 << g
                        pp, pz = next_pair(), next_pair()
                        inproj(sp_[0], sp_[1], ci, pp, ncols)
                        inproj(sz[0], sz[1], ci, pz, ncols)
                        s.op("act", lambda h, pp=pp: h.activation(out=Tf(0)[:, 15:15 + ncols], in_=pairv(pp, ncols), func=AF.Copy),
                             reads=[], writes=bk(2 * pp, 2 * pp + 1) + Tk(0))
                        s.op("act", lambda h, pz=pz, c=c: h.activation(out=ysT[:, c, 0:ncols], in_=pairv(pz, ncols), func=AF.Silu),
                             reads=[], writes=bk(2 * pz, 2 * pz + 1) + ["ysT.%d" % c])
                        fix_hist(Tf(0), 15, (32, 47), "T.0", c)
                        s.op("dve", lambda h, c=c: h.tensor_copy(out=hist[:, c, 32:47], in_=Tf(0)[:, 15 + ep - 15:15 + ep]), reads=Tk(0), writes=["hist"])
                        if sd["sample"]:
                            s.op("dve", lambda h, c=c: h.tensor_copy(out=outS[:, c, 32:47], in_=Tf(0)[:, 15 + 689:15 + 704]), reads=Tk(0), writes=["outS"])
                        s.op("dve", lambda h: h.tensor_tensor(out=Tf(1)[:, 1:Wd], in0=Tf(0)[:, 1:Wd], in1=Tf(0)[:, 0:Wd - 1], op=ALU.add), reads=Tk(0), writes=Tk(1))
                        fin = 1
                        if win >= 4:
                            s.op("dve", lambda h: h.tensor_tensor(out=Tf(2)[:, 3:Wd], in0=Tf(1)[:, 3:Wd], in1=Tf(1)[:, 1:Wd - 2], op=ALU.add), reads=Tk(1), writes=Tk(2))
                            fin = 2
                        if win >= 8:
                            s.op("dve", lambda h: h.tensor_tensor(out=Tf(1)[:, 7:Wd], in0=Tf(2)[:, 7:Wd], in1=Tf(2)[:, 3:Wd - 4], op=ALU.add), reads=Tk(2), writes=Tk(1))
                            fin = 1
                        if win >= 16:
                            s.op("dve", lambda h: h.tensor_tensor(out=Tf(2)[:, 15:Wd], in0=Tf(1)[:, 15:Wd], in1=Tf(1)[:, 7:Wd - 8], op=ALU.add), reads=Tk(1), writes=Tk(2))
                            fin = 2
                        s.op("dve", lambda h, fin=fin, win=win, c=c: h.scalar_tensor_tensor(out=AUX[:, c, 0:ncols], in0=Tf(fin)[:, 15:Wd], scalar=1.0 / win,
                                                                                            in1=Tf(0)[:, 15:Wd], op0=ALU.mult, op1=ALU.subtract),
                             reads=Tk(0, fin), writes=["aux.%d" % c])
                        if sti == 0:
                            s.op("dve", lambda h, fin=fin, g=g: h.tensor_tensor(out=Tf(3)[:, 0:16], in0=Tf(fin)[:, 15 + 128:15 + 144], in1=invc[:, g * 16:(g + 1) * 16], op=ALU.mult),
                                 reads=Tk(fin) + ["invc"], writes=Tk(3))
                            s.op("dve", lambda h, c=c: h.tensor_tensor(out=AUX[:, c, 128:144], in0=Tf(3)[:, 0:16], in1=Tf(0)[:, 15 + 128:15 + 144], op=ALU.subtract),
                                 reads=Tk(0, 3) + ["aux.%d" % c], writes=["aux.%d" % c])
                wp = wp_box["wp"]
                for fc in range(16):
                    g, fi = fc // 4, fc % 4
                    p = next_pair()
                    nts = ntiles(p, ncols)

                    def f(h, g=g, fi=fi, nts=nts):
                        ins = None
                        for cc in range(4):
                            for (c0, n, b, bo) in nts:
                                ins = h.matmul(bank(b)[:, bo:bo + n], lhsT=wp[g][1][:, cc, fi * 128:(fi + 1) * 128], rhs=AUX[:, 4 * g + cc, c0:c0 + n],
                                               start=(cc == 0), stop=(cc == 3))
                        return ins
                    s.op("pe", f, reads=["rs.%d" % wp[g][0]] + ["aux.%d" % (4 * g + cc) for cc in range(4)], writes=bk(2 * p, 2 * p + 1))
                    s.op("dve", lambda h, p=p, fc=fc: h.scalar_tensor_tensor(out=ysT[:, fc, 0:ncols], in0=pairv(p, ncols), scalar=prmT[:, fc, 6:7],
                                                                              in1=ysT[:, fc, 0:ncols], op0=ALU.mult, op1=ALU.mult),
                         reads=["prmT", "ysT.%d" % fc], writes=bk(2 * p, 2 * p + 1) + ["ysT.%d" % fc])

            hoist_wo()
            wo = wo_box["wo"]

            pendq = []
            for bi, bl in enumerate(blocks):
                n, xt, sq, c0 = bl["n"], bl["xt"], bl["sq"], bl["col0"]
                p = next_pair()
                gsrc = gateS if sq == 1 else gP
                gkey = "gateS" if sq == 1 else "gateP%d" % (l % 2)

                def f(h, n=n, c0=c0, p=p):
                    ins = None
                    for ec in range(16):
                        for hf in range(2):
                            ins = h.matmul(bank(2 * p + hf)[0:n, :], lhsT=ysT[:, ec, c0:c0 + n], rhs=wo[ec // 2][1][:, ec % 2, hf * 512:(hf + 1) * 512],
                                           start=(ec == 0), stop=(ec == 15))
                    return ins
                s.op("pe", f, reads=["ysT.%d" % i for i in range(16)] + ["rs.%d" % i for i, _ in wo], writes=bk(2 * p, 2 * p + 1))
                is_tail = (bi == len(blocks) - 1)
                if len(pendq) >= 2 and not is_tail:
                    pb = pendq.pop(0)
                    h_post(pb[0], pb[1], l + 1)
                for hf in range(2):
                    tmp = Tf(hf)[0:n, 0:512]
                    s.op("dve", lambda h, hf=hf, p=p, n=n, tmp=tmp, gsrc=gsrc: h.tensor_tensor(out=tmp, in0=bank(2 * p + hf)[0:n, :], in1=gsrc[0:n, hf * 512:(hf + 1) * 512], op=ALU.mult),
                         reads=[gkey], writes=bk(2 * p + hf) + Tk(hf))
                    s.op("dve", lambda h, hf=hf, xt=xt, tmp=tmp: h.tensor_tensor(out=xt[:, hf * 512:(hf + 1) * 512], in0=xt[:, hf * 512:(hf + 1) * 512], in1=tmp, op=ALU.add),
                         reads=Tk(hf) + [bl["key"]], writes=[bl["key"]])
                if not last:
                    h_pre(bi, bl, l + 1)
                    pendq.append((bi, bl))
                elif bl["gb"] is None or bl["gb"] >= 1:
                    so = 32 + (bi % 4) * 4
                    s.op("dve", lambda h, so=so, n=n: h.memset(stat[0:n, so:so + 1], 0.0), reads=[], writes=["statf%d" % (bi % 4)])
                    s.op("act", lambda h, xt=xt, so=so, n=n: h.activation(out=junkv[0:n, :], in_=xt, func=AF.Square, scale=1.0 / 32, accum_out=stat[0:n, so:so + 1]),
                         reads=[bl["key"]], writes=["junk", "statf%d" % (bi % 4)])
                    s.op("act", lambda h, so=so, n=n: h.activation(out=stat[0:n, so + 1:so + 2], in_=stat[0:n, so:so + 1], func=AF.Sqrt, bias=EPS, scale=1.0),
                         reads=["statf%d" % (bi % 4)], writes=["statf%d" % (bi % 4)])
                    s.op("dve", lambda h, so=so, n=n: h.reciprocal(out=stat[0:n, so + 2:so + 3], in_=stat[0:n, so + 1:so + 2]), reads=["statf%d" % (bi % 4)], writes=["statf%d" % (bi % 4)])
                    yq = bi % 2
                    yt = AUX_raw[0:n, 1024 + yq * 1024:2048 + yq * 1024]
                    yk = ["aux.g%d" % yq]
                    s.op("dve", lambda h, xt=xt, so=so, n=n, yt=yt: h.scalar_tensor_tensor(out=yt, in0=xt, scalar=stat[0:n, so + 2:so + 3], in1=gf_bc[0:n, :],
                                                                                            op0=ALU.mult, op1=ALU.mult),
                         reads=[bl["key"], "statf%d" % (bi % 4), "gf_bc"], writes=yk)
                    if bl["gb"] is None:
                        s.dma("sp", "o_y%d" % yq, ysm, yt, reads=yk, writes=["O.ysm"])
                    else:
                        gb = bl["gb"] - 1
                        s.dma("sp", "o_y%d" % yq, yp[gb * 128:(gb + 1) * 128, :], yt, reads=yk, writes=["O.yp%d" % gb])
            for pb in pendq:
                h_post(pb[0], pb[1], l + 1)

        for l in range(4):
            run_layer(l)
            if sti == 2:
                emit_state_outputs(l)

    for sti, sd in enumerate(ST_DEFS):
        run_st(sti, sd)

    okeys = ["O.ysm", "O.vrows", "O.cbp", "O.cbs", "O.ccp", "O.ccs", "O.pdp", "O.pds"] + ["O.yp%d" % i for i in range(16)]
    s.wait_all("sp", okeys)
    with nc.Block() as block:
        s.emit_all(block)
    return nc


_WKEYS = ["w_ada", "b_ada", "g_norm", "w_in_a", "w_in_b", "w_in_c", "w_in_d", "w_out_a", "w_out_b", "w_out_c", "w_out_d",
          "w_conv_c", "w_s_a", "b_s_a", "w_pool_d"]


def kernel(**inp):
    f = lambda k: np.ascontiguousarray(np.asarray(inp[k], dtype=np.float32))
    xpr, xsa = f("x_prompt"), f("x_sample")
    shared = {k: f(k) for k in _WKEYS}
    shared["prm9"] = np.ascontiguousarray(np.concatenate(
        [f("w_conv_b"), f("b_conv_c")[None], f("ln_c_g")[None], f("ln_c_b")[None], f("scale_pool_d")[None], f("ln_a_g")[None], f("ln_a_b")[None]], 0))
    shared["ln_a_gb"] = np.ascontiguousarray(np.stack([f("ln_a_g"), f("ln_a_b")], 0))
    shared["g_final"] = f("g_final")[None]
    cpr, csa = f("c_prompt"), f("c_sample")
    sb, sc, sd_ = f("state_conv_b"), f("state_conv_c"), f("state_pool_d")
    wins = (2, 4, 8, 16)
    in_maps = []
    for i in range(8):
        b, half = i // 2, i % 2
        halo = xpr[b, 1920:2048] if half == 1 else np.zeros((128, D), np.float32)
        m = dict(shared)
        m["xp"] = np.ascontiguousarray(np.concatenate([halo, xpr[b, half * 2048:(half + 1) * 2048]], 0))
        m["xsm"] = xsa[i]
        m["st_b"], m["st_c"], m["st_d"] = sb[i], sc[i], sd_[i]
        m["cvec"] = np.ascontiguousarray(np.stack([cpr[b], csa[i]], 0))
        m["maskc"] = np.full((128, 1), float(half), np.float32)
        tab = np.zeros((4, 16), np.float32)
        for g, w in enumerate(wins):
            for j in range(16):
                tab[g, j] = 1.0 / (min(w, j + 1) if half == 0 else w)
        m["invc"] = np.ascontiguousarray(np.broadcast_to(tab.reshape(1, 64), (128, 64)))
        in_maps.append(m)
    nc = build_program()
    res = run_bass_kernel_spmd(nc, in_maps, core_ids=list(range(8)))
    R = res.results
    y_prompt = np.zeros((4, 4096, D), np.float32)
    for i in range(8):
        y_prompt[i // 2, (i % 2) * 2048:(i % 2 + 1) * 2048] = R[i]["yp"]
    stk = lambda k, ids: np.stack([np.asarray(R[i][k], np.float32) for i in ids], 0)
    allc = list(range(8))
    odd = [1, 3, 5, 7]
    return (y_prompt, stk("ysm", allc), stk("vrows", allc), stk("cb_p", odd), stk("cb_s", allc),
            stk("cc_p", odd), stk("cc_s", allc), stk("pd_p", odd), stk("pd_s", allc))
```

```python
import numpy as np
import concourse.bass as bass
import concourse.mybir as mybir
from concourse.bass_utils import run_bass_kernel_spmd

F32 = mybir.dt.float32
BF16 = mybir.dt.bfloat16
ALU = mybir.AluOpType
AF = mybir.ActivationFunctionType

ENGS = ("pe", "act", "dve", "pool", "sp")
D = 1024
E = 2048
EPS = 1e-6


class Sched:
    def __init__(self, nc):
        self.nc = nc
        self.streams = {e: [] for e in ENGS}
        self.cnt = {}
        self.seen = {e: {} for e in ENGS}
        self.res = {}
        self.sem = {}
        for e in ENGS:
            self.sem[e] = nc.alloc_semaphore("p_" + e)
            self.cnt[e] = 0

    def _deps(self, eng, reads, writes):
        waits = {}

        def add(sv):
            if sv is None:
                return
            sname, v = sv
            if v > waits.get(sname, 0):
                waits[sname] = v

        for k in reads:
            r = self.res.get(k)
            if r is not None:
                add(r[0])
        for k in writes:
            r = self.res.get(k)
            if r is not None:
                add(r[0])
                for sname, v in r[1].items():
                    add((sname, v))
        out = []
        for sname, v in waits.items():
            if self.seen[eng].get(sname, 0) >= v:
                continue
            self.seen[eng][sname] = v
            out.append((sname, v))
        return out

    def _mark(self, src, val, reads, writes):
        for k in writes:
            self.res[k] = [(src, val), {}]
        for k in reads:
            r = self.res.get(k)
            if r is None:
                r = self.res[k] = [None, {}]
            r[1][src] = val

    def op(self, eng, fn, reads=(), writes=()):
        ro = getattr(self, "ring_out", None)
        if ro:
            for k in reads:
                ro.discard(k)
        waits = self._deps(eng, reads, writes)
        if eng == "pe":
            waits = [(a, v) for (a, v) in waits if a != "pe"]
        self.cnt[eng] += 1
        val = self.cnt[eng]
        sems = self.sem
        wl = [(sems[a], v) for a, v in waits]
        mysem = sems[eng]

        def emit(h):
            for sm, v in wl:
                h.wait_ge(sm, v)
            ins = fn(h)
            ins.then_inc(mysem, 1)

        self.streams[eng].append(emit)
        self._mark(eng, val, reads, writes)

    def dma(self, q, slot, out, in_, reads=(), writes=(), **kw):
        if slot not in self.sem:
            self.sem[slot] = self.nc.alloc_semaphore("d_" + slot)
            self.cnt[slot] = 0
        waits = self._deps(q, reads, writes)
        self.cnt[slot] += 16
        val = self.cnt[slot]
        sems = self.sem
        wl = [(sems[a], v) for a, v in waits]
        dsm = sems[slot]

        def emit(h):
            for sm, v in wl:
                h.wait_ge(sm, v)
            h.dma_start(out=out, in_=in_, **kw).then_inc(dsm, 16)

        self.streams[q].append(emit)
        self._mark(slot, val, reads, writes)

    def wait_all(self, eng, keys):
        waits = self._deps(eng, keys, keys)
        sems = self.sem
        wl = [(sems[a], v) for a, v in waits]

        def emit(h):
            for sm, v in wl:
                h.wait_ge(sm, v)

        self.streams[eng].append(emit)

    def emit_all(self, block):
        st = self.streams

        @block.tensor
        def _(h):
            for f in st["pe"]:
                f(h)

        @block.scalar
        def _(h):
            for f in st["act"]:
                f(h)

        @block.vector
        def _(h):
            for f in st["dve"]:
                f(h)

        @block.gpsimd
        def _(h):
            for f in st["pool"]:
                f(h)

        @block.sync
        def _(h):
            for f in st["sp"]:
                f(h)


def build_program():
    nc = bass.Bass("TRN2", target_bir_lowering=False)

    def din(name, shape):
        return nc.dram_tensor(name, list(shape), F32, kind="ExternalInput").ap()

    def dout(name, shape):
        return nc.dram_tensor(name, list(shape), F32, kind="ExternalOutput").ap()

    xp = din("xp", [2176, D])
    xsm = din("xsm", [32, D])
    st_b = din("st_b", [2, E])
    st_c = din("st_c", [30, E])
    st_d = din("st_d", [15, E])
    cvec = din("cvec", [2, D])
    maskc_d = din("maskc", [128, 1])
    invc_d = din("invc", [128, 64])
    w_ada = din("w_ada", [4, D, 3 * D])
    b_ada = din("b_ada", [4, 3 * D])
    g_norm = din("g_norm", [4, D])
    w_in = [din("w_in_a", [D, 3 * E]), din("w_in_b", [D, 4 * E]), din("w_in_c", [D, 3 * E]), din("w_in_d", [D, 2 * E])]
    w_out = [din("w_out_a", [E, D]), din("w_out_b", [E, D]), din("w_out_c", [E, D]), din("w_out_d", [E, D])]
    prm9 = din("prm9", [9, E])
    w_conv_c = din("w_conv_c", [31, E])
    w_s_a = din("w_s_a", [8, 128, 128])
    b_s_a = din("b_s_a", [8, 128])
    w_pool = din("w_pool_d", [4, 512, 512])
    g_final = din("g_final", [1, D])
    ln_a_gb = din("ln_a_gb", [2, E])

    yp = dout("yp", [2048, D])
    ysm = dout("ysm", [32, D])
    vrows = dout("vrows", [32, E])
    o_cb_p = dout("cb_p", [2, E]); o_cb_s = dout("cb_s", [2, E])
    o_cc_p = dout("cc_p", [30, E]); o_cc_s = dout("cc_s", [30, E])
    o_pd_p = dout("pd_p", [15, E]); o_pd_s = dout("pd_s", [15, E])

    s = Sched(nc)
    TOTAL = 53000
    big = nc.alloc_sbuf_tensor("big", [128, TOTAL], F32)
    off = [0]

    def carve(nw):
        v = big[:, off[0]:off[0] + nw]
        off[0] += nw
        assert off[0] <= TOTAL, off[0]
        return v

    def v3(v, a):
        return v.rearrange("p (a b) -> p a b", a=a)

    X = v3(carve(6 * 1024), 6)
    XS = X[:, 5, :]
    hT = v3(carve(3072).bitcast(BF16), 8)
    ysT_raw = carve(6144)
    ysT = v3(ysT_raw.bitcast(BF16), 16)
    AUX_raw = carve(6400)
    AUX = v3(AUX_raw.bitcast(BF16), 16)
    RING = carve(16 * 1024)
    Traw = [carve(800) for _ in range(4)]
    gateP = [carve(1024), carve(1024)]
    gateS = carve(1024)
    dscr = [carve(128), carve(128)]
    scTb = v3(carve(8).bitcast(BF16), 8)
    gf_bc = carve(1024)
    biasT = v3(carve(2048), 16)
    biasT_s = v3(carve(512), 16)
    WsT = v3(carve(512).bitcast(BF16), 8)
    WsT_s = v3(carve(128).bitcast(BF16), 8)
    identb = carve(64).bitcast(BF16)
    identf = carve(128)
    onesf = carve(128)
    onesb = carve(64).bitcast(BF16)
    prmT = v3(carve(16 * 9), 16)
    wccT = v3(carve(16 * 32), 16)
    stT = v3(carve(16 * 48), 16)
    hist = v3(carve(16 * 48), 16)
    outS = v3(carve(16 * 48), 16)
    gnT = v3(carve(32), 8)
    badaT = v3(carve(96), 24)
    cT = v3(carve(16), 8)
    scT = v3(carve(16), 8)
    modT = carve(4 * 48)
    gsT = carve(4 * 16)
    stat = carve(64)
    maskc = carve(1)
    invc = carve(64)

    def modv(l):
        return v3(modT[:, l * 48:(l + 1) * 48], 24)

    def gsv(l):
        return v3(gsT[:, l * 16:(l + 1) * 16], 8)

    PS = nc.alloc_psum_tensor("ps", [128, 4096], F32)

    def bank(b):
        return PS[:, b * 512:(b + 1) * 512]

    def bk(*bs):
        return ["bk.%d" % b for b in bs]

    def Tf(i, n=800):
        return Traw[i][:, 0:n]

    def Tb(i, n=1600):
        return Traw[i].bitcast(BF16)[:, 0:n]

    def Tk(*idx):
        return ["T.%d" % i for i in idx]

    Tcat = big[:, Traw[0].offset - big.offset:Traw[0].offset - big.offset + 3200] if False else None

    t0_off = 6 * 1024 + 1024 + 3072 + 6144 + 6400 + 16 * 1024
    stage = big[:, t0_off:t0_off + 3200]
    Tpair01 = big[:, t0_off:t0_off + 1600]

    s.op("pool", lambda h: h.memset(identf, 0.0), writes=["identf"])
    s.op("pool", lambda h: h.affine_select(out=identf, in_=identf, pattern=[[-1, 128]], compare_op=ALU.not_equal,
                                           fill=1.0, base=0, channel_multiplier=1), reads=["identf"], writes=["identf"])
    s.op("dve", lambda h: h.tensor_copy(out=identb, in_=identf), reads=["identf"], writes=["identb"])
    s.op("dve", lambda h: h.memset(onesf, 1.0), writes=["onesf"])
    s.op("dve", lambda h: h.memset(onesb, 1.0), writes=["onesb"])
    s.op("dve", lambda h: h.memset(stat[:, 0:63], 0.0), writes=["stat_init"])
    s.op("dve", lambda h: h.memset(stat[:, 63:64], -0.5), writes=["neghalf"])
    s.op("dve", lambda h: h.memset(hist.rearrange("p a b -> p (a b)"), 0.0), writes=["hist"])
    s.dma("sp", "su_m", maskc, maskc_d, writes=["maskc"])
    s.dma("sp", "su_i", invc, invc_d, writes=["invc"])
    s.dma("sp", "su_g", gf_bc, g_final.partition_broadcast(128), writes=["gf_bc"])

    def load_rows_T(src, R, C, dst3, dkey, rp=32):
        nch = C // 128
        s.dma("sp", "su_st", stage[0:R, 0:C], src, writes=Tk(0, 1, 2, 3))
        pst = v3(bank(0)[:, 0:nch * rp], nch)

        def f(h):
            ins = None
            for c in range(nch):
                ins = h.transpose(out=pst[:, c, 0:R], in_=stage[0:R, c * 128:(c + 1) * 128], identity=identf[0:R, 0:R])
            return ins
        s.op("pe", f, reads=Tk(0, 1, 2, 3) + ["identf"], writes=bk(0))
        s.op("dve", lambda h: h.tensor_copy(out=dst3[:, :, 0:R], in_=pst[:, :, 0:R]), reads=[], writes=bk(0) + [dkey])

    load_rows_T(cvec, 2, D, cT, "cT")
    load_rows_T(g_norm, 4, D, gnT, "gnT")
    load_rows_T(b_ada, 4, 3 * D, badaT, "badaT", rp=4)
    load_rows_T(prm9, 9, E, prmT, "prmT")
    load_rows_T(w_conv_c, 31, E, wccT, "wccT")
    load_rows_T(st_b, 2, E, stT[:, :, 0:2], "stT")
    load_rows_T(st_c, 30, E, stT[:, :, 2:32], "stT")
    load_rows_T(st_d, 15, E, stT[:, :, 32:47], "stT")
    s.op("act", lambda h: h.activation(out=scT.rearrange("p a b -> p (a b)"), in_=cT.rearrange("p a b -> p (a b)"), func=AF.Silu),
         reads=["cT"], writes=["scT"])

    wsI = v3(stage[:, 0:1024], 8)
    s.dma("sp", "su_st", wsI, w_s_a.rearrange("g i j -> i g j"), writes=Tk(0, 1, 2))
    ps_w = v3(PS[:, 0:1024], 8)

    def f_ws_s(h):
        ins = None
        for g in range(8):
            ins = h.transpose(out=ps_w[0:32, g, 0:32], in_=wsI[0:32, g, 0:32], identity=identf[0:32, 0:32])
        return ins
    s.op("pe", f_ws_s, reads=Tk(0, 1, 2) + ["identf"], writes=bk(0, 1))
    s.op("dve", lambda h: h.tensor_copy(out=WsT_s[0:32, :, :], in_=ps_w[0:32, :, 0:32]), reads=[], writes=bk(0, 1) + ["WsT_s"])
    s.op("dve", lambda h: h.memset(wsI[0:64, :, 64:128], 0.0), reads=[], writes=Tk(0, 1, 2))

    def f_ws(h):
        ins = None
        for g in range(8):
            ins = h.transpose(out=ps_w[:, g, :], in_=wsI[:, g, :], identity=identf)
        return ins
    s.op("pe", f_ws, reads=Tk(0, 1, 2) + ["identf"], writes=bk(0, 1))
    s.op("dve", lambda h: h.tensor_copy(out=WsT, in_=ps_w), reads=[], writes=bk(0, 1) + ["WsT"])
    def f_rs(h):
        ins = None
        for hf in range(2):
            ins = h.matmul(bank(2 + hf), lhsT=onesb, rhs=WsT[:, hf * 4:(hf + 1) * 4, :].rearrange("p a b -> p (a b)"), start=True, stop=True)
        return ins
    s.op("pe", f_rs, reads=["WsT", "onesb"], writes=bk(2, 3))
    rs_bc = v3(PS[:, 1024:2048], 8)

    def f_rs_s(h):
        return h.matmul(bank(4)[:, 0:256], lhsT=onesb[0:32, :], rhs=WsT_s[0:32, :, :].rearrange("p a b -> p (a b)"), start=True, stop=True)
    s.op("pe", f_rs_s, reads=["WsT_s", "onesb"], writes=bk(4))
    rs_bc_s = v3(bank(4)[:, 0:256], 8)
    bs_bc = v3(stage[:, 1024:2048], 8)
    s.dma("sp", "su_bs", bs_bc.rearrange("p a b -> p (a b)"), b_s_a.rearrange("g i -> (g i)").partition_broadcast(128), writes=Tk(1, 2))
    for cc in range(16):
        g = cc // 2
        s.op("dve", lambda h, cc=cc, g=g: h.scalar_tensor_tensor(out=biasT[:, cc, :], in0=rs_bc[:, g, :], scalar=prmT[:, cc, 8:9],
                                                                  in1=bs_bc[:, g, :], op0=ALU.mult, op1=ALU.add),
             reads=["prmT"] + Tk(1, 2), writes=bk(2, 3) + ["biasT"])
        s.op("dve", lambda h, cc=cc, g=g: h.scalar_tensor_tensor(out=biasT_s[:, cc, :], in0=rs_bc_s[:, g, :], scalar=prmT[:, cc, 8:9],
                                                                  in1=bs_bc[:, g, 0:32], op0=ALU.mult, op1=ALU.add),
             reads=["prmT"] + Tk(1, 2), writes=bk(4) + ["biasT_s"])

    s.op("dve", lambda h: h.tensor_copy(out=scTb.rearrange("p a b -> p (a b)"), in_=scT.rearrange("p a b -> p (a b)")), reads=["scT"], writes=["scTb"])
    ring_state = {"next": 0}

    ring_out = set()
    s.ring_out = ring_out

    def ring_slot():
        i = ring_state["next"]
        ring_state["next"] = (i + 1) % 16
        assert ("rs.%d" % i) not in ring_out, "ring slot %d re-allocated before its consumer was recorded" % i
        ring_out.add("rs.%d" % i)
        return i

    def slot_bf(i):
        return RING[:, i * 1024:(i + 1) * 1024].bitcast(BF16)

    def fetch_in(l, col0):
        i = ring_slot()
        v = v3(slot_bf(i), 8)
        s.dma("pool", "rs%d" % i, v, w_in[l].rearrange("(kc p) n -> p kc n", p=128)[:, :, col0:col0 + 256], writes=["rs.%d" % i])
        return i, v

    def fetch_out(l, e2):
        i = ring_slot()
        v = v3(slot_bf(i), 2)
        s.dma("pool", "rs%d" % i, v, w_out[l][e2 * 256:(e2 + 1) * 256, :].rearrange("(a p) n -> p a n", p=128), writes=["rs.%d" % i])
        return i, v

    def ada_fetch(l, part):
        sl = []
        for j in range(4):
            cb = part * 4 + j
            i = ring_slot()
            v = v3(slot_bf(i), 8)
            s.dma("pool", "rs%d" % i, v, w_ada[l].rearrange("(kc p) n -> p kc n", p=128)[:, :, cb * 256:(cb + 1) * 256], writes=["rs.%d" % i])
            sl.append((i, v))
        return sl

    def ada_compute(l, part, sl):
        psA = bank(7)

        def f(h):
            ins = None
            for j in range(4):
                for ec in range(2):
                    for kc in range(8):
                        col = (j * 2 + ec) * 2
                        ins = h.matmul(psA[:, col:col + 2], lhsT=sl[j][1][:, kc, ec * 128:(ec + 1) * 128], rhs=scTb[:, kc, :],
                                       start=(kc == 0), stop=(kc == 7))
            return ins
        s.op("pe", f, reads=["rs.%d" % i for i, _ in sl] + ["scTb"], writes=bk(7))
        psA3 = v3(psA[:, 0:16], 8)
        mv = modv(l)
        for sq in range(2):
            s.op("dve", lambda h, sq=sq: h.tensor_tensor(out=mv[:, 8 * part:8 * part + 8, sq], in0=psA3[:, :, sq], in1=badaT[:, 8 * part:8 * part + 8, l], op=ALU.add),
                 reads=["badaT"], writes=bk(7) + ["modT"])
            if part == 1:
                s.op("dve", lambda h, sq=sq: h.scalar_tensor_tensor(out=gsv(l)[:, :, sq], in0=mv[:, 8:16, sq], scalar=1.0,
                                                                     in1=gnT[:, :, l], op0=ALU.add, op1=ALU.mult),
                     reads=["modT", "gnT"], writes=["gsT"])

    def gate_rows(l, sq, dst, dkey):
        mv = modv(l)
        p = next_pair()

        def f(h):
            ins = None
            for kc in range(8):
                col = mv[:, 16 + kc, sq:sq + 1]
                lb = bass.AP(col.tensor, col.offset, [list(col.ap[0]), [0, 128]])
                ins = h.matmul(PS[:, p * 1024 + kc * 128:p * 1024 + (kc + 1) * 128], lhsT=lb, rhs=identf, start=True, stop=True)
            return ins
        s.op("pe", f, reads=["modT", "identf"], writes=bk(2 * p, 2 * p + 1))
        s.op("act", lambda h: h.activation(out=dst, in_=PS[:, p * 1024:(p + 1) * 1024], func=AF.Copy), reads=[], writes=bk(2 * p, 2 * p + 1) + [dkey])

    class Pref:
        def __init__(self, thunks, depth=1):
            self.th, self.res, self.n, self.depth = thunks, {}, 0, depth

        def get(self, g):
            while self.n < len(self.th) and self.n <= g + self.depth:
                self.res[self.n] = self.th[self.n]()
                self.n += 1
            return self.res.pop(g)

    def fetch_pool(g):
        i = ring_slot()
        v = v3(slot_bf(i), 4)
        s.dma("pool", "rs%d" % i, v, w_pool[g].rearrange("(a p) n -> p a n", p=128), writes=["rs.%d" % i])
        return i, v

    for part in range(2):
        ada_compute(0, part, ada_fetch(0, part))

    ST_DEFS = [
        dict(b0=0, nb=6, ncols=768, ep=768, sample=False),
        dict(b0=6, nb=6, ncols=768, ep=768, sample=False),
        dict(b0=12, nb=5, ncols=704, ep=640, sample=True),
    ]
    pair_rr = {"i": 0}

    def next_pair():
        p = pair_rr["i"]
        pair_rr["i"] = (p + 1) % 4
        return p

    def pairv(p, n):
        return PS[:, p * 1024:p * 1024 + n]

    cs_box = {"cs": 0}

    def ntiles(p, ncols):
        cs = cs_box["cs"]
        return [(cs, 512 - cs, 2 * p, cs), (512, ncols - 512, 2 * p + 1, 0)]

    def inproj(slot_i, wv, ci, p, ncols):
        nts = ntiles(p, ncols)

        def f(h):
            ins = None
            for kc in range(8):
                for (c0, n, b, bo) in nts:
                    ins = h.matmul(bank(b)[:, bo:bo + n], lhsT=wv[:, kc, ci * 128:(ci + 1) * 128], rhs=hT[:, kc, c0:c0 + n],
                                   start=(kc == 0), stop=(kc == 7))
            return ins
        s.op("pe", f, reads=["rs.%d" % slot_i] + ["hT.%d" % i for i in range(6)] + ["hTb.%d" % i for i in range(6)], writes=bk(2 * p, 2 * p + 1))

    def store_rows_T(src3, R, dst, okey, skey):
        pso = PS[0:R, 0:2048]

        def f(h):
            ins = None
            for c in range(16):
                ins = h.transpose(out=pso[:, c * 128:(c + 1) * 128], in_=src3[:, c, :], identity=identf)
            return ins
        s.op("pe", f, reads=[skey, "identf"], writes=bk(0, 1, 2, 3))
        s.op("act", lambda h: h.activation(out=stage[0:R, 0:1024], in_=pso[:, 0:1024], func=AF.Copy), reads=[], writes=bk(0, 1) + Tk(0, 1, 2))
        s.op("act", lambda h: h.activation(out=stage[0:R, 1024:2048], in_=pso[:, 1024:2048], func=AF.Copy), reads=[], writes=bk(2, 3) + Tk(0, 1, 2))
        s.dma("sp", "o_st", dst, stage[0:R, 0:2048], reads=Tk(0, 1, 2), writes=[okey])


    def emit_state_outputs(l):
        if l == 1:
            store_rows_T(hist[:, :, 0:2], 2, o_cb_p, "O.cbp", "hist")
            store_rows_T(outS[:, :, 0:2], 2, o_cb_s, "O.cbs", "outS")
        elif l == 2:
            store_rows_T(hist[:, :, 2:32], 30, o_cc_p, "O.ccp", "hist")
            store_rows_T(outS[:, :, 2:32], 30, o_cc_s, "O.ccs", "outS")
        elif l == 3:
            store_rows_T(hist[:, :, 32:47], 15, o_pd_p, "O.pdp", "hist")
            store_rows_T(outS[:, :, 32:47], 15, o_pd_s, "O.pds", "outS")

    def run_st(sti, sd):
        ncols, ep, nb = sd["ncols"], sd["ep"], sd["nb"]
        blocks = []
        for j in range(nb):
            blocks.append(dict(xt=X[:, j, :], n=128, col0=j * 128, sq=0, key="x.%d" % j, gb=sd["b0"] + j))
        if sd["sample"]:
            blocks.append(dict(xt=XS[0:32, :], n=32, col0=672, sq=1, key="x.5", gb=None))
        for j in range(nb):
            gb = sd["b0"] + j
            s.dma("sp", "x%d" % j, X[:, j, :], xp[gb * 128:(gb + 1) * 128, :], writes=["x.%d" % j])
        if sd["sample"]:
            s.dma("sp", "x5", XS[0:32, :], xsm, writes=["x.5"])

        def fix_hist(buf, H, hsl, key, c, first_mask_ok=True):
            h0, h1 = hsl
            if sti == 0:
                s.op("dve", lambda h: h.memset(buf[:, 0:H], 0.0), writes=[key])
                s.op("dve", lambda h: h.tensor_scalar_mul(out=buf[:, H + 128 - H:H + 128], in0=buf[:, H + 128 - H:H + 128], scalar1=maskc[:, 0:1]),
                     reads=["maskc", key], writes=[key])
            else:
                s.op("dve", lambda h: h.tensor_copy(out=buf[:, 0:H], in_=hist[:, c, h0:h1]), reads=["hist"], writes=[key])
            if sd["sample"]:
                s.op("dve", lambda h: h.tensor_copy(out=buf[:, H + 672 - H:H + 672], in_=stT[:, c, h0:h1]), reads=["stT", key], writes=[key])

        junkv = AUX_raw[:, 0:512].bitcast(BF16)

        def h_pre(bi, bl, lt):
            n, xt = bl["n"], bl["xt"]
            so = (bi % 4) * 4
            xi = bi % 3
            xn = AUX_raw[:, 3072 + xi * 512:3584 + xi * 512].bitcast(BF16)[0:n, 0:1024]
            s.op("dve", lambda h: h.memset(stat[0:n, so:so + 1], 0.0), reads=[], writes=["stat%d" % (bi % 4)])
            s.op("act", lambda h: h.activation(out=junkv[0:n, :], in_=xt, func=AF.Square, scale=1.0 / 32, accum_out=stat[0:n, so:so + 1]),
                 reads=[bl["key"]], writes=["junk", "stat%d" % (bi % 4)])
            s.op("act", lambda h: h.activation(out=stat[0:n, so + 1:so + 2], in_=stat[0:n, so:so + 1], func=AF.Sqrt, bias=EPS, scale=1.0),
                 reads=["stat%d" % (bi % 4)], writes=["stat%d" % (bi % 4)])
            s.op("dve", lambda h: h.reciprocal(out=stat[0:n, so + 2:so + 3], in_=stat[0:n, so + 1:so + 2]), reads=["stat%d" % (bi % 4)], writes=["stat%d" % (bi % 4)])
            s.op("act", lambda h: h.activation(out=xn, in_=xt, func=AF.Copy, scale=stat[0:n, so + 2:so + 3]),
                 reads=[bl["key"], "stat%d" % (bi % 4)], writes=["xn%d" % xi])

        def h_post(bi, bl, lt):
            n, sq, c0 = bl["n"], bl["sq"], bl["col0"]
            xi = bi % 3
            xn = AUX_raw[:, 3072 + xi * 512:3584 + xi * 512].bitcast(BF16)[0:n, 0:1024]
            mvt, gst = modv(lt), gsv(lt)
            p = next_pair()
            ptA = v3(bank(2 * p).bitcast(BF16)[:, 0:512], 4)
            ptB = v3(bank(2 * p + 1).bitcast(BF16)[:, 0:512], 4)

            def f(h):
                ins = None
                for kc in range(8):
                    dstp = ptA[:, kc, 0:n] if kc < 4 else ptB[:, kc - 4, 0:n]
                    ins = h.transpose(out=dstp, in_=xn[:, kc * 128:(kc + 1) * 128], identity=identb[0:n, 0:n])
                return ins
            s.op("pe", f, reads=["xn%d" % xi, "identb"], writes=bk(2 * p, 2 * p + 1))
            for kc in range(4):
                s.op("dve", lambda h, kc=kc: h.tensor_scalar(
                    out=hT[:, kc, c0:c0 + n], in0=ptA[:, kc, 0:n], scalar1=gst[:, kc, sq:sq + 1], scalar2=mvt[:, kc, sq:sq + 1],
                    op0=ALU.mult, op1=ALU.add), reads=["gsT", "modT"], writes=bk(2 * p) + ["hT.%d" % bi])
            for kc in range(4, 8):
                s.op("act", lambda h, kc=kc: h.activation(
                    out=hT[:, kc, c0:c0 + n], in_=ptB[:, kc - 4, 0:n], func=AF.Identity, scale=gst[:, kc, sq:sq + 1], bias=mvt[:, kc, sq:sq + 1]),
                    reads=["gsT", "modT"], writes=bk(2 * p + 1) + ["hTb.%d" % bi])

        hT_keys = ["hT.%d" % i for i in range(len(blocks))]

        def run_layer(l):
            mv = modv(l)
            last = (l == 3)
            cs_box["cs"] = 80 if (sti == 0 and l >= 1) else 0
            gP = gateP[l % 2]
            if l == 0:
                if sti > 0:
                    gate_rows(0, 0, gP, "gateP0")
                for bi, bl in enumerate(blocks):
                    h_pre(bi, bl, 0)
                    if bi >= 2:
                        h_post(bi - 2, blocks[bi - 2], 0)
                for bi in range(max(0, len(blocks) - 2), len(blocks)):
                    h_post(bi, blocks[bi], 0)
            if sd["sample"]:
                gate_rows(l, 1, gateS, "gateS")

            ada_sl = {}
            wo_box = {}

            def hoist_wo():
                if "wo" not in wo_box:
                    wo_box["wo"] = [fetch_out(l, e2) for e2 in range(8)]

            def mid_hook(grp):
                if l >= 3:
                    return
                if sti == 0:
                    if 2 <= grp <= 4:
                        ada_compute(l + 1, grp - 2, ada_sl[grp - 2])
                    if 1 <= grp <= 3:
                        ada_sl[grp - 1] = ada_fetch(l + 1, grp - 1)
                if grp == 5:
                    gate_rows(l + 1, 0, gateP[(l + 1) % 2], "gateP%d" % ((l + 1) % 2))

            if l == 0:
                wv_slots = [fetch_in(0, E + q * 256) for q in range(8)]

                def v_front(bi, bl):
                    n, c0 = bl["n"], bl["col0"]
                    smp = bl["sq"] == 1
                    gv = AUX_raw[0:n, (bi % 2) * 2048:(bi % 2) * 2048 + 2048]
                    vn = AUX_raw[:, 4096:6144].bitcast(BF16)[0:n, (bi % 2) * 2048:(bi % 2) * 2048 + 2048]
                    gk = ["aux.g%d" % (bi % 2)]
                    vk = ["aux.v%d" % (bi % 2)]
                    sk = ["statv%d" % (bi % 2)]

                    for vh in range(2):
                        def f(h, vh=vh):
                            ins = None
                            for kc in range(8):
                                for q in range(4 * vh, 4 * vh + 4):
                                    ins = h.matmul(bank(q // 2)[0:n, (q % 2) * 256:(q % 2) * 256 + 256], lhsT=hT[:, kc, c0:c0 + n],
                                                   rhs=wv_slots[q][1][:, kc, :], start=(kc == 0 and q % 2 == 0), stop=(kc == 7),
                                                   skip_group_check=True)
                            return ins
                        s.op("pe", f, reads=["hT.%d" % bi, "hTb.%d" % bi] + ["rs.%d" % i for i, _ in wv_slots[4 * vh:4 * vh + 4]], writes=bk(2 * vh, 2 * vh + 1))
                    so = 16 + (bi % 2) * 8
                    s.op("act", lambda h: h.activation(out=stat[0:n, so:so + 3], in_=stat[0:n, so:so + 3], func=AF.Copy, scale=0.0), reads=sk + ["stat_init"], writes=sk)
                    for hf in range(2):
                        s.op("act", lambda h, hf=hf: h.activation(out=gv[:, hf * 1024:(hf + 1) * 1024], in_=PS[0:n, hf * 1024:(hf + 1) * 1024],
                                                                  func=AF.Gelu, accum_out=stat[0:n, so + hf:so + hf + 1]),
                             reads=[], writes=bk(2 * hf, 2 * hf + 1) + gk + sk + (["xn0", "xn1"] if bi % 2 == 1 else []))
                    junk = Tpair01.bitcast(BF16)[0:n, 0:2048]
                    s.op("act", lambda h: h.activation(out=junk, in_=gv, func=AF.Square, accum_out=stat[0:n, so + 2:so + 3]),
                         reads=gk, writes=Tk(0, 1) + sk)
                    return gv, vn, gk, vk, sk, so, n, smp

                def v_stats(bi, bl, ctx):
                    gv, vn, gk, vk, sk, so, n, smp = ctx
                    s.op("dve", lambda h: h.tensor_tensor(out=stat[0:n, so + 3:so + 4], in0=stat[0:n, so:so + 1], in1=stat[0:n, so + 1:so + 2], op=ALU.add),
                         reads=sk, writes=sk)
                    s.op("dve", lambda h: h.tensor_scalar_mul(out=stat[0:n, so + 3:so + 4], in0=stat[0:n, so + 3:so + 4], scalar1=1.0 / E),
                         reads=sk, writes=sk)
                    s.op("dve", lambda h: h.tensor_tensor(out=stat[0:n, so + 4:so + 5], in0=stat[0:n, so + 3:so + 4], in1=stat[0:n, so + 3:so + 4], op=ALU.mult),
                         reads=sk, writes=sk)
                    s.op("dve", lambda h: h.scalar_tensor_tensor(out=stat[0:n, so + 5:so + 6], in0=stat[0:n, so + 2:so + 3], scalar=1.0 / E,
                                                                 in1=stat[0:n, so + 4:so + 5], op0=ALU.mult, op1=ALU.subtract),
                         reads=sk, writes=sk)
                    s.op("dve", lambda h: h.tensor_scalar_add(out=stat[0:n, so + 5:so + 6], in0=stat[0:n, so + 5:so + 6], scalar1=EPS), reads=sk, writes=sk)
                    s.op("pool", lambda h: h.tensor_tensor(out=stat[0:n, so + 6:so + 7], in0=stat[0:n, so + 5:so + 6], in1=stat[0:n, 63:64], op=ALU.pow),
                         reads=sk + ["neghalf"], writes=sk)
                    s.op("dve", lambda h: h.scalar_tensor_tensor(out=stat[0:n, so + 7:so + 8], in0=stat[0:n, so + 3:so + 4], scalar=-1.0,
                                                                 in1=stat[0:n, so + 6:so + 7], op0=ALU.mult, op1=ALU.mult),
                         reads=sk, writes=sk)
                    s.op("act", lambda h: h.activation(out=vn, in_=gv, func=AF.Identity, scale=stat[0:n, so + 6:so + 7], bias=stat[0:n, so + 7:so + 8]),
                         reads=gk + sk, writes=vk + (["xn2"] if bi % 2 == 0 else []))
                    if smp:
                        def vrows_out():
                            gx = gk + ["xn0", "xn1"]
                            s.op("dve", lambda h: h.tensor_scalar(out=gv, in0=gv, scalar1=stat[0:n, so + 6:so + 7], scalar2=stat[0:n, so + 7:so + 8],
                                                                  op0=ALU.mult, op1=ALU.add), reads=gx + vk + sk, writes=gx)
                            s.dma("sp", "su_st", stage[0:32, 0:2048], ln_a_gb[0:1, :].rearrange("a n -> (a n)").partition_broadcast(32), writes=Tk(0, 1, 2))
                            s.op("dve", lambda h: h.tensor_tensor(out=gv, in0=gv, in1=stage[0:32, 0:2048], op=ALU.mult), reads=Tk(0, 1, 2) + gx, writes=gx)
                            s.dma("sp", "su_st", stage[0:32, 0:2048], ln_a_gb[1:2, :].rearrange("a n -> (a n)").partition_broadcast(32), writes=Tk(0, 1, 2))
                            s.op("dve", lambda h: h.tensor_tensor(out=gv, in0=gv, in1=stage[0:32, 0:2048], op=ALU.add), reads=Tk(0, 1, 2) + gx, writes=gx)
                            s.dma("sp", "o_vr", vrows, gv, reads=gx, writes=["O.vrows"])
                        deferred_v.append(vrows_out)

                def v_back(bi, bl):
                    n, c0 = bl["n"], bl["col0"]
                    smp = bl["sq"] == 1
                    vn = AUX_raw[:, 4096:6144].bitcast(BF16)[0:n, (bi % 2) * 2048:(bi % 2) * 2048 + 2048]
                    vk = ["aux.v%d" % (bi % 2)]
                    wst = WsT_s if smp else WsT
                    bt = biasT_s if smp else biasT
                    mps = PS[:, 2048:2048 + 16 * n].rearrange("p (a b) -> p a b", a=16)

                    def f(h):
                        ins = None
                        for cc in range(16):
                            ins = h.matmul(mps[:, cc, :], lhsT=vn[:, cc * 128:(cc + 1) * 128], rhs=wst[0:n, cc // 2, 0:n], start=True, stop=True)
                        return ins
                    s.op("pe", f, reads=vk + ["WsT", "WsT_s"], writes=bk(4, 5, 6, 7))

                    def evac(lo, hi):
                        for cc in range(lo, hi):
                            s.op("dve", lambda h, cc=cc: h.scalar_tensor_tensor(
                                out=ysT[:, cc, c0:c0 + n], in0=mps[:, cc, :], scalar=prmT[:, cc, 7:8], in1=bt[:, cc, 0:n], op0=ALU.mult, op1=ALU.add),
                                reads=["prmT", "biasT", "biasT_s"], writes=bk(4, 5, 6, 7) + ["ysT.%d" % cc])
                    evac(0, 8)
                    return lambda: evac(8, 16)

                prevb = None
                deferred_v = []
                for bi, bl in enumerate(blocks):
                    ctx = v_front(bi, bl)
                    tail = v_back(*prevb) if prevb is not None else None
                    v_stats(bi, bl, ctx)
                    if tail is not None:
                        tail()
                    prevb = (bi, bl)
                v_back(*prevb)()
                for fdef in deferred_v:
                    fdef()
                pf = Pref([(lambda g=g: fetch_in(0, (g % 8) * 256 + (0 if g < 8 else 2 * E))) for g in range(16)], depth=3)
                for it in range(16):
                    if it == 0 and sti == 0:
                        ada_compute(0, 2, ada_fetch(0, 2))
                        gate_rows(0, 0, gP, "gateP0")
                    if it % 2 == 0:
                        mid_hook(it // 2)
                    sw = pf.get(it)
                    if it == 8:
                        hoist_wo()
                    isz = it >= 8
                    for ci in range(2):
                        c = (it % 8) * 2 + ci
                        pq = next_pair()
                        inproj(sw[0], sw[1], ci, pq, ncols)
                        tt_ = (2 if isz else 0) + (c % 2)
                        s.op("act", lambda h, pq=pq, tt_=tt_, isz=isz: h.activation(out=Tb(tt_)[:, 0:ncols], in_=pairv(pq, ncols), func=(AF.Silu if isz else AF.Gelu)),
                             reads=[], writes=bk(2 * pq, 2 * pq + 1) + Tk(tt_))
                        s.op("dve", lambda h, tt_=tt_, c=c: h.tensor_tensor(out=ysT[:, c, 0:ncols], in0=ysT[:, c, 0:ncols], in1=Tb(tt_)[:, 0:ncols], op=ALU.mult),
                             reads=Tk(tt_) + ["ysT.%d" % c], writes=["ysT.%d" % c])

            elif l == 1:
                pf = Pref([(lambda g=g: [fetch_in(1, b * E + g * 256) for b in range(4)]) for g in range(8)])
                for grp in range(8):
                    mid_hook(grp)
                    sl = pf.get(grp)
                    if grp == 4:
                        hoist_wo()
                    for ci in range(2):
                        c = grp * 2 + ci
                        pc_, ph_, pz_, pb_ = next_pair(), next_pair(), next_pair(), next_pair()
                        inproj(sl[1][0], sl[1][1], ci, pc_, ncols)
                        inproj(sl[2][0], sl[2][1], ci, ph_, ncols)
                        inproj(sl[3][0], sl[3][1], ci, pz_, ncols)
                        inproj(sl[0][0], sl[0][1], ci, pb_, ncols)
                        Pb = Tf(1)
                        s.op("act", lambda h, p=pc_: h.activation(out=Tf(0)[:, 0:ncols], in_=pairv(p, ncols), func=AF.Copy),
                             reads=[], writes=bk(2 * pc_, 2 * pc_ + 1) + Tk(0))
                        s.op("dve", lambda h, p=ph_: h.tensor_tensor(out=Tf(1)[:, 2:2 + ncols], in0=Tf(0)[:, 0:ncols], in1=pairv(p, ncols), op=ALU.mult),
                             reads=Tk(0), writes=bk(2 * ph_, 2 * ph_ + 1) + Tk(1))
                        s.op("act", lambda h, p=pz_, c=c: h.activation(out=ysT[:, c, 0:ncols], in_=pairv(p, ncols), func=AF.Silu),
                             reads=[], writes=bk(2 * pz_, 2 * pz_ + 1) + ["ysT.%d" % c])
                        s.op("act", lambda h, p=pb_: h.activation(out=Tf(3)[:, 0:ncols], in_=pairv(p, ncols), func=AF.Copy),
                             reads=[], writes=bk(2 * pb_, 2 * pb_ + 1) + Tk(3))
                        fix_hist(Pb, 2, (0, 2), "T.1", c)
                        s.op("dve", lambda h, c=c: h.tensor_copy(out=hist[:, c, 0:2], in_=Tf(1)[:, 2 + ep - 2:2 + ep]), reads=Tk(1), writes=["hist"])
                        if sd["sample"]:
                            s.op("dve", lambda h, c=c: h.tensor_copy(out=outS[:, c, 0:2], in_=Tf(1)[:, 2 + 702:2 + 704]), reads=Tk(1), writes=["outS"])
                        s.op("dve", lambda h, c=c: h.tensor_scalar_mul(out=Tf(2)[:, 0:ncols], in0=Tf(1)[:, 2:2 + ncols], scalar1=prmT[:, c, 2:3]),
                             reads=Tk(1) + ["prmT"], writes=Tk(2))
                        s.op("dve", lambda h, c=c: h.scalar_tensor_tensor(out=Tf(2)[:, 0:ncols], in0=Tf(1)[:, 1:1 + ncols], scalar=prmT[:, c, 1:2],
                                                                           in1=Tf(2)[:, 0:ncols], op0=ALU.mult, op1=ALU.add), reads=Tk(1, 2) + ["prmT"], writes=Tk(2))
                        s.op("dve", lambda h, c=c: h.scalar_tensor_tensor(out=Tf(2)[:, 0:ncols], in0=Tf(1)[:, 0:ncols], scalar=prmT[:, c, 0:1],
                                                                           in1=Tf(2)[:, 0:ncols], op0=ALU.mult, op1=ALU.add), reads=Tk(1, 2) + ["prmT"], writes=Tk(2))
                        s.op("dve", lambda h: h.tensor_tensor(out=Tf(2)[:, 0:ncols], in0=Tf(2)[:, 0:ncols], in1=Tf(3)[:, 0:ncols], op=ALU.mult),
                             reads=Tk(2, 3), writes=Tk(2))
                        s.op("dve", lambda h, c=c: h.tensor_tensor(out=ysT[:, c, 0:ncols], in0=ysT[:, c, 0:ncols], in1=Tf(2)[:, 0:ncols], op=ALU.mult),
                             reads=Tk(2) + ["ysT.%d" % c], writes=["ysT.%d" % c])

            elif l == 2:
                NTD = 3
                NTP = 31 - NTD

                def diag_view(c):
                    par = c % 2
                    dwords = ysT_raw[:, par * 2304:par * 2304 + NTP * 64].bitcast(BF16)
                    return dwords.rearrange("p (a b) -> p a b", a=NTP), ["ysT.%d" % i for i in range(par * 6, par * 6 + 6)]

                def build_diag(c, eng="pool"):
                    dg, dkeys = diag_view(c)
                    id_bc = bass.AP(identb.tensor, identb.offset, [list(identb.ap[0]), [0, NTP], [1, 128]])
                    w_bc = wccT[:, c, NTD:31].unsqueeze(2).broadcast_to([128, NTP, 128])
                    s.op(eng, lambda h: h.tensor_tensor(out=dg, in0=id_bc, in1=w_bc, op=ALU.mult), reads=["identb", "wccT"], writes=dkeys)

                pf = Pref([(lambda g=g: (fetch_in(2, g * 256), fetch_in(2, E + g * 256))) for g in range(8)])
                for grp in range(8):
                    mid_hook(grp)
                    if grp >= 6:
                        build_diag(grp - 6, "dve")
                    sa, sg = pf.get(grp)
                    for ci in range(2):
                        c = grp * 2 + ci
                        pa, pg = next_pair(), next_pair()
                        inproj(sa[0], sa[1], ci, pa, ncols)
                        inproj(sg[0], sg[1], ci, pg, ncols)
                        tt = c % 2
                        G = AUX[:, c, :]
                        s.op("act", lambda h, pg=pg, tt=tt: h.activation(out=Tf(tt)[:, 0:ncols], in_=pairv(pg, ncols), func=AF.Sigmoid),
                             reads=[], writes=bk(2 * pg, 2 * pg + 1) + Tk(tt))
                        s.op("dve", lambda h, pa=pa, tt=tt, G=G: h.tensor_tensor(out=G[:, 30:30 + ncols], in0=Tf(tt)[:, 0:ncols], in1=pairv(pa, ncols), op=ALU.mult),
                             reads=Tk(tt), writes=bk(2 * pa, 2 * pa + 1) + ["aux.%d" % c])
                        fix_hist(G, 30, (2, 32), "aux.%d" % c, c)
                        s.op("dve", lambda h, pa=pa, tt=tt, c=c: h.tensor_tensor(out=hist[:, c, 2:32], in0=Tf(tt)[:, ep - 30:ep], in1=pairv(pa, ncols)[:, ep - 30:ep], op=ALU.mult),
                             reads=Tk(tt) + ["aux.%d" % c], writes=bk(2 * pa, 2 * pa + 1) + ["hist"])
                        if sd["sample"]:
                            s.op("dve", lambda h, pa=pa, tt=tt, c=c: h.tensor_tensor(out=outS[:, c, 2:32], in0=Tf(tt)[:, 674:704], in1=pairv(pa, ncols)[:, 674:704], op=ALU.mult),
                                 reads=Tk(tt), writes=bk(2 * pa, 2 * pa + 1) + ["outS"])
                zsl = [fetch_in(2, 2 * E + g * 256) for g in range(8)]
                hoist_wo()
                pend_stats = None
                for c in range(16):
                    par = c % 2
                    dg, dkeys = diag_view(c)
                    pc = c % 2
                    G = AUX[:, c, :]
                    nts = ntiles(pc, ncols)

                    acc = Tf(c % 2)
                    s.op("act", lambda h, G=G, acc=acc, c=c: h.activation(out=acc[:, 0:ncols], in_=G[:, 0:ncols], func=AF.Copy, scale=wccT[:, c, 0:1]),
                         reads=["aux.%d" % c, "wccT"], writes=Tk(c % 2))
                    for k in range(1, NTD):
                        s.op("dve", lambda h, G=G, acc=acc, c=c, k=k: h.scalar_tensor_tensor(out=acc[:, 0:ncols], in0=G[:, k:k + ncols], scalar=wccT[:, c, k:k + 1],
                                                                                        in1=acc[:, 0:ncols], op0=ALU.mult, op1=ALU.add),
                             reads=["aux.%d" % c, "wccT"] + Tk(c % 2), writes=Tk(c % 2))

                    def f(h, dg=dg, G=G, nts=nts):
                        ins = None
                        for (c0, n, b, bo) in nts:
                            for k in range(NTD, 31):
                                ins = h.matmul(bank(b)[:, bo:bo + n], lhsT=dg[:, k - NTD, :], rhs=G[:, c0 + k:c0 + k + n], start=(k == NTD), stop=(k == 30))
                        return ins
                    s.op("pe", f, reads=dkeys + ["aux.%d" % c], writes=bk(2 * pc, 2 * pc + 1))
                    if c + 2 < 16:
                        build_diag(c + 2)
                    s.op("dve", lambda h, pc=pc, G=G, c=c, acc=acc: h.scalar_tensor_tensor(out=G[:, 30:30 + ncols], in0=pairv(pc, ncols), scalar=prmT[:, c, 3:4],
                                                                                          in1=acc[:, 0:ncols], op0=ALU.add, op1=ALU.add),
                         reads=["prmT"] + Tk(c % 2), writes=bk(2 * pc, 2 * pc + 1) + ["aux.%d" % c])
                    sqt = 2 + (c % 2)
                    s.op("act", lambda h, G=G, sqt=sqt: h.activation(out=Tb(sqt)[:, 0:ncols], in_=G[:, 30:30 + ncols], func=AF.Square),
                         reads=["aux.%d" % c], writes=Tk(sqt))

                    def f2(h, G=G, sqt=sqt, c=c):
                        ins = None
                        for (c0, n, bq, bo) in ntiles(0, ncols):
                            ins = h.matmul(bank(4 + bq)[:, bo:bo + n], lhsT=onesb, rhs=G[:, 30 + c0:30 + c0 + n], start=(c == 0), stop=(c == 15))
                            ins = h.matmul(bank(6 + bq)[:, bo:bo + n], lhsT=onesb, rhs=Tb(sqt)[:, c0:c0 + n], start=(c == 0), stop=(c == 15))
                        return ins
                    if pend_stats is not None:
                        s.op("pe", pend_stats[0], reads=pend_stats[1], writes=bk(4, 5, 6, 7))
                    pend_stats = (f2, ["aux.%d" % c, "onesb"] + Tk(sqt))
                s.op("pe", pend_stats[0], reads=pend_stats[1], writes=bk(4, 5, 6, 7))
                s.op("dve", lambda h: h.tensor_scalar_mul(out=Tf(0)[:, 0:ncols], in0=pairv(2, ncols), scalar1=1.0 / E), reads=[], writes=bk(4, 5) + Tk(0))
                s.op("dve", lambda h: h.tensor_tensor(out=Tf(2)[:, 0:ncols], in0=Tf(0)[:, 0:ncols], in1=Tf(0)[:, 0:ncols], op=ALU.mult), reads=Tk(0), writes=Tk(2))
                s.op("dve", lambda h: h.scalar_tensor_tensor(out=Tf(1)[:, 0:ncols], in0=pairv(3, ncols), scalar=1.0 / E, in1=Tf(2)[:, 0:ncols],
                                                              op0=ALU.mult, op1=ALU.subtract), reads=Tk(2), writes=bk(6, 7) + Tk(1))
                s.op("dve", lambda h: h.tensor_scalar_max(out=Tf(1)[:, 0:ncols], in0=Tf(1)[:, 0:ncols], scalar1=0.0), reads=Tk(1), writes=Tk(1))
                s.op("act", lambda h: h.activation(out=Tf(1)[:, 0:ncols], in_=Tf(1)[:, 0:ncols], func=AF.Sqrt, bias=EPS, scale=1.0), reads=Tk(1), writes=Tk(1))
                s.op("dve", lambda h: h.reciprocal(out=Tf(1)[:, 0:ncols], in_=Tf(1)[:, 0:ncols]), reads=Tk(1), writes=Tk(1))
                pend_m = None
                pair_rr["i"] = 0
                for grp in range(8):
                    sz = zsl[grp]
                    for ci in range(2):
                        c = grp * 2 + ci
                        pz = next_pair()
                        inproj(sz[0], sz[1], ci, pz, ncols)
                        G = AUX[:, c, :]
                        tq = 2 + (c % 2)
                        s.op("act", lambda h, pz=pz, c=c: h.activation(out=ysT[:, c, 0:ncols], in_=pairv(pz, ncols), func=AF.Silu),
                             reads=[], writes=bk(2 * pz, 2 * pz + 1) + ["ysT.%d" % c])
                        s.op("dve", lambda h, G=G, tq=tq: h.tensor_tensor(out=Tf(tq)[:, 0:ncols], in0=G[:, 30:30 + ncols], in1=Tf(0)[:, 0:ncols], op=ALU.subtract),
                             reads=["aux.%d" % c] + Tk(0), writes=Tk(tq))
                        s.op("dve", lambda h, tq=tq: h.tensor_tensor(out=Tf(tq)[:, 0:ncols], in0=Tf(tq)[:, 0:ncols], in1=Tf(1)[:, 0:ncols], op=ALU.mult),
                             reads=Tk(1, tq), writes=Tk(tq))
                        s.op("act", lambda h, c=c, G=G, tq=tq: h.activation(out=G[:, 30:30 + ncols], in_=Tf(tq)[:, 0:ncols], func=AF.Silu, scale=prmT[:, c, 4:5], bias=prmT[:, c, 5:6]),
                             reads=Tk(tq) + ["prmT"], writes=["aux.%d" % c])
                        if pend_m is not None:
                            pend_m()
                        pend_m = (lambda c=c, G=G: s.op("dve", lambda h: h.tensor_tensor(out=ysT[:, c, 0:ncols], in0=ysT[:, c, 0:ncols], in1=G[:, 30:30 + ncols], op=ALU.mult),
                                                          reads=["aux.%d" % c, "ysT.%d" % c], writes=["ysT.%d" % c]))
                pend_m()

            else:
                Wd = 15 + ncols
                pf = Pref([(lambda g=g: (fetch_in(3, g * 256), fetch_in(3, E + g * 256))) for g in range(8)])
                wp_box = {}
                for grp in range(8):
                    sp_, sz = pf.get(grp)
                    if grp == 4:
                        wp_box["wp"] = [fetch_pool(g) for g in range(4)]
                        hoist_wo()
                    for ci in range(2):
                        c = grp * 2 + ci
                        g = c // 4
                        win = 2 << g
                        pp, pz = next_pair(), next_pair()
                        inproj(sp_[0], sp_[1], ci, pp, ncols)
                        inproj(sz[0], sz[1], ci, pz, ncols)
                        s.op("act", lambda h, pp=pp: h.activation(out=Tf(0)[:, 15:15 + ncols], in_=pairv(pp, ncols), func=AF.Copy),
                             reads=[], writes=bk(2 * pp, 2 * pp + 1) + Tk(0))
                        s.op("act", lambda h, pz=pz, c=c: h.activation(out=ysT[:, c, 0:ncols], in_=pairv(pz, ncols), func=AF.Silu),
                             reads=[], writes=bk(2 * pz, 2 * pz + 1) + ["ysT.%d" % c])
                        fix_hist(Tf(0), 15, (32, 47), "T.0", c)
                        s.op("dve", lambda h, c=c: h.tensor_copy(out=hist[:, c, 32:47], in_=Tf(0)[:, 15 + ep - 15:15 + ep]), reads=Tk(0), writes=["hist"])
                        if sd["sample"]:
                            s.op("dve", lambda h, c=c: h.tensor_copy(out=outS[:, c, 32:47], in_=Tf(0)[:, 15 + 689:15 + 704]), reads=Tk(0), writes=["outS"])
                        s.op("dve", lambda h: h.tensor_tensor(out=Tf(1)[:, 1:Wd], in0=Tf(0)[:, 1:Wd], in1=Tf(0)[:, 0:Wd - 1], op=ALU.add), reads=Tk(0), writes=Tk(1))
                        fin = 1
                        if win >= 4:
                            s.op("dve", lambda h: h.tensor_tensor(out=Tf(2)[:, 3:Wd], in0=Tf(1)[:, 3:Wd], in1=Tf(1)[:, 1:Wd - 2], op=ALU.add), reads=Tk(1), writes=Tk(2))
                            fin = 2
                        if win >= 8:
                            s.op("dve", lambda h: h.tensor_tensor(out=Tf(1)[:, 7:Wd], in0=Tf(2)[:, 7:Wd], in1=Tf(2)[:, 3:Wd - 4], op=ALU.add), reads=Tk(2), writes=Tk(1))
                            fin = 1
                        if win >= 16:
                            s.op("dve", lambda h: h.tensor_tensor(out=Tf(2)[:, 15:Wd], in0=Tf(1)[:, 15:Wd], in1=Tf(1)[:, 7:Wd - 8], op=ALU.add), reads=Tk(1), writes=Tk(2))
                            fin = 2
                        s.op("dve", lambda h, fin=fin, win=win, c=c: h.scalar_tensor_tensor(out=AUX[:, c, 0:ncols], in0=Tf(fin)[:, 15:Wd], scalar=1.0 / win,
                                                                                            in1=Tf(0)[:, 15:Wd], op0=ALU.mult, op1=ALU.subtract),
                             reads=Tk(0, fin), writes=["aux.%d" % c])
                        if sti == 0:
                            s.op("dve", lambda h, fin=fin, g=g: h.tensor_tensor(out=Tf(3)[:, 0:16], in0=Tf(fin)[:, 15 + 128:15 + 144], in1=invc[:, g * 16:(g + 1) * 16], op=ALU.mult),
                                 reads=Tk(fin) + ["invc"], writes=Tk(3))
                            s.op("dve", lambda h, c=c: h.tensor_tensor(out=AUX[:, c, 128:144], in0=Tf(3)[:, 0:16], in1=Tf(0)[:, 15 + 128:15 + 144], op=ALU.subtract),
                                 reads=Tk(0, 3) + ["aux.%d" % c], writes=["aux.%d" % c])
                wp = wp_box["wp"]
                for fc in range(16):
                    g, fi = fc // 4, fc % 4
                    p = next_pair()
                    nts = ntiles(p, ncols)

                    def f(h, g=g, fi=fi, nts=nts):
                        ins = None
                        for cc in range(4):
                            for (c0, n, b, bo) in nts:
                                ins = h.matmul(bank(b)[:, bo:bo + n], lhsT=wp[g][1][:, cc, fi * 128:(fi + 1) * 128], rhs=AUX[:, 4 * g + cc, c0:c0 + n],
                                               start=(cc == 0), stop=(cc == 3))
                        return ins
                    s.op("pe", f, reads=["rs.%d" % wp[g][0]] + ["aux.%d" % (4 * g + cc) for cc in range(4)], writes=bk(2 * p, 2 * p + 1))
                    s.op("dve", lambda h, p=p, fc=fc: h.scalar_tensor_tensor(out=ysT[:, fc, 0:ncols], in0=pairv(p, ncols), scalar=prmT[:, fc, 6:7],
                                                                              in1=ysT[:, fc, 0:ncols], op0=ALU.mult, op1=ALU.mult),
                         reads=["prmT", "ysT.%d" % fc], writes=bk(2 * p, 2 * p + 1) + ["ysT.%d" % fc])

            hoist_wo()
            wo = wo_box["wo"]

            pendq = []
            for bi, bl in enumerate(blocks):
                n, xt, sq, c0 = bl["n"], bl["xt"], bl["sq"], bl["col0"]
                p = next_pair()
                gsrc = gateS if sq == 1 else gP
                gkey = "gateS" if sq == 1 else "gateP%d" % (l % 2)

                def f(h, n=n, c0=c0, p=p):
                    ins = None
                    for ec in range(16):
                        for hf in range(2):
                            ins = h.matmul(bank(2 * p + hf)[0:n, :], lhsT=ysT[:, ec, c0:c0 + n], rhs=wo[ec // 2][1][:, ec % 2, hf * 512:(hf + 1) * 512],
                                           start=(ec == 0), stop=(ec == 15))
                    return ins
                s.op("pe", f, reads=["ysT.%d" % i for i in range(16)] + ["rs.%d" % i for i, _ in wo], writes=bk(2 * p, 2 * p + 1))
                is_tail = (bi == len(blocks) - 1)
                if len(pendq) >= 2 and not is_tail:
                    pb = pendq.pop(0)
                    h_post(pb[0], pb[1], l + 1)
                tmp = Tpair01[0:n, 0:1024]
                s.op("dve", lambda h, p=p, n=n, tmp=tmp, gsrc=gsrc: h.tensor_tensor(out=tmp, in0=PS[0:n, p * 1024:(p + 1) * 1024], in1=gsrc[0:n, :], op=ALU.mult),
                     reads=[gkey], writes=bk(2 * p, 2 * p + 1) + Tk(0, 1))
                s.op("dve", lambda h, xt=xt, tmp=tmp: h.tensor_tensor(out=xt, in0=xt, in1=tmp, op=ALU.add),
                     reads=Tk(0, 1) + [bl["key"]], writes=[bl["key"]])
                if not last:
                    h_pre(bi, bl, l + 1)
                    pendq.append((bi, bl))
                elif bl["gb"] is None or bl["gb"] >= 1:
                    so = 32 + (bi % 4) * 4
                    s.op("dve", lambda h, so=so, n=n: h.memset(stat[0:n, so:so + 1], 0.0), reads=[], writes=["statf%d" % (bi % 4)])
                    s.op("act", lambda h, xt=xt, so=so, n=n: h.activation(out=junkv[0:n, :], in_=xt, func=AF.Square, scale=1.0 / 32, accum_out=stat[0:n, so:so + 1]),
                         reads=[bl["key"]], writes=["junk", "statf%d" % (bi % 4)])
                    s.op("act", lambda h, so=so, n=n: h.activation(out=stat[0:n, so + 1:so + 2], in_=stat[0:n, so:so + 1], func=AF.Sqrt, bias=EPS, scale=1.0),
                         reads=["statf%d" % (bi % 4)], writes=["statf%d" % (bi % 4)])
                    s.op("dve", lambda h, so=so, n=n: h.reciprocal(out=stat[0:n, so + 2:so + 3], in_=stat[0:n, so + 1:so + 2]), reads=["statf%d" % (bi % 4)], writes=["statf%d" % (bi % 4)])
                    yq = bi % 2
                    yt = AUX_raw[0:n, 1024 + yq * 1024:2048 + yq * 1024]
                    yk = ["aux.g%d" % yq]
                    s.op("dve", lambda h, xt=xt, so=so, n=n, yt=yt: h.scalar_tensor_tensor(out=yt, in0=xt, scalar=stat[0:n, so + 2:so + 3], in1=gf_bc[0:n, :],
                                                                                            op0=ALU.mult, op1=ALU.mult),
                         reads=[bl["key"], "statf%d" % (bi % 4), "gf_bc"], writes=yk)
                    if bl["gb"] is None:
                        s.dma("sp", "o_y%d" % yq, ysm, yt, reads=yk, writes=["O.ysm"])
                    else:
                        gb = bl["gb"] - 1
                        s.dma("sp", "o_y%d" % yq, yp[gb * 128:(gb + 1) * 128, :], yt, reads=yk, writes=["O.yp%d" % gb])
            for pb in pendq:
                h_post(pb[0], pb[1], l + 1)

        for l in range(4):
            run_layer(l)
            if sti == 2:
                emit_state_outputs(l)

    for sti, sd in enumerate(ST_DEFS):
        run_st(sti, sd)

    okeys = ["O.ysm", "O.vrows", "O.cbp", "O.cbs", "O.ccp", "O.ccs", "O.pdp", "O.pds"] + ["O.yp%d" % i for i in range(16)]
    s.wait_all("sp", okeys)
    with nc.Block() as block:
        s.emit_all(block)
    return nc


_WKEYS = ["w_ada", "b_ada", "g_norm", "w_in_a", "w_in_b", "w_in_c", "w_in_d", "w_out_a", "w_out_b", "w_out_c", "w_out_d",
          "w_conv_c", "w_s_a", "b_s_a", "w_pool_d"]


def kernel(**inp):
    f = lambda k: np.ascontiguousarray(np.asarray(inp[k], dtype=np.float32))
    xpr, xsa = f("x_prompt"), f("x_sample")
    shared = {k: f(k) for k in _WKEYS}
    shared["prm9"] = np.ascontiguousarray(np.concatenate(
        [f("w_conv_b"), f("b_conv_c")[None], f("ln_c_g")[None], f("ln_c_b")[None], f("scale_pool_d")[None], f("ln_a_g")[None], f("ln_a_b")[None]], 0))
    shared["ln_a_gb"] = np.ascontiguousarray(np.stack([f("ln_a_g"), f("ln_a_b")], 0))
    shared["g_final"] = f("g_final")[None]
    cpr, csa = f("c_prompt"), f("c_sample")
    sb, sc, sd_ = f("state_conv_b"), f("state_conv_c"), f("state_pool_d")
    wins = (2, 4, 8, 16)
    in_maps = []
    for i in range(8):
        b, half = i // 2, i % 2
        halo = xpr[b, 1920:2048] if half == 1 else np.zeros((128, D), np.float32)
        m = dict(shared)
        m["xp"] = np.ascontiguousarray(np.concatenate([halo, xpr[b, half * 2048:(half + 1) * 2048]], 0))
        m["xsm"] = xsa[i]
        m["st_b"], m["st_c"], m["st_d"] = sb[i], sc[i], sd_[i]
        m["cvec"] = np.ascontiguousarray(np.stack([cpr[b], csa[i]], 0))
        m["maskc"] = np.full((128, 1), float(half), np.float32)
        tab = np.zeros((4, 16), np.float32)
        for g, w in enumerate(wins):
            for j in range(16):
                tab[g, j] = 1.0 / (min(w, j + 1) if half == 0 else w)
        m["invc"] = np.ascontiguousarray(np.broadcast_to(tab.reshape(1, 64), (128, 64)))
        in_maps.append(m)
    nc = build_program()
    res = run_bass_kernel_spmd(nc, in_maps, core_ids=list(range(8)))
    R = res.results
    y_prompt = np.zeros((4, 4096, D), np.float32)
    for i in range(8):
        y_prompt[i // 2, (i % 2) * 2048:(i % 2 + 1) * 2048] = R[i]["yp"]
    stk = lambda k, ids: np.stack([np.asarray(R[i][k], np.float32) for i in ids], 0)
    allc = list(range(8))
    odd = [1, 3, 5, 7]
    return (y_prompt, stk("ysm", allc), stk("vrows", allc), stk("cb_p", odd), stk("cb_s", allc),
            stk("cc_p", odd), stk("cc_s", allc), stk("pd_p", odd), stk("pd_s", allc))
```
